# Optimizing a Trainium2 kernel written in Bass

```python
import math
import jax
import jax.numpy as jnp
from jax import lax
import numpy as np


D_MODEL = 1024
BATCH = 4
SEQ = 4096
DEPTH = 2

NORM_EPS = 1e-6
GDN_HEADS = 4
GDN_HEAD_DIM = 128
GDN_WIDTH = GDN_HEADS * GDN_HEAD_DIM
GDN_CHUNK = 64
CONV_WIDTH = 4
GDN_COLS = 4 * GDN_WIDTH + 2 * GDN_HEADS
RWKV_HEADS = 8
RWKV_HEAD_DIM = 64
RWKV_WIDTH = RWKV_HEADS * RWKV_HEAD_DIM
DECAY_LORA = 64
AAA_LORA = 64
GATE_LORA = 128
RWKV_COLS = 3 * RWKV_WIDTH + DECAY_LORA + AAA_LORA + GATE_LORA
RWKV_GN_EPS = 64e-5
MIX_WIDTH = GDN_WIDTH + RWKV_WIDTH
LRU_WIDTH = D_MODEL
LRU_BLOCKS = 4
LRU_BLOCK = LRU_WIDTH // LRU_BLOCKS
LRU_C = 8.0
D_FF = -(-8 * D_MODEL // (3 * 256)) * 256

kernel_name = 'hybrid_gdn_rwkv7_rglru_block'


def rms_norm(x, w):
    xf = x.astype(jnp.float32)
    y = xf * lax.rsqrt(jnp.mean(xf * xf, axis=-1, keepdims=True) + NORM_EPS)
    return (y * w.astype(jnp.float32)).astype(x.dtype)


def l2_normalize(x):
    return x * lax.rsqrt(jnp.sum(x * x, axis=-1, keepdims=True) + NORM_EPS)


def adaln_params(c, w, b):
    m = (jax.nn.silu(c) @ w + b)[:, None, :]
    shift, scale, gate = jnp.split(m, 3, axis=-1)
    return shift, scale, gate


def causal_depthwise_conv(x, w):
    width = w.shape[0]
    T = x.shape[1]
    xp = jnp.pad(x, ((0, 0), (width - 1, 0), (0, 0)))
    y = xp[:, 0:T] * w[0]
    for j in range(1, width):
        y = y + xp[:, j:j + T] * w[j]
    return y


def token_shift(x):
    return jnp.pad(x, ((0, 0), (1, 0), (0, 0)))[:, :-1]


def chunk_gated_delta_rule(q, k, v, g, beta):
    Bsz, T, H, Dk = q.shape
    Dv = v.shape[-1]
    C = GDN_CHUNK
    N = T // C

    def chunk(t):
        t = t.reshape((Bsz, N, C, H) + t.shape[3:])
        return jnp.moveaxis(t, 3, 1)

    q = chunk(q) * (Dk ** -0.5)
    k = chunk(k)
    v = chunk(v)
    g = chunk(g)
    beta = chunk(beta)
    G = jnp.cumsum(g, axis=-1)
    causal = jnp.tril(jnp.ones((C, C), dtype=bool))
    strict = jnp.tril(jnp.ones((C, C), dtype=bool), -1)
    decay = jnp.exp(jnp.where(causal, G[..., :, None] - G[..., None, :], -jnp.inf))
    k_beta = k * beta[..., None]
    m = jnp.where(strict, jnp.einsum('bhnik,bhnjk->bhnij', k_beta, k) * decay, 0.0)
    rhs = jnp.concatenate([v * beta[..., None], k_beta * jnp.exp(G)[..., None]], axis=-1)
    sol = lax.linalg.triangular_solve(m + jnp.eye(C, dtype=m.dtype), rhs, left_side=True,
                                      lower=True, unit_diagonal=True)
    u = sol[..., :Dv]
    w = sol[..., Dv:]
    attn = jnp.where(causal, jnp.einsum('bhnik,bhnjk->bhnij', q, k) * decay, 0.0)
    q_dec = q * jnp.exp(G)[..., None]
    k_dec = k * jnp.exp(G[..., -1:] - G)[..., None]
    g_last = jnp.exp(G[..., -1])
    xs = tuple(jnp.moveaxis(t, 2, 0) for t in (q_dec, k_dec, u, w, attn, g_last))

    def step(S, inp):
        qd, kd, uc, wc, ac, gl = inp
        v_new = uc - jnp.einsum('bhck,bhkv->bhcv', wc, S)
        o = jnp.einsum('bhck,bhkv->bhcv', qd, S) + jnp.einsum('bhcs,bhsv->bhcv', ac, v_new)
        S = S * gl[..., None, None] + jnp.einsum('bhck,bhcv->bhkv', kd, v_new)
        return S, o

    S0 = jnp.zeros((Bsz, H, Dk, Dv), dtype=q.dtype)
    _, o = lax.scan(step, S0, xs)
    o = jnp.moveaxis(o, 0, 2)
    return jnp.moveaxis(o, 1, 3).reshape(Bsz, T, H, Dv)


def gated_deltanet_group(cols, conv_w, a_log, dt_bias, norm_w):
    Bsz, T, _ = cols.shape
    W = GDN_WIDTH
    qkv = jax.nn.silu(causal_depthwise_conv(cols[..., :3 * W], conv_w)).astype(jnp.float32)
    z = cols[..., 3 * W:4 * W].astype(jnp.float32).reshape(Bsz, T, GDN_HEADS, GDN_HEAD_DIM)
    alpha = cols[..., 4 * W:4 * W + GDN_HEADS].astype(jnp.float32)
    b = cols[..., 4 * W + GDN_HEADS:].astype(jnp.float32)
    q, k, v = jnp.split(qkv, 3, axis=-1)
    q = l2_normalize(q.reshape(Bsz, T, GDN_HEADS, GDN_HEAD_DIM))
    k = l2_normalize(k.reshape(Bsz, T, GDN_HEADS, GDN_HEAD_DIM))
    v = v.reshape(Bsz, T, GDN_HEADS, GDN_HEAD_DIM)
    beta = jax.nn.sigmoid(b)
    g = -jnp.exp(a_log.astype(jnp.float32)) * jax.nn.softplus(alpha + dt_bias.astype(jnp.float32))
    o = chunk_gated_delta_rule(q, k, v, g, beta)
    o = rms_norm(o, norm_w) * jax.nn.silu(z)
    return o.reshape(Bsz, T, W).astype(cols.dtype)


def rwkv7_group(cols, mu, w0, w2, a0, a2, g2, k_k, k_a, r_k, ln_w, ln_b):
    Bsz, T, _ = cols.shape
    W = RWKV_WIDTH
    H = RWKV_HEADS
    Dh = RWKV_HEAD_DIM
    cf = cols.astype(jnp.float32)
    cf = cf + mu * (token_shift(cf) - cf)
    r = cf[..., :W]
    k = cf[..., W:2 * W]
    v = cf[..., 2 * W:3 * W]
    off = 3 * W
    wd = cf[..., off:off + DECAY_LORA]
    off = off + DECAY_LORA
    ad = cf[..., off:off + AAA_LORA]
    off = off + AAA_LORA
    gd = cf[..., off:off + GATE_LORA]
    w_log = -jax.nn.softplus(-(w0 + jnp.tanh(wd) @ w2)) - 0.5
    decay = jnp.exp(-jnp.exp(w_log))
    a = jax.nn.sigmoid(a0 + ad @ a2)
    g = jax.nn.sigmoid(gd) @ g2

    def heads(t):
        return t.reshape(Bsz, T, H, Dh)

    kk = l2_normalize(heads(k * k_k))
    k = k * (1.0 + (a - 1.0) * k_a)
    r, k, v, decay, a = heads(r), heads(k), heads(v), heads(decay), heads(a)
    kka = kk * a
    xs = tuple(jnp.moveaxis(t, 1, 0) for t in (r, decay, k, v, kk, kka))

    def step(S, inp):
        r_t, w_t, k_t, v_t, kk_t, b_t = inp
        sa = -jnp.einsum('bhvk,bhk->bhv', S, kk_t)
        S = S * w_t[:, :, None, :] + sa[..., None] * b_t[:, :, None, :] + v_t[..., None] * k_t[:, :, None, :]
        return S, jnp.einsum('bhvk,bhk->bhv', S, r_t)

    S0 = jnp.zeros((Bsz, H, Dh, Dh), dtype=jnp.float32)
    _, y = lax.scan(step, S0, xs)
    y = jnp.moveaxis(y, 0, 1)
    mean = jnp.mean(y, axis=-1, keepdims=True)
    var = jnp.mean(jnp.square(y - mean), axis=-1, keepdims=True)
    y = ((y - mean) * lax.rsqrt(var + RWKV_GN_EPS)).reshape(Bsz, T, W) * ln_w + ln_b
    bonus = (jnp.sum(r * k * r_k, axis=-1, keepdims=True) * v).reshape(Bsz, T, W)
    y = (y + bonus) * g
    return y.astype(cols.dtype)


def delta_rwkv_mixer(h, w_in, w_out, gdn_conv_w, gdn_a_log, gdn_dt_bias, gdn_norm_w,
                     rwkv_mu, rwkv_w0, rwkv_w2, rwkv_a0, rwkv_a2, rwkv_g2, rwkv_k_k, rwkv_k_a,
                     rwkv_r_k, rwkv_ln_w, rwkv_ln_b):
    cols = h @ w_in
    out_a = gated_deltanet_group(cols[..., :GDN_COLS], gdn_conv_w, gdn_a_log, gdn_dt_bias, gdn_norm_w)
    out_b = rwkv7_group(cols[..., GDN_COLS:], rwkv_mu, rwkv_w0, rwkv_w2, rwkv_a0, rwkv_a2, rwkv_g2,
                        rwkv_k_k, rwkv_k_a, rwkv_r_k, rwkv_ln_w, rwkv_ln_b)
    return jnp.concatenate([out_a, out_b], axis=-1) @ w_out


def rglru_mixer(h, w_in, conv_w, conv_b, wa, ba, wx, bx, lam, w_out):
    Bsz, T, _ = h.shape
    gate_branch, xb = jnp.split(h @ w_in, 2, axis=-1)
    xb = (causal_depthwise_conv(xb, conv_w) + conv_b).astype(jnp.float32)
    xblk = xb.reshape(Bsz, T, LRU_BLOCKS, LRU_BLOCK)
    r = jax.nn.sigmoid(jnp.einsum('btnd,nde->btne', xblk, wa).reshape(Bsz, T, LRU_WIDTH) + ba)
    i = jax.nn.sigmoid(jnp.einsum('btnd,nde->btne', xblk, wx).reshape(Bsz, T, LRU_WIDTH) + bx)
    log_a = -LRU_C * r * jax.nn.softplus(-lam)
    a = jnp.exp(log_a)
    u = xb * i * jnp.sqrt(-jnp.expm1(2.0 * log_a))

    def combine(left, right):
        a_l, u_l = left
        a_r, u_r = right
        return a_l * a_r, a_r * u_l + u_r

    _, hs = lax.associative_scan(combine, (a, u), axis=1)
    y = hs.astype(h.dtype) * jax.nn.gelu(gate_branch)
    return y @ w_out


def swiglu_ffn(h, w_gate, w_up, w_down):
    return (jax.nn.silu(h @ w_gate) * (h @ w_up)) @ w_down


def setup_inputs(seed: int = 0) -> dict:
    key = jax.random.key(seed)
    keys = iter(jax.random.split(key, 48))
    f32 = jnp.float32

    def normal(shape, scale):
        return scale * jax.random.normal(next(keys), shape, f32)

    def uniform(shape, lo, hi):
        return jax.random.uniform(next(keys), shape, f32, lo, hi)

    D = D_MODEL
    L = DEPTH
    NE = (DEPTH + 1) // 2
    NO = DEPTH // 2
    x = normal((BATCH, SEQ, D), 1.0)
    c = normal((BATCH, D), 1.0)
    norm_pre = 1.0 + normal((L, 2, D), 0.05)
    norm_post = 1.0 + normal((L, 2, D), 0.05)
    ada_w = normal((L, 2, D, 3 * D), D ** -0.5)
    ada_b = normal((L, 2, 3 * D), 0.02)
    ffn_w_gate = normal((L, D, D_FF), D ** -0.5)
    ffn_w_up = normal((L, D, D_FF), D ** -0.5)
    ffn_w_down = normal((L, D_FF, D), D_FF ** -0.5)
    mix_w_in = normal((NE, D, GDN_COLS + RWKV_COLS), D ** -0.5)
    mix_w_out = normal((NE, MIX_WIDTH, D), MIX_WIDTH ** -0.5)
    gdn_conv_w = normal((NE, CONV_WIDTH, 3 * GDN_WIDTH), CONV_WIDTH ** -0.5)
    gdn_a_log = jnp.log(uniform((NE, GDN_HEADS), 1.0, 16.0))
    dt = jnp.exp(uniform((NE, GDN_HEADS), math.log(1e-3), math.log(1e-1)))
    gdn_dt_bias = dt + jnp.log(-jnp.expm1(-dt))
    gdn_norm_w = 1.0 + normal((NE, GDN_HEAD_DIM), 0.05)
    rwkv_mu = uniform((NE, RWKV_COLS), 0.0, 1.0)
    rwkv_w0 = uniform((NE, RWKV_WIDTH), -6.0, -1.0)
    rwkv_w2 = normal((NE, DECAY_LORA, RWKV_WIDTH), 0.3 * DECAY_LORA ** -0.5)
    rwkv_a0 = normal((NE, RWKV_WIDTH), 0.1)
    rwkv_a2 = normal((NE, AAA_LORA, RWKV_WIDTH), AAA_LORA ** -0.5)
    rwkv_g2 = normal((NE, GATE_LORA, RWKV_WIDTH), GATE_LORA ** -0.5)
    rwkv_k_k = 0.85 + normal((NE, RWKV_WIDTH), 0.05)
    rwkv_k_a = 1.0 + normal((NE, RWKV_WIDTH), 0.05)
    rwkv_r_k = normal((NE, RWKV_HEADS, RWKV_HEAD_DIM), 0.1)
    rwkv_ln_w = 1.0 + normal((NE, RWKV_WIDTH), 0.05)
    rwkv_ln_b = normal((NE, RWKV_WIDTH), 0.02)
    lru_w_in = normal((NO, D, 2 * LRU_WIDTH), D ** -0.5)
    lru_conv_w = normal((NO, CONV_WIDTH, LRU_WIDTH), CONV_WIDTH ** -0.5)
    lru_conv_b = normal((NO, LRU_WIDTH), 0.02)
    lru_wa = normal((NO, LRU_BLOCKS, LRU_BLOCK, LRU_BLOCK), LRU_BLOCK ** -0.5)
    lru_ba = normal((NO, LRU_WIDTH), 0.02)
    lru_wx = normal((NO, LRU_BLOCKS, LRU_BLOCK, LRU_BLOCK), LRU_BLOCK ** -0.5)
    lru_bx = normal((NO, LRU_WIDTH), 0.02)
    a_init = uniform((NO, LRU_WIDTH), 0.9, 0.999) ** (1.0 / LRU_C)
    lru_lambda = jnp.log(a_init) - jnp.log1p(-a_init)
    lru_w_out = normal((NO, LRU_WIDTH, D), LRU_WIDTH ** -0.5)
    return {'x': x, 'c': c, 'norm_pre': norm_pre, 'norm_post': norm_post, 'ada_w': ada_w, 'ada_b': ada_b,
            'ffn_w_gate': ffn_w_gate, 'ffn_w_up': ffn_w_up, 'ffn_w_down': ffn_w_down,
            'mix_w_in': mix_w_in, 'mix_w_out': mix_w_out, 'gdn_conv_w': gdn_conv_w, 'gdn_a_log': gdn_a_log,
            'gdn_dt_bias': gdn_dt_bias, 'gdn_norm_w': gdn_norm_w, 'rwkv_mu': rwkv_mu, 'rwkv_w0': rwkv_w0,
            'rwkv_w2': rwkv_w2, 'rwkv_a0': rwkv_a0, 'rwkv_a2': rwkv_a2, 'rwkv_g2': rwkv_g2,
            'rwkv_k_k': rwkv_k_k, 'rwkv_k_a': rwkv_k_a, 'rwkv_r_k': rwkv_r_k, 'rwkv_ln_w': rwkv_ln_w,
            'rwkv_ln_b': rwkv_ln_b, 'lru_w_in': lru_w_in, 'lru_conv_w': lru_conv_w, 'lru_conv_b': lru_conv_b,
            'lru_wa': lru_wa, 'lru_ba': lru_ba, 'lru_wx': lru_wx, 'lru_bx': lru_bx, 'lru_lambda': lru_lambda,
            'lru_w_out': lru_w_out}


def reference(x, c, norm_pre, norm_post, ada_w, ada_b, ffn_w_gate, ffn_w_up, ffn_w_down,
              mix_w_in, mix_w_out, gdn_conv_w, gdn_a_log, gdn_dt_bias, gdn_norm_w,
              rwkv_mu, rwkv_w0, rwkv_w2, rwkv_a0, rwkv_a2, rwkv_g2, rwkv_k_k, rwkv_k_a, rwkv_r_k,
              rwkv_ln_w, rwkv_ln_b, lru_w_in, lru_conv_w, lru_conv_b, lru_wa, lru_ba, lru_wx, lru_bx,
              lru_lambda, lru_w_out):
    for layer in range(DEPTH):
        j = layer // 2
        shift, scale, gate = adaln_params(c, ada_w[layer, 0], ada_b[layer, 0])
        h = rms_norm(x, norm_pre[layer, 0]) * (1.0 + scale) + shift
        if layer % 2 == 0:
            y = delta_rwkv_mixer(h, mix_w_in[j], mix_w_out[j], gdn_conv_w[j], gdn_a_log[j], gdn_dt_bias[j],
                                 gdn_norm_w[j], rwkv_mu[j], rwkv_w0[j], rwkv_w2[j], rwkv_a0[j], rwkv_a2[j],
                                 rwkv_g2[j], rwkv_k_k[j], rwkv_k_a[j], rwkv_r_k[j], rwkv_ln_w[j], rwkv_ln_b[j])
        else:
            y = rglru_mixer(h, lru_w_in[j], lru_conv_w[j], lru_conv_b[j], lru_wa[j], lru_ba[j], lru_wx[j],
                            lru_bx[j], lru_lambda[j], lru_w_out[j])
        x = x + gate * rms_norm(y, norm_post[layer, 0])
        shift, scale, gate = adaln_params(c, ada_w[layer, 1], ada_b[layer, 1])
        h = rms_norm(x, norm_pre[layer, 1]) * (1.0 + scale) + shift
        y = swiglu_ffn(h, ffn_w_gate[layer], ffn_w_up[layer], ffn_w_down[layer])
        x = x + gate * rms_norm(y, norm_post[layer, 1])
    return x
```

```python
import contextlib
import math
import numpy as np
import concourse.bass as bass
import concourse.mybir as mybir
from concourse.bass_utils import run_bass_kernel_spmd

F32 = mybir.dt.float32
BF16 = mybir.dt.bfloat16
AF = mybir.ActivationFunctionType
ALU = mybir.AluOpType
AX = mybir.AxisListType

SEM_CHUNK = 20000
ATTACH_WAITS = True
N_DMA_SEMS = 40
DMA_CAST = True

D = 1024
KC = 8
SEQ = 4096
DFF = 2816
FC = 22
EPS = 1e-6


class Prog:
    def __init__(self, nc):
        self.nc = nc
        self.ins = []
        self.last_w = {}
        self.readers = {}
        self.stack = contextlib.ExitStack()
        self._n = 0

    def sb(self, shape, dtype, name=None):
        self._n += 1
        return self.stack.enter_context(self.nc.sbuf_tensor(name or f"sb{self._n}", list(shape), dtype))

    def ps(self, shape, dtype=F32, name=None):
        self._n += 1
        return self.stack.enter_context(self.nc.psum_tensor(name or f"ps{self._n}", list(shape), dtype))

    def op(self, eng, fn, reads=(), writes=(), dma=False):
        i = len(self.ins)
        deps = set()
        for k in reads:
            w = self.last_w.get(k)
            if w is not None:
                deps.add(w)
        for k in writes:
            w = self.last_w.get(k)
            if w is not None:
                deps.add(w)
            for r in self.readers.get(k, ()):
                deps.add(r)
        best = {}
        keep = set()
        for d in deps:
            Dd = self.ins[d]
            if Dd['dma'] or Dd['eng'] in ('pool', 'sp'):
                keep.add(d)
            elif Dd['eng'] not in best or d > best[Dd['eng']]:
                best[Dd['eng']] = d
        deps = keep | set(best.values())
        self.ins.append(dict(eng=eng, fn=fn, deps=deps, dma=dma))
        for k in reads:
            self.readers.setdefault(k, []).append(i)
        for k in writes:
            self.last_w[k] = i
            self.readers[k] = []
        return i

    def emit(self, S):
        nc = self.nc
        ins = self.ins
        engs = ['pe', 'act', 'dve', 'pool', 'sp']
        need = [False] * len(ins)
        for i, I in enumerate(ins):
            for d in I['deps']:
                Dd = ins[d]
                if Dd['dma'] or Dd['eng'] != I['eng'] or I['eng'] != 'pe' or I['dma']:
                    need[d] = True
        ticket = [None] * len(ins)
        streams = {e: [] for e in engs}
        for i, I in enumerate(ins):
            e = I['eng']
            st = streams[e]
            for d in sorted(I['deps']):
                t = ticket[d]
                if t is None:
                    continue
                if t[0] == 'c':
                    _, oe, g = t
                    if S.seen_c[e][oe] >= g:
                        continue
                    S.seen_c[e][oe] = g
                    st.append(('wait', S.csem(oe, g // SEM_CHUNK), g % SEM_CHUNK + 1))
                else:
                    _, k, v = t
                    if S.seen_d[e][k] >= v:
                        continue
                    S.seen_d[e][k] = v
                    st.append(('wait', S.dsems[k], v))
            if I['dma']:
                k = S.ndma % N_DMA_SEMS
                S.ndma += 1
                if S.dcount[k] > 0 and S.seen_d[e][k] < S.dcount[k]:
                    st.append(('wait', S.dsems[k], S.dcount[k]))
                    S.seen_d[e][k] = S.dcount[k]
                S.dcount[k] += 16
                ticket[i] = ('d', k, S.dcount[k])
                st.append(('ins', I['fn'], S.dsems[k], 16))
            elif need[i]:
                g = S.cnt[e]
                S.cnt[e] += 1
                ticket[i] = ('c', e, g)
                st.append(('ins', I['fn'], S.csem(e, g // SEM_CHUNK), 1))
            else:
                st.append(('ins', I['fn'], None, 0))
        for k in range(N_DMA_SEMS):
            if S.dcount[k] > S.seen_d['sp'][k]:
                streams['sp'].append(('wait', S.dsems[k], S.dcount[k]))
                S.seen_d['sp'][k] = S.dcount[k]
        self.n_ins = {e: len(streams[e]) for e in engs}

        def run(engobj, st, fuse=False):
            pend = None
            for it in st:
                if it[0] == 'wait':
                    if pend is not None:
                        engobj.wait_ge(pend[0], pend[1])
                    pend = (it[1], it[2])
                    if not (fuse and ATTACH_WAITS):
                        engobj.wait_ge(pend[0], pend[1])
                        pend = None
                else:
                    if pend is not None and it[3] == 16:
                        engobj.wait_ge(pend[0], pend[1])
                        pend = None
                    r = it[1](engobj)
                    if pend is not None:
                        r._wait_ge(pend[0], pend[1])
                        pend = None
                    if it[2] is not None:
                        r.then_inc(it[2], it[3])
            if pend is not None:
                engobj.wait_ge(pend[0], pend[1])

        with nc.Block() as block:
            @block.tensor
            def _(e):
                run(e, streams['pe'])

            @block.scalar
            def _(e):
                run(e, streams['act'], fuse=True)

            @block.vector
            def _(e):
                run(e, streams['dve'], fuse=True)

            @block.gpsimd
            def _(e):
                run(e, streams['pool'])

            @block.sync
            def _(e):
                run(e, streams['sp'])

    def close(self):
        self.stack.close()


class SemState:
    def __init__(self, nc, stack):
        engs = ['pe', 'act', 'dve', 'pool', 'sp']
        self.nc = nc
        self.stack = stack
        self.cs = {e: [] for e in engs}
        self.cnt = {e: 0 for e in engs}
        self.dsems = [stack.enter_context(nc.semaphore(f"d_{k}")) for k in range(N_DMA_SEMS)]
        self.dcount = [0] * N_DMA_SEMS
        self.ndma = 0
        self.seen_c = {e: {o: -1 for o in engs} for e in engs}
        self.seen_d = {e: [0] * N_DMA_SEMS for e in engs}
        for e in engs:
            self.csem(e, 0)

    def csem(self, e, gen):
        while len(self.cs[e]) <= gen:
            self.cs[e].append(self.stack.enter_context(self.nc.semaphore(f"c_{e}_{len(self.cs[e])}")))
        return self.cs[e][gen]


def fm_vec(v):
    v = np.asarray(v, np.float32)
    n = v.shape[-1] // 128
    return np.ascontiguousarray(v.reshape(n, 128).T)


def fm_mat(w):
    w = np.asarray(w, np.float32)
    K, N = w.shape
    return np.ascontiguousarray(w.reshape(K // 128, 128, N).transpose(1, 0, 2))


class Builder:
    def __init__(self, nt=SEQ):
        self.NT = nt
        self.nc = bass.Bass("TRN2", target_bir_lowering=False)
        self.G = contextlib.ExitStack()
        self.S = SemState(self.nc, self.G)
        self.P = None
        self.dram = {}
        self.uid = 0
        self.ninstr = {}

    def din(self, name, shape, dtype=F32):
        t = self.nc.dram_tensor(name, list(shape), dtype, kind="ExternalInput").ap()
        self.dram[name] = t
        return t

    def dout(self, name, shape, dtype=F32):
        t = self.nc.dram_tensor(name, list(shape), dtype, kind="ExternalOutput").ap()
        self.dram[name] = t
        return t

    def dscratch(self, name, shape, dtype=F32):
        t = self.nc.dram_tensor(name, list(shape), dtype, kind="Internal").ap()
        self.dram[name] = t
        return t

    def key(self, s):
        self.uid += 1
        return f"{s}#{self.uid}"

    def global_extra(self):
        pass

    def gsb(self, shape, dtype, name):
        return self.G.enter_context(self.nc.sbuf_tensor(name, list(shape), dtype))

    def begin(self, tag):
        self.P = Prog(self.nc)
        self.tag = tag
        self.psb = [self.P.ps([128, 512], F32, f"{tag}_psb{i}") for i in range(8)]
        self.ps_rr = 0
        return self.P

    def end(self):
        self.P.emit(self.S)
        self.ninstr[self.tag] = dict(self.P.n_ins)
        self.P.close()
        self.P = None

    def defer_begin(self, pool=None):
        self.ps_pool = pool
        if not hasattr(self, 'ps_prr'):
            self.ps_prr = {}
        self._buf = []
        self.P.op = lambda *a, **k: self._buf.append((a, k))

    def defer_end(self):
        del self.P.op
        self.ps_pool = None
        b = self._buf
        self._buf = None
        return b

    def replay(self, lists):
        lists = [l for l in lists if l]
        idx = [0] * len(lists)
        while True:
            best, bf = None, None
            for i, l in enumerate(lists):
                if idx[i] < len(l):
                    f = idx[i] / len(l)
                    if bf is None or f < bf:
                        best, bf = i, f
            if best is None:
                break
            a, k = lists[best][idx[best]]
            idx[best] += 1
            self.P.op(*a, **k)

    def psum(self):
        pool = getattr(self, 'ps_pool', None)
        if pool is not None:
            self.ps_prr[pool[0]] = self.ps_prr.get(pool[0], 0) + 1
            i = pool[(self.ps_prr[pool[0]] - 1) % len(pool)]
            return self.psb[i], f"psb{i}"
        i = self.ps_rr % 8
        self.ps_rr += 1
        return self.psb[i], f"psb{i}"

    def prologue(self, cT, ada_w, ada_b, npre, npost):
        self.ones_bf = self.gsb([128, 128], BF16, "ones_bf")
        self.ones128_bf = self.gsb([128, 128], BF16, "ones128_bf")
        self.mhalf = self.gsb([128, 1], F32, "mhalf")
        self.phalf = self.gsb([128, 1], F32, "phalf")
        self.epsc = {EPS: self.gsb([128, 1], F32, "eps_a"), 64e-5: self.gsb([128, 1], F32, "eps_b")}
        mres = self.gsb([128, 4, 24], F32, "ada_m")
        s1s = [self.gsb([128, 8], F32, f"ada_s1_{s}") for s in range(4)]
        g1s = [self.gsb([128, 8], F32, f"ada_g1_{s}") for s in range(4)]
        self.global_extra()
        P = self.begin("pro")
        P.op('pool', lambda e: e.memset(self.ones_bf[:], 1.0 / 1024.0), writes=['ones_bf'])
        P.op('pool', lambda e: e.memset(self.ones128_bf[:], 1.0 / 128.0), writes=['ones128_bf'])
        P.op('pool', lambda e: e.memset(self.mhalf[:], -0.5), writes=['mhalf'])
        P.op('pool', lambda e: e.memset(self.phalf[:], 0.5), writes=['phalf'])
        P.op('pool', lambda e: e.memset(self.epsc[EPS][:], EPS), writes=['eps_a'])
        P.op('pool', lambda e: e.memset(self.epsc[64e-5][:], 64e-5), writes=['eps_b'])
        sc = P.sb([128, 8], F32, "ada_sc")
        craw = P.sb([128, 8], F32, "ada_craw")
        P.op('sp', lambda e: e.dma_start(out=craw[:], in_=cT), writes=['craw'], dma=True)
        P.op('act', lambda e: e.activation(out=sc[:], in_=craw[:], func=AF.Tanh, scale=0.5), reads=['craw'], writes=['sc'])
        P.op('dve', lambda e: e.scalar_tensor_tensor(out=sc[:], in0=sc[:], scalar=1.0, in1=craw[:], op0=ALU.add, op1=ALU.mult),
             reads=['sc', 'craw'], writes=['sc'])
        P.op('dve', lambda e: e.tensor_scalar(out=sc[:], in0=sc[:], scalar1=0.5, scalar2=None, op0=ALU.mult), reads=['sc'], writes=['sc'])
        wbuf = [P.sb([128, 3072], BF16, f"ada_wbuf{i}") for i in range(6)]
        sc_bf = P.sb([128, 8], BF16, "ada_sc_bf")
        P.op('dve', lambda e: e.tensor_copy(out=sc_bf[:], in_=sc[:]), reads=['sc'], writes=['sc_bf'])
        bias = P.sb([128, 4, 24], F32, "ada_bias")
        pre = P.sb([128, 4, 8], F32, "ada_pre")
        post = P.sb([128, 4, 8], F32, "ada_post")
        P.op('sp', lambda e: e.dma_start(out=bias[:], in_=ada_b.rearrange("s p n -> p s n")), writes=['ada_bias'], dma=True)
        P.op('sp', lambda e: e.dma_start(out=pre[:], in_=npre.rearrange("s p n -> p s n")), writes=['ada_pre'], dma=True)
        P.op('sp', lambda e: e.dma_start(out=post[:], in_=npost.rearrange("s p n -> p s n")), writes=['ada_post'], dma=True)
        self.mod = {}
        it = 0
        for s in range(4):
            ps, psk = self.psum()
            for k in range(8):
                wb = wbuf[it % 6]
                wk = f"ada_wbuf{it % 6}"
                it += 1
                P.op('pool', lambda e, wb=wb, s=s, k=k: e.dma_start(out=wb[:], in_=ada_w[s, k]),
                     writes=[wk], dma=True)
                for n in range(24):
                    P.op('pe', lambda e, wb=wb, n=n, k=k, ps=ps: e.matmul(
                        ps[:, n * 8 + k:n * 8 + k + 1], lhsT=wb[:, n * 128:(n + 1) * 128], rhs=sc_bf[:, k:k + 1],
                        start=True, stop=True), reads=[wk, 'sc_bf'], writes=[psk])
            P.op('dve', lambda e, ps=ps, s=s: e.tensor_reduce(
                out=mres[:, s, :], in_=ps[:, 0:192].rearrange("p (n k) -> p n k", k=8), axis=AX.X, op=ALU.add),
                reads=[psk], writes=[f'ada_m{s}'])
            P.op('dve', lambda e, s=s: e.tensor_tensor(out=mres[:, s, :], in0=mres[:, s, :], in1=bias[:, s, :], op=ALU.add),
                 reads=[f'ada_m{s}', 'ada_bias'], writes=[f'ada_m{s}'])
            s1 = s1s[s]
            g1 = g1s[s]
            P.op('dve', lambda e, s=s, s1=s1: e.scalar_tensor_tensor(
                out=s1[:], in0=mres[:, s, 8:16], scalar=1.0, in1=pre[:, s, :], op0=ALU.add, op1=ALU.mult),
                reads=[f'ada_m{s}', 'ada_pre'], writes=[f'ada_s1_{s}'])
            P.op('dve', lambda e, s=s, g1=g1: e.tensor_tensor(out=g1[:], in0=mres[:, s, 16:24], in1=post[:, s, :], op=ALU.mult),
                 reads=[f'ada_m{s}', 'ada_post'], writes=[f'ada_g1_{s}'])
            self.mod[s] = dict(s1=s1, shift=mres, sidx=s, g1=g1)
        self.end()

    def rsqrt_ps(self, out_ap, outk, ps_ap, psk, eps):
        P = self.P
        P.op('act', lambda e: e.activation(out=out_ap, in_=ps_ap, func=AF.Ln, bias=self.epsc[eps][0:out_ap.shape[0], 0:1]), reads=[psk], writes=[outk])
        P.op('act', lambda e: e.activation(out=out_ap, in_=out_ap, func=AF.Exp, scale=-0.5), reads=[outk], writes=[outk])

    def rms_rstd(self, src, srckeys, TT, sq, sqk, rstd, rstdk, nchunk=KC, ones=None, eps=EPS):
        P = self.P
        ones = ones if ones is not None else self.ones_bf
        sqk = sqk if isinstance(sqk, list) else [sqk]
        P.op('act', lambda e: e.activation(out=sq[:, 0:nchunk, 0:TT], in_=src, func=AF.Square), reads=srckeys, writes=sqk)
        ps, psk = self.psum()
        for k in range(nchunk):
            P.op('pe', lambda e, k=k: e.matmul(ps[:, 0:TT], lhsT=ones[:], rhs=sq[:, k, 0:TT], start=(k == 0), stop=(k == nchunk - 1)),
                 reads=sqk, writes=[psk])
        self.rsqrt_ps(rstd[:, 0:TT], rstdk, ps[:, 0:TT], psk, eps)

    def prenorm(self, xt, xk, TT, s, sq, sqk, rstd, rstdk, tmp, tmpk, h, hk):
        P = self.P
        m = self.mod[s]
        self.rms_rstd(xt[:, :, 0:TT], [xk], TT, sq, sqk, rstd, rstdk)
        for k in range(KC):
            P.op('dve', lambda e, k=k: e.tensor_tensor(out=tmp[:, k, 0:TT], in0=xt[:, k, 0:TT], in1=rstd[:, 0:TT], op=ALU.mult),
                 reads=[xk, rstdk], writes=[f"{tmpk}{k}"])
            P.op('act', lambda e, k=k: e.activation(out=h[:, k, 0:TT], in_=tmp[:, k, 0:TT], func=AF.Identity,
                                                     scale=m['s1'][:, k:k + 1], bias=m['shift'][:, m['sidx'], k:k + 1]),
                 reads=[f"{tmpk}{k}"], writes=[f"{hk}{k}"])

    def postnorm_residual(self, y, yk_list, xt, xk, TT, s, sq, sqk, rstd, rstdk, xo, xok):
        P = self.P
        m = self.mod[s]
        self.rms_rstd(y[:, :, 0:TT], yk_list, TT, sq, sqk, rstd, rstdk)
        for k in range(KC):
            P.op('dve', lambda e, k=k: e.tensor_tensor(out=y[:, k, 0:TT], in0=y[:, k, 0:TT], in1=rstd[:, 0:TT], op=ALU.mult),
                 reads=[yk_list[k], rstdk], writes=[yk_list[k]])
            P.op('dve', lambda e, k=k: e.scalar_tensor_tensor(out=xo[:, k, 0:TT], in0=y[:, k, 0:TT], scalar=m['g1'][:, k:k + 1],
                                                               in1=xt[:, k, 0:TT], op0=ALU.mult, op1=ALU.add),
                 reads=[yk_list[k], xk], writes=[f"{xok}{k}"])

    def _stg(self):
        P = self.P
        if not hasattr(P, 'stg'):
            P.stg = [P.sb([128, getattr(self, 'stg_cols', 1408)], F32, f"{self.tag}_stg{i}") for i in range(2)]
            P.stg_i = 0
        i = P.stg_i % 2
        P.stg_i += 1
        return i, P.stg[i]

    def load_piece(self, dst, src, k, c0, w, key, first):
        P = self.P
        if DMA_CAST:
            P.op('pool', lambda e: e.dma_start(out=dst[:, k, c0:c0 + w], in_=src[:, k, c0:c0 + w]), reads=([] if first else [key]), writes=[key], dma=True)
            return
        i, sg = self._stg()
        P.op('sp', lambda e: e.dma_start(out=sg[:, 0:w], in_=src[:, k, c0:c0 + w]), writes=[f"stg{i}"], dma=True)
        P.op('pool', lambda e: e.tensor_copy(out=dst[:, k, c0:c0 + w], in_=sg[:, 0:w]), reads=[f"stg{i}"] + ([] if first else [key]), writes=[key])

    def load_w_bf16(self, dst, dstk, src, kchunks, ncols, piece=1408):
        for k in range(kchunks):
            c0 = 0
            first = True
            while c0 < ncols:
                w = min(piece, ncols - c0)
                self.load_piece(dst, src, k, c0, w, f"{dstk}{k}", first)
                first = False
                c0 += w

    def ffn_sublayer(self, layer, x_in, x_out, wg_d, wu_d, wd_d):
        P = self.begin(f"ffn{layer}")
        s = layer * 2 + 1
        TT = 256
        L = f"f{layer}_"
        wg = P.sb([128, KC, DFF], BF16, L + "wg")
        wu = P.sb([128, KC, DFF], BF16, L + "wu")
        wd = P.sb([128, FC, D], BF16, L + "wd")
        for pc in range(2):
            for k in range(KC):
                self.load_piece(wg, wg_d, k, pc * 1408, 1408, f"{L}wg{k}p{pc}", True)
                self.load_piece(wu, wu_d, k, pc * 1408, 1408, f"{L}wu{k}p{pc}", True)
        self.load_w_bf16(wd, L + "wd", wd_d, FC, D)
        wdk = [f"{L}wd{k}" for k in range(FC)]
        xt = [P.sb([128, KC, TT], F32, L + f"xt{i}") for i in range(2)]
        hL = [P.sb([128, KC, TT], BF16, L + f"h_{i}") for i in range(2)]
        tmp = P.sb([128, KC, TT], F32, L + "tmp")
        ftmp = P.sb([128, KC, TT], F32, L + "ftmp")
        fsq = P.sb([128, KC, TT], BF16, L + "fsq")
        frstd = P.sb([128, TT], F32, L + "frstd")
        rstd = P.sb([128, TT], F32, L + "rstd")
        act = P.sb([128, FC, TT], BF16, L + "act")
        sq = act
        SQK = [f"{L}act{f}" for f in range(KC)]
        sg = [P.sb([128, TT], F32, L + f"sg{i}") for i in range(2)]
        y = tmp
        xo = tmp
        xin_v = x_in.rearrange("(k p) t -> p k t", p=128)
        xout_v = x_out.rearrange("(k p) t -> p k t", p=128)
        ntile = self.NT // TT
        Fl, Bl = [], []

        def do_tile(j):
            t0 = j * TT
            pj = j % 2
            h = hL[pj]
            xb = xt[pj]
            xk = L + f"xt{pj}"
            self.defer_begin(pool=[0])
            P.op('act', lambda e, xb=xb, t0=t0: e.dma_start(out=xb[:], in_=xin_v[:, :, t0:t0 + TT]), writes=[xk], dma=True)
            self.prenorm(xb, xk, TT, s, fsq, L + "fsq", frstd, L + "frstd", ftmp, L + "ftmp", h, L + f"h{pj}_")
            hk = [f"{L}h{pj}_{k}" for k in range(KC)]
            Fl.append(self.defer_end())
            self.defer_begin(pool=[1, 2, 3, 4, 5, 6, 7])
            for f in range(FC):
                pg, pgk = self.psum()
                pu, puk = self.psum()
                for k in range(KC):
                    P.op('pe', lambda e, f=f, k=k, pg=pg: e.matmul(pg[:, 0:TT], lhsT=wg[:, k, f * 128:(f + 1) * 128], rhs=h[:, k, :],
                                                                    start=(k == 0), stop=(k == KC - 1)),
                         reads=[f"{L}wg{k}p{0 if f < 11 else 1}", hk[k]], writes=[pgk])
                for k in range(KC):
                    P.op('pe', lambda e, f=f, k=k, pu=pu: e.matmul(pu[:, 0:TT], lhsT=wu[:, k, f * 128:(f + 1) * 128], rhs=h[:, k, :],
                                                                    start=(k == 0), stop=(k == KC - 1)),
                         reads=[f"{L}wu{k}p{0 if f < 11 else 1}", hk[k]], writes=[puk])
                sgb = sg[f % 2]
                sgk = L + f"sg{f % 2}"
                P.op('act', lambda e, sgb=sgb, pg=pg: e.activation(out=sgb[:], in_=pg[:, 0:TT], func=AF.Tanh, scale=0.5), reads=[pgk], writes=[sgk])
                P.op('dve', lambda e, sgb=sgb, pg=pg: e.scalar_tensor_tensor(out=sgb[:], in0=sgb[:], scalar=1.0, in1=pg[:, 0:TT], op0=ALU.add, op1=ALU.mult),
                     reads=[sgk, pgk], writes=[sgk])
                P.op('dve', lambda e, sgb=sgb, pu=pu, f=f: e.scalar_tensor_tensor(out=act[:, f, :], in0=sgb[:], scalar=0.5, in1=pu[:, 0:TT], op0=ALU.mult, op1=ALU.mult),
                     reads=[sgk, puk], writes=[f"{L}act{f}"])
            yk = [f"{L}tmp{k}" for k in range(KC)]
            for d in range(KC):
                pd, pdk = self.psum()
                for f in range(FC):
                    P.op('pe', lambda e, f=f, d=d, pd=pd: e.matmul(pd[:, 0:TT], lhsT=wd[:, f, d * 128:(d + 1) * 128], rhs=act[:, f, :],
                                                                    start=(f == 0), stop=(f == FC - 1)),
                         reads=[wdk[f], f"{L}act{f}"], writes=[pdk])
                P.op('act', lambda e, d=d, pd=pd: e.activation(out=y[:, d, :], in_=pd[:, 0:TT], func=AF.Identity), reads=[pdk], writes=[yk[d]])
            self.postnorm_residual(y, yk, xb, xk, TT, s, sq, SQK, rstd, L + "rstd", xo, L + "tmp")
            P.op('sp', lambda e, t0=t0: e.dma_start(out=xout_v[:, :, t0:t0 + TT], in_=xo[:]),
                 reads=[f"{L}tmp{k}" for k in range(KC)], writes=[self.key('xout')], dma=True)
            Bl.append(self.defer_end())
        for j in range(ntile):
            do_tile(j)
        self.replay([Fl[0]])
        for j in range(ntile):
            self.replay([Bl[j]] + ([Fl[j + 1]] if j + 1 < ntile else []))
        self.end()

    def lru_sublayer(self, x_in, x_out, win_d, wa_d, wx_d, wout_d, vec_d):
        P = self.begin("lru")
        s = 2
        TT = 256
        L = "m1_"
        win = P.sb([128, KC, 2048], BF16, L + "win")
        wa = P.sb([128, 8, 256], BF16, L + "wa")
        wx = P.sb([128, 8, 256], BF16, L + "wx")
        wout = P.sb([128, KC, D], BF16, L + "wout")
        vec = P.sb([128, 8, 8], F32, L + "vec")
        P.op('sp', lambda e: e.dma_start(out=vec[:], in_=vec_d), writes=[L + "vec"], dma=True)
        self.load_w_bf16(win, L + "win", win_d, KC, 2048, piece=1024)
        self.load_w_bf16(wa, L + "wa", wa_d, 8, 256)
        self.load_w_bf16(wx, L + "wx", wx_d, 8, 256)
        self.load_w_bf16(wout, L + "wout", wout_d, KC, D)
        c8 = P.sb([128, 8], F32, L + "c8")
        c4 = P.sb([128, 8], F32, L + "c4")
        hb = P.sb([128, 2, 8], F32, L + "hb")
        P.op('act', lambda e: e.activation(out=c8[:], in_=vec[:, 7, :], func=AF.Exp, scale=-1.0), reads=[L + "vec"], writes=[L + "c8"])
        P.op('act', lambda e: e.activation(out=c8[:], in_=c8[:], func=AF.Ln, bias=1.0), reads=[L + "c8"], writes=[L + "c8"])
        P.op('dve', lambda e: e.tensor_scalar(out=c4[:], in0=c8[:], scalar1=-4.0, scalar2=None, op0=ALU.mult), reads=[L + "c8"], writes=[L + "c4"])
        P.op('dve', lambda e: e.tensor_scalar(out=c8[:], in0=c8[:], scalar1=-8.0, scalar2=None, op0=ALU.mult), reads=[L + "c8", L + "c4"], writes=[L + "c8"])
        P.op('dve', lambda e: e.tensor_scalar(out=hb[:], in0=vec[:, 5:7, :], scalar1=0.5, scalar2=None, op0=ALU.mult), reads=[L + "vec"], writes=[L + "hb"])
        xt = [P.sb([128, KC, TT], F32, L + f"xt{i}") for i in range(2)]
        h = P.sb([128, KC, TT], BF16, L + "h")
        tmp = P.sb([128, KC, TT], F32, L + "tmp")
        sq = P.sb([128, KC, TT], BF16, L + "sq")
        rstd = P.sb([128, TT], F32, L + "rstd")
        ggL = [P.sb([128, KC, TT], F32, L + f"gg_{i}") for i in range(2)]
        xb = P.sb([128, KC, TT + 3], F32, L + "xb")
        cvL = [P.sb([128, KC, TT], F32, L + f"cv_{i}") for i in range(2)]
        cvbL = [P.sb([128, KC, TT], BF16, L + f"cvb_{i}") for i in range(2)]
        f1 = [P.sb([128, TT], F32, L + f"f1_{i}") for i in range(2)]
        f2 = [P.sb([128, TT], F32, L + f"f2_{i}") for i in range(2)]
        ybuf = P.sb([128, KC, TT], F32, L + "ybuf")
        aA = P.sb([128, KC, TT], F32, L + "aA")
        mA = P.sb([128, KC, TT], F32, L + "mA")
        qA = P.sb([128, KC, TT], F32, L + "qA")
        sq2 = P.sb([128, KC, TT], BF16, L + "sq2")
        rstd2 = P.sb([128, TT], F32, L + "rstd2")
        t1 = [P.sb([128, TT], F32, L + f"t1_{i}") for i in range(2)]
        t2 = [P.sb([128, TT], F32, L + f"t2_{i}") for i in range(2)]
        t3 = [P.sb([128, TT], F32, L + f"t3_{i}") for i in range(2)]
        t4 = [P.sb([128, TT], F32, L + f"t4_{i}") for i in range(2)]
        hs = [P.sb([128, TT], F32, L + f"hs_{i}") for i in range(2)]
        st = P.sb([128, KC], F32, L + "state")
        yb = P.sb([128, KC, TT], BF16, L + "yb")
        y = ybuf
        P.op('pool', lambda e: e.memset(st[:], 0.0), writes=[L + f"state{k}" for k in range(KC)])
        P.op('pool', lambda e: e.memset(xb[:, :, 0:3], 0.0), writes=[L + f"xbh{k}" for k in range(KC)])
        xin_v = x_in.rearrange("(k p) t -> p k t", p=128)
        xout_v = x_out.rearrange("(k p) t -> p k t", p=128)
        C1 = 0.044715
        C2 = math.sqrt(2.0 / math.pi)
        Fl, Bl = [], []

        def do_tile(j):
            t0 = j * TT
            pj = j % 2
            gg, cv, cvb = ggL[pj], cvL[pj], cvbL[pj]
            xo = gg
            LP = L + f"p{pj}_"
            self.defer_begin(pool=[0, 1, 2])
            xtb = xt[j % 2]
            xk = L + f"xt{j % 2}"
            P.op('act', lambda e, xtb=xtb, t0=t0: e.dma_start(out=xtb[:], in_=xin_v[:, :, t0:t0 + TT]), writes=[xk], dma=True)
            self.prenorm(xtb, xk, TT, s, sq, L + "sq", rstd, L + "rstd", tmp, L + "tmp", h, L + "h")
            hk = [f"{L}h{k}" for k in range(KC)]
            wink = [f"{L}win{k}" for k in range(KC)]
            for n in range(KC):
                ps, psk = self.psum()
                for k in range(KC):
                    P.op('pe', lambda e, n=n, k=k, ps=ps: e.matmul(ps[:, 0:TT], lhsT=win[:, k, n * 128:(n + 1) * 128], rhs=h[:, k, :],
                                                                    start=(k == 0), stop=(k == KC - 1)), reads=[wink[k], hk[k]], writes=[psk])
                a1 = f1[n % 2]; a1k = L + f"f1_{n % 2}"
                a2 = f2[n % 2]; a2k = L + f"f2_{n % 2}"
                P.op('act', lambda e, a1=a1, ps=ps: e.activation(out=a1[:], in_=ps[:, 0:TT], func=AF.Square), reads=[psk], writes=[a1k])
                P.op('dve', lambda e, a1=a1: e.tensor_scalar(out=a1[:], in0=a1[:], scalar1=C1, scalar2=1.0, op0=ALU.mult, op1=ALU.add),
                     reads=[a1k], writes=[a1k])
                P.op('dve', lambda e, a1=a1, ps=ps: e.tensor_tensor(out=a1[:], in0=a1[:], in1=ps[:, 0:TT], op=ALU.mult), reads=[a1k, psk], writes=[a1k])
                P.op('act', lambda e, a1=a1, a2=a2: e.activation(out=a2[:], in_=a1[:], func=AF.Tanh, scale=C2), reads=[a1k], writes=[a2k])
                P.op('dve', lambda e, a2=a2, ps=ps, n=n: e.scalar_tensor_tensor(out=gg[:, n, :], in0=a2[:], scalar=1.0, in1=ps[:, 0:TT], op0=ALU.add, op1=ALU.mult),
                     reads=[a2k, psk], writes=[f"{LP}gg{n}"])
            for n in range(KC):
                ps, psk = self.psum()
                for k in range(KC):
                    P.op('pe', lambda e, n=n, k=k, ps=ps: e.matmul(ps[:, 0:TT], lhsT=win[:, k, D + n * 128:D + (n + 1) * 128], rhs=h[:, k, :],
                                                                    start=(k == 0), stop=(k == KC - 1)), reads=[wink[k], hk[k]], writes=[psk])
                P.op('act', lambda e, n=n, ps=ps: e.activation(out=xb[:, n, 3:3 + TT], in_=ps[:, 0:TT], func=AF.Identity),
                     reads=[psk], writes=[f"{L}xbm{n}"])
                rk = [f"{L}xbm{n}", f"{L}xbh{n}", L + "vec"]
                P.op('dve', lambda e, n=n: e.tensor_scalar(out=cv[:, n, :], in0=xb[:, n, 0:TT], scalar1=vec[:, 0, n:n + 1], scalar2=vec[:, 4, n:n + 1],
                                                            op0=ALU.mult, op1=ALU.add), reads=rk, writes=[f"{LP}cv{n}"])
                for jj in range(1, 4):
                    P.op('dve', lambda e, n=n, jj=jj: e.scalar_tensor_tensor(
                        out=cv[:, n, :], in0=xb[:, n, jj:jj + TT], scalar=vec[:, jj, n:n + 1], in1=cv[:, n, :],
                        op0=ALU.mult, op1=ALU.add), reads=rk + [f"{LP}cv{n}"], writes=[f"{LP}cv{n}"])
                P.op('act', lambda e, n=n: e.activation(out=cvb[:, n, :], in_=cv[:, n, :], func=AF.Identity), reads=[f"{LP}cv{n}"], writes=[f"{LP}cvb{n}"])
                P.op('pool', lambda e, n=n: e.tensor_copy(out=xb[:, n, 0:3], in_=xb[:, n, TT:TT + 3]),
                     reads=[f"{L}xbm{n}", f"{LP}cv{n}"], writes=[f"{L}xbh{n}"])
            Fl.append(self.defer_end())
            self.defer_begin(pool=[3, 4, 5, 6, 7])
            for n in range(KC):
                blk, eo = n // 2, n % 2
                pr, prk = self.psum()
                pi, pik = self.psum()
                for kk in range(2):
                    P.op('pe', lambda e, kk=kk, blk=blk, eo=eo, pr=pr: e.matmul(
                        pr[:, 0:TT], lhsT=wa[:, blk * 2 + kk, eo * 128:(eo + 1) * 128], rhs=cvb[:, blk * 2 + kk, :], start=(kk == 0), stop=(kk == 1)),
                        reads=[f"{L}wa{blk * 2 + kk}", f"{LP}cvb{blk * 2 + kk}"], writes=[prk])
                for kk in range(2):
                    P.op('pe', lambda e, kk=kk, blk=blk, eo=eo, pi=pi: e.matmul(
                        pi[:, 0:TT], lhsT=wx[:, blk * 2 + kk, eo * 128:(eo + 1) * 128], rhs=cvb[:, blk * 2 + kk, :], start=(kk == 0), stop=(kk == 1)),
                        reads=[f"{L}wx{blk * 2 + kk}", f"{LP}cvb{blk * 2 + kk}"], writes=[pik])
                tr = t1[n % 2]; trk = L + f"t1_{n % 2}"
                ti = t2[n % 2]; tik = L + f"t2_{n % 2}"
                P.op('act', lambda e, tr=tr, pr=pr, n=n: e.activation(out=tr[:], in_=pr[:, 0:TT], func=AF.Tanh, scale=0.5, bias=hb[:, 0, n:n + 1]),
                     reads=[prk, L + "hb"], writes=[trk])
                P.op('act', lambda e, ti=ti, pi=pi, n=n: e.activation(out=ti[:], in_=pi[:, 0:TT], func=AF.Tanh, scale=0.5, bias=hb[:, 1, n:n + 1]),
                     reads=[pik, L + "hb"], writes=[tik])
                P.op('act', lambda e, tr=tr, n=n: e.activation(out=aA[:, n, :], in_=tr[:], func=AF.Exp, scale=c4[:, n:n + 1], bias=c4[:, n:n + 1]),
                     reads=[trk, L + "c4"], writes=[f"{L}aA{n}"])
                P.op('act', lambda e, tr=tr, n=n: e.activation(out=mA[:, n, :], in_=tr[:], func=AF.Exp, scale=c8[:, n:n + 1], bias=c8[:, n:n + 1]),
                     reads=[trk, L + "c8"], writes=[f"{L}mA{n}"])
                P.op('dve', lambda e, n=n: e.tensor_scalar(out=mA[:, n, :], in0=mA[:, n, :], scalar1=-1.0, scalar2=1.0, op0=ALU.mult, op1=ALU.add),
                     reads=[f"{L}mA{n}"], writes=[f"{L}mA{n}"])
                P.op('dve', lambda e, n=n: e.tensor_scalar(out=mA[:, n, :], in0=mA[:, n, :], scalar1=1e-30, scalar2=None, op0=ALU.max), reads=[f"{L}mA{n}"], writes=[f"{L}mA{n}"])
                P.op('dve', lambda e, ti=ti, n=n: e.scalar_tensor_tensor(out=qA[:, n, :], in0=ti[:], scalar=1.0, in1=cv[:, n, :], op0=ALU.add, op1=ALU.mult),
                     reads=[tik, f"{LP}cv{n}"], writes=[f"{L}qA{n}"])
            mks = [f"{L}mA{n}" for n in range(KC)]
            P.op('act', lambda e: e.activation(out=mA[:], in_=mA[:], func=AF.Ln), reads=mks, writes=mks)
            P.op('act', lambda e: e.activation(out=mA[:], in_=mA[:], func=AF.Exp, scale=0.5), reads=mks, writes=mks)
            for n in range(KC):
                hs_ = hs[n % 2]; hsk = L + f"hs_{n % 2}"
                P.op('dve', lambda e, n=n: e.scalar_tensor_tensor(out=qA[:, n, :], in0=qA[:, n, :], scalar=0.5, in1=mA[:, n, :], op0=ALU.mult, op1=ALU.mult),
                     reads=[f"{L}qA{n}", f"{L}mA{n}"], writes=[f"{L}qA{n}"])
                P.op('dve', lambda e, hs_=hs_, n=n: e.tensor_tensor_scan(
                    out=hs_[:], data0=aA[:, n, :], data1=qA[:, n, :], initial=st[:, n:n + 1], op0=ALU.mult, op1=ALU.add),
                    reads=[f"{L}aA{n}", f"{L}qA{n}", f"{L}state{n}"], writes=[hsk])
                P.op('dve', lambda e, hs_=hs_, n=n: e.tensor_copy(out=st[:, n:n + 1], in_=hs_[:, TT - 1:TT]), reads=[hsk], writes=[f"{L}state{n}"])
                P.op('dve', lambda e, hs_=hs_, n=n: e.scalar_tensor_tensor(out=yb[:, n, :], in0=hs_[:], scalar=0.5, in1=gg[:, n, :], op0=ALU.mult, op1=ALU.mult),
                     reads=[hsk, f"{LP}gg{n}"], writes=[f"{L}yb{n}"])
            yk = [f"{L}ybuf{k}" for k in range(KC)]
            for d in range(KC):
                pd, pdk = self.psum()
                for k in range(KC):
                    P.op('pe', lambda e, k=k, d=d, pd=pd: e.matmul(pd[:, 0:TT], lhsT=wout[:, k, d * 128:(d + 1) * 128], rhs=yb[:, k, :],
                                                                    start=(k == 0), stop=(k == KC - 1)),
                         reads=[f"{L}wout{k}", f"{L}yb{k}"], writes=[pdk])
                P.op('act', lambda e, d=d, pd=pd: e.activation(out=y[:, d, :], in_=pd[:, 0:TT], func=AF.Identity), reads=[pdk], writes=[yk[d]])
            self.postnorm_residual(y, yk, xtb, xk, TT, s, sq2, L + "sq2", rstd2, L + "rstd2", xo, LP + "gg")
            P.op('sp', lambda e, t0=t0: e.dma_start(out=xout_v[:, :, t0:t0 + TT], in_=xo[:]),
                 reads=[f"{LP}gg{k}" for k in range(KC)], writes=[self.key('xout')], dma=True)
            Bl.append(self.defer_end())
        NTL = self.NT // TT
        for j in range(NTL):
            do_tile(j)
        self.replay([Fl[0]])
        for j in range(NTL):
            self.replay([Bl[j]] + ([Fl[j + 1]] if j + 1 < NTL else []))
        self.end()

    def tri_inv_alloc(self, L, nb, mk_d):
        P = self.P
        W = dict(L=L, nb=nb)
        W['mk32'] = P.sb([64, 13, 64], F32, L + "ti_mk32")
        W['mk'] = P.sb([64, 13, 64], BF16, L + "ti_mk")
        W['I'] = P.sb([64, 64], BF16, L + "ti_I")
        P.op('sp', lambda e: e.dma_start(out=W['mk32'][:], in_=mk_d), writes=[L + "ti_mk32"], dma=True)
        P.op('act', lambda e: e.activation(out=W['mk'][:], in_=W['mk32'][:], func=AF.Identity), reads=[L + "ti_mk32"], writes=[L + "ti_mk"])
        P.op('act', lambda e: e.activation(out=W['I'][:], in_=W['mk32'][:, 12, :], func=AF.Identity), reads=[L + "ti_mk32"], writes=[L + "ti_I"])
        W['NoA'] = P.sb([64, nb, 6, 64], BF16, L + "ti_NoA")
        W['NoTA'] = P.sb([64, nb, 6, 64], BF16, L + "ti_NoTA")
        W['X'] = P.sb([64, nb, 64], BF16, L + "ti_X")
        W['Ub'] = P.sb([64, nb, 64], BF16, L + "ti_Ub")
        W['Upb'] = P.sb([64, nb, 64], BF16, L + "ti_Upb")
        return W

    def tri_inv(self, N_, Nk, NT_, NTk, XT, XTk, nb, W):
        P = self.P
        L = W['L']
        NW = nb * 64
        mk, I_, NoA, NoTA, X, Ub, Upb = W['mk'], W['I'], W['NoA'], W['NoTA'], W['X'], W['Ub'], W['Upb']
        Xk, Ubk, Upbk = L + "ti_X", L + "ti_Ub", L + "ti_Upb"
        f2 = lambda t: t[:].rearrange("p c t -> p (c t)")
        P.op('dve', lambda e: e.tensor_tensor(out=NoA[:], in0=N_[:, :, None, :].to_broadcast([64, nb, 6, 64]), in1=mk[:, None, 0:6, :].to_broadcast([64, nb, 6, 64]), op=ALU.mult),
             reads=[Nk, L + "ti_mk"], writes=[L + "ti_NoA"])
        P.op('dve', lambda e: e.tensor_tensor(out=NoTA[:], in0=NT_[:, :, None, :].to_broadcast([64, nb, 6, 64]), in1=mk[:, None, 6:12, :].to_broadcast([64, nb, 6, 64]), op=ALU.mult),
             reads=[NTk, L + "ti_mk"], writes=[L + "ti_NoTA"])
        P.op('dve', lambda e: e.tensor_tensor(out=X[:], in0=NoA[:, :, 0, :], in1=I_[:, None, :].to_broadcast([64, nb, 64]), op=ALU.add),
             reads=[L + "ti_NoA", L + "ti_I"], writes=[Xk])
        P.op('dve', lambda e: e.tensor_tensor(out=XT[:], in0=NoTA[:, :, 0, :], in1=I_[:, None, :].to_broadcast([64, nb, 64]), op=ALU.add),
             reads=[L + "ti_NoTA", L + "ti_I"], writes=[XTk])
        for lvl in range(1, 6):
            last = (lvl == 5)
            pu, puk = self.psum()
            for c in range(nb):
                P.op('pe', lambda e, c=c, pu=pu, lvl=lvl: e.matmul(pu[0:64, c * 64:(c + 1) * 64], lhsT=NoA[:, c, lvl, :], rhs=XT[:, c, :], start=True, stop=True),
                     reads=[L + "ti_NoA", XTk], writes=[puk])
            P.op('act', lambda e, pu=pu: e.activation(out=f2(Ub), in_=pu[0:64, 0:NW], func=AF.Identity), reads=[puk], writes=[Ubk])
            if not last:
                pu2, pu2k = self.psum()
                for c in range(nb):
                    P.op('pe', lambda e, c=c, pu2=pu2, lvl=lvl: e.matmul(pu2[0:64, c * 64:(c + 1) * 64], lhsT=NoTA[:, c, lvl, :], rhs=X[:, c, :], start=True, stop=True),
                         reads=[L + "ti_NoTA", Xk], writes=[pu2k])
                P.op('act', lambda e, pu2=pu2: e.activation(out=f2(Upb), in_=pu2[0:64, 0:NW], func=AF.Identity), reads=[pu2k], writes=[Upbk])
            pv, pvk = self.psum()
            for c in range(nb):
                P.op('pe', lambda e, c=c, pv=pv: e.matmul(pv[0:64, c * 64:(c + 1) * 64], lhsT=X[:, c, :], rhs=Ub[:, c, :], start=True, stop=True),
                     reads=[Xk, Ubk], writes=[pvk])
            if not last:
                pv2, pv2k = self.psum()
                for c in range(nb):
                    P.op('pe', lambda e, c=c, pv2=pv2: e.matmul(pv2[0:64, c * 64:(c + 1) * 64], lhsT=XT[:, c, :], rhs=Upb[:, c, :], start=True, stop=True),
                         reads=[XTk, Upbk], writes=[pv2k])
            P.op('dve', lambda e, pv=pv: e.tensor_tensor(out=f2(XT), in0=f2(XT), in1=pv[0:64, 0:NW], op=ALU.add), reads=[XTk, pvk], writes=[XTk])
            if not last:
                P.op('dve', lambda e, pv2=pv2: e.tensor_tensor(out=f2(X), in0=f2(X), in1=pv2[0:64, 0:NW], op=ALU.add), reads=[Xk, pv2k], writes=[Xk])

    def gdn_section(self, x_in, gout, wgd_d, cvec_d, tok_d, nw_d, cm_d, mk_d):
        P = self.begin("gdn")
        s = 0
        TT = 256
        NCH = TT // 64
        L = "gd_"
        NW = NCH * 64
        wgd = P.sb([128, KC, 2056], BF16, L + "wgd")
        self.stg_cols = 1028
        self.defer_begin()
        self.load_w_bf16(wgd, L + "wgd", wgd_d, KC, 2056, piece=1028)
        WL = self.defer_end()
        self.stg_cols = 1408
        wk = [f"{L}wgd{k}" for k in range(KC)]
        cvec = P.sb([128, 4, 12], F32, L + "cvec")
        tokc = P.sb([64, 2, 4], F32, L + "tokc")
        nw = P.sb([128, 1], F32, L + "nw")
        cm = P.sb([64, 5, 64], F32, L + "cm")
        P.op('sp', lambda e: e.dma_start(out=cvec[:], in_=cvec_d), writes=[L + "cvec"], dma=True)
        P.op('sp', lambda e: e.dma_start(out=tokc[:], in_=tok_d), writes=[L + "tokc"], dma=True)
        P.op('sp', lambda e: e.dma_start(out=nw[:], in_=nw_d), writes=[L + "nw"], dma=True)
        P.op('sp', lambda e: e.dma_start(out=cm[:], in_=cm_d), writes=[L + "cm"], dma=True)
        SLm, UTm, Im = cm[:, 0, :], cm[:, 1, :], cm[:, 2, :]
        ones64 = P.sb([64, 128], F32, L + "ones64")
        ident = P.sb([128, 128], F32, L + "ident")
        P.op('pool', lambda e: e.memset(ones64[:], 1.0), writes=[L + "ones64"])
        P.op('pool', lambda e: e.memset(ident[:], 0.0), writes=[L + "ident"])
        P.op('sp', lambda e: e.dma_start(out=ident[0:64, 0:64], in_=cm_d[:, 2, :]), reads=[L + "ident"], writes=[L + "ident"], dma=True)
        P.op('sp', lambda e: e.dma_start(out=ident[64:128, 64:128], in_=cm_d[:, 2, :]), reads=[L + "ident"], writes=[L + "ident"], dma=True)
        ones4_bf = P.sb([128, 128], BF16, L + "ones4")
        P.op('pool', lambda e: e.memset(ones4_bf[:], 0.25), writes=[L + "ones4"])
        nA = P.sb([64, 4], F32, L + "nA")
        P.op('act', lambda e: e.activation(out=nA[:], in_=tokc[:, 0, :], func=AF.Exp), reads=[L + "tokc"], writes=[L + "nA"])
        P.op('dve', lambda e: e.tensor_scalar(out=nA[:], in0=nA[:], scalar1=-1.0, scalar2=None, op0=ALU.mult), reads=[L + "nA"], writes=[L + "nA"])
        self.replay([WL])
        S32 = P.sb([128, 4, 128], F32, L + "S32")
        Sbf = P.sb([128, 4, 128], BF16, L + "Sbf")
        P.op('pool', lambda e: e.memset(S32[:], 0.0), writes=[L + "S32_0", L + "S32_1"])
        P.op('pool', lambda e: e.memset(Sbf[:], 0.0), writes=[L + "Sbf_0", L + "Sbf_1"])
        xq = P.sb([128, 12, TT + 3], F32, L + "xq")
        P.op('pool', lambda e: e.memset(xq[:, :, 0:3], 0.0), writes=[L + f"xqh{n}" for n in range(12)])
        xt = P.sb([128, KC, TT], F32, L + "xt")
        h = P.sb([128, KC, TT], BF16, L + "h")
        tmp = P.sb([128, KC, TT], F32, L + "tmp")
        sq = P.sb([128, KC, TT], BF16, L + "sq")
        rstd = P.sb([128, TT], F32, L + "rstd")
        cva = [P.sb([128, TT], F32, L + f"cva{i}") for i in range(2)]
        cvt = [P.sb([128, TT], F32, L + f"cvt{i}") for i in range(2)]
        qnL = [P.sb([128, 4, TT], BF16, L + f"qn_{i}") for i in range(2)]
        knL = [P.sb([128, 4, TT], BF16, L + f"kn_{i}") for i in range(2)]
        kn32L = [P.sb([128, 4, TT], F32, L + f"kn32_{i}") for i in range(2)]
        vTL = [P.sb([128, 4, TT], F32, L + f"vT_{i}") for i in range(2)]
        zsL = [P.sb([128, 4, TT], F32, L + f"zs_{i}") for i in range(2)]
        rq = P.sb([128, 2, TT], F32, L + "rq")
        rq2 = P.sb([128, TT], F32, L + "rq2")
        sqo = P.sb([128, 4, TT], BF16, L + "sqo")
        ab = P.sb([64, NCH, 8], F32, L + "ab")
        betaL = [P.sb([64, NCH, 4], F32, L + f"beta_{i}") for i in range(2)]
        nbetaL = [P.sb([64, NCH, 4], F32, L + f"nbeta_{i}") for i in range(2)]
        gtL = [P.sb([64, NCH, 4], F32, L + f"gt_{i}") for i in range(2)]
        SCR = []
        for ss in range(2):
            LSs = L + f"s{ss}_"
            d_ = dict(LS=LSs)
            for nm in ("BSL", "BUT", "BI", "eD", "eDT", "eDTs", "bbc", "Nm"):
                d_[nm] = P.sb([64, NCH, 64], F32, LSs + nm)
            d_["eGl"] = P.sb([64, NCH], F32, LSs + "eGl")
            d_["Pm"] = [P.sb([64, NCH, 64], BF16, LSs + f"Pm{i}") for i in range(2)]
            d_["PTm"] = [P.sb([64, NCH, 64], BF16, LSs + f"PTm{i}") for i in range(2)]
            SCR.append(d_)
        ATm = [P.sb([64, NCH, 64], BF16, L + f"AT{hd}") for hd in range(4)]
        for ss in range(2):
            SCR[ss]["TW"] = self.tri_inv_alloc(SCR[ss]["LS"], NCH, mk_d)
        attnT = [P.sb([64, NCH, 64], BF16, L + f"attnT{hd}") for hd in range(4)]
        eG = [P.sb([128, NW], F32, L + f"eG{hd}") for hd in range(4)]
        kgT = [P.sb([128, NW], BF16, L + f"kgT{hd}") for hd in range(4)]
        qdT = [P.sb([128, NW], BF16, L + f"qdT{hd}") for hd in range(4)]
        vb = P.sb([64, NCH, 4, 128], F32, L + "vb")
        kdec = P.sb([64, NCH, 4, 128], BF16, L + "kdec")
        Rt = P.sb([64, 4, 128], F32, L + "Rt")
        Rb = P.sb([64, 4, 128], BF16, L + "Rb")
        vnew = P.sb([64, 4, 128], BF16, L + "vnew")
        oT = P.sb([128, 4, TT], F32, L + "oT")
        ob = P.sb([128, 4, TT], F32, L + "ob")
        xin_v = x_in.rearrange("(k p) t -> p k t", p=128)
        gout_v = gout.rearrange("(k p) t -> p k t", p=128)
        SCQ = 0.5 * (128 ** -0.5)

        def bc(ap, shape):
            return ap.to_broadcast(shape)

        Fl, Bl, HAl, HBl, SCl, EPl0, EPl = [], [], [], [], [], [], []

        def do_tile(j):
            t0 = j * TT
            pj = j % 2
            qn, kn, kn32, vT, zs, beta, nbeta, gt = qnL[pj], knL[pj], kn32L[pj], vTL[pj], zsL[pj], betaL[pj], nbetaL[pj], gtL[pj]
            LP = L + f"p{pj}_"
            self.defer_begin(pool=[0, 7])
            xk = L + "xt"
            P.op('act', lambda e, t0=t0: e.dma_start(out=xt[:], in_=xin_v[:, :, t0:t0 + TT]), writes=[xk], dma=True)
            self.prenorm(xt, xk, TT, s, sq, L + "sq", rstd, L + "rstd", tmp, L + "tmp", h, L + "h")
            hk = [f"{L}h{k}" for k in range(KC)]
            for n in range(12):
                ps, psk = self.psum()
                for k in range(KC):
                    P.op('pe', lambda e, n=n, k=k, ps=ps: e.matmul(ps[:, 0:TT], lhsT=wgd[:, k, n * 128:(n + 1) * 128], rhs=h[:, k, :],
                                                                    start=(k == 0), stop=(k == KC - 1)), reads=[wk[k], hk[k]], writes=[psk])
                P.op('act', lambda e, n=n, ps=ps: e.activation(out=xq[:, n, 3:3 + TT], in_=ps[:, 0:TT], func=AF.Identity),
                     reads=[psk], writes=[f"{L}xqm{n}"])
                ca = cva[n % 2]; cak = L + f"cva{n % 2}"
                ct = cvt[n % 2]; ctk = L + f"cvt{n % 2}"
                rk = [f"{L}xqm{n}", f"{L}xqh{n}", L + "cvec"]
                P.op('dve', lambda e, n=n, ca=ca: e.tensor_scalar(out=ca[:], in0=xq[:, n, 0:TT], scalar1=cvec[:, 0, n:n + 1], scalar2=None, op0=ALU.mult),
                     reads=rk, writes=[cak])
                for jj in range(1, 4):
                    P.op('dve', lambda e, n=n, jj=jj, ca=ca: e.scalar_tensor_tensor(
                        out=ca[:], in0=xq[:, n, jj:jj + TT], scalar=cvec[:, jj, n:n + 1], in1=ca[:], op0=ALU.mult, op1=ALU.add),
                        reads=rk + [cak], writes=[cak])
                P.op('pool', lambda e, n=n: e.tensor_copy(out=xq[:, n, 0:3], in_=xq[:, n, TT:TT + 3]),
                     reads=[f"{L}xqm{n}", cak], writes=[f"{L}xqh{n}"])
                P.op('act', lambda e, ca=ca, ct=ct: e.activation(out=ct[:], in_=ca[:], func=AF.Tanh, scale=0.5), reads=[cak], writes=[ctk])
                typ, hd = n // 4, n % 4
                if typ == 2:
                    P.op('dve', lambda e, ca=ca, ct=ct, hd=hd: e.scalar_tensor_tensor(out=vT[:, hd, :], in0=ct[:], scalar=1.0, in1=ca[:], op0=ALU.add, op1=ALU.mult),
                         reads=[cak, ctk], writes=[f"{LP}vT{hd}"])
                else:
                    P.op('dve', lambda e, ca=ca, ct=ct, n=n: e.scalar_tensor_tensor(out=tmp[:, n, :], in0=ct[:], scalar=1.0, in1=ca[:], op0=ALU.add, op1=ALU.mult),
                         reads=[cak, ctk], writes=[f"{L}tmp{n}"])
            P.op('act', lambda e: e.activation(out=sq[:], in_=tmp[:], func=AF.Square), reads=[f"{L}tmp{n}" for n in range(8)], writes=[L + "sq"])
            for g2_ in range(4):
                p2, p2k = self.psum()
                for u_ in range(2):
                    n = g2_ * 2 + u_
                    P.op('pe', lambda e, p2=p2, n=n, u_=u_: e.matmul(p2[:, u_ * TT:(u_ + 1) * TT], lhsT=ones4_bf[:], rhs=sq[:, n, :], start=True, stop=True),
                         reads=[L + "sq", L + "ones4"], writes=[p2k])
                self.rsqrt_ps(rq[:].rearrange("p u t -> p (u t)"), L + "rq", p2[:, 0:2 * TT], p2k, EPS)
                for u_ in range(2):
                    n = g2_ * 2 + u_
                    typ, hd = n // 4, n % 4
                    if typ == 0:
                        P.op('dve', lambda e, n=n, hd=hd, u_=u_: e.scalar_tensor_tensor(out=qn[:, hd, :], in0=tmp[:, n, :], scalar=SCQ, in1=rq[:, u_, :], op0=ALU.mult, op1=ALU.mult),
                             reads=[f"{L}tmp{n}", L + "rq"], writes=[f"{LP}qn{hd}"])
                    else:
                        P.op('dve', lambda e, n=n, hd=hd, u_=u_: e.scalar_tensor_tensor(out=kn32[:, hd, :], in0=tmp[:, n, :], scalar=0.5, in1=rq[:, u_, :], op0=ALU.mult, op1=ALU.mult),
                             reads=[f"{L}tmp{n}", L + "rq"], writes=[f"{LP}kn32{hd}"])
                        P.op('act', lambda e, hd=hd: e.activation(out=kn[:, hd, :], in_=kn32[:, hd, :], func=AF.Identity), reads=[f"{LP}kn32{hd}"], writes=[f"{LP}kn{hd}"])
            for hd in range(4):
                ps, psk = self.psum()
                for k in range(KC):
                    P.op('pe', lambda e, hd=hd, k=k, ps=ps: e.matmul(ps[:, 0:TT], lhsT=wgd[:, k, 1536 + hd * 128:1536 + (hd + 1) * 128], rhs=h[:, k, :],
                                                                      start=(k == 0), stop=(k == KC - 1)), reads=[wk[k], hk[k]], writes=[psk])
                ct = cvt[hd % 2]; ctk = L + f"cvt{hd % 2}"
                P.op('act', lambda e, ct=ct, ps=ps: e.activation(out=ct[:], in_=ps[:, 0:TT], func=AF.Tanh, scale=0.5), reads=[psk], writes=[ctk])
                P.op('dve', lambda e, ct=ct, ps=ps, hd=hd: e.scalar_tensor_tensor(out=zs[:, hd, :], in0=ct[:], scalar=1.0, in1=ps[:, 0:TT], op0=ALU.add, op1=ALU.mult),
                     reads=[ctk, psk], writes=[f"{LP}zs{hd}"])
            ps, psk = self.psum()
            for c in range(NCH):
                for k in range(KC):
                    P.op('pe', lambda e, c=c, k=k, ps=ps: e.matmul(ps[0:64, c * 8:(c + 1) * 8], lhsT=h[:, k, c * 64:(c + 1) * 64], rhs=wgd[:, k, 2048:2056],
                                                                    start=(k == 0), stop=(k == KC - 1)), reads=[wk[k], hk[k]], writes=[psk])
            P.op('act', lambda e, ps=ps: e.activation(out=ab[:], in_=ps[0:64, 0:NCH * 8].rearrange("p (c n) -> p c n", n=8), func=AF.Identity),
                 reads=[psk], writes=[L + "ab"])
            P.op('act', lambda e: e.activation(out=beta[:], in_=ab[:, :, 4:8], func=AF.Tanh, scale=0.5), reads=[L + "ab"], writes=[LP + "beta"])
            P.op('dve', lambda e: e.tensor_scalar(out=nbeta[:], in0=beta[:], scalar1=-0.5, scalar2=-0.5, op0=ALU.mult, op1=ALU.add),
                 reads=[LP + "beta"], writes=[LP + "nbeta"])
            P.op('dve', lambda e: e.tensor_scalar(out=beta[:], in0=beta[:], scalar1=0.5, scalar2=0.5, op0=ALU.mult, op1=ALU.add),
                 reads=[LP + "beta", LP + "nbeta"], writes=[LP + "beta"])
            P.op('dve', lambda e: e.tensor_tensor(out=gt[:], in0=ab[:, :, 0:4], in1=bc(tokc[:, 1:2, :], [64, NCH, 4]), op=ALU.add),
                 reads=[L + "ab", L + "tokc"], writes=[LP + "gt"])
            P.op('act', lambda e: e.activation(out=gt[:], in_=gt[:], func=AF.Exp), reads=[LP + "gt"], writes=[LP + "gt"])
            P.op('act', lambda e: e.activation(out=gt[:], in_=gt[:], func=AF.Ln, bias=1.0), reads=[LP + "gt"], writes=[LP + "gt"])
            P.op('dve', lambda e: e.tensor_tensor(out=gt[:], in0=gt[:], in1=bc(nA[:, None, :], [64, NCH, 4]), op=ALU.mult),
                 reads=[LP + "gt", L + "nA"], writes=[LP + "gt"])
            Fl.append(self.defer_end())
            def do_head(hd):
                SS = SCR[hd // 2]
                LS = SS["LS"]
                BSL, BUT, BI, eD, eDT, eDTs, bbc, Nm, eGl, Pm, PTm, TW = (SS[k_] for k_ in ("BSL", "BUT", "BI", "eD", "eDT", "eDTs", "bbc", "Nm", "eGl", "Pm", "PTm", "TW"))
                gh = gt[:, :, hd:hd + 1]
                bh = beta[:, :, hd:hd + 1]
                nbh = nbeta[:, :, hd:hd + 1]
                P.op('dve', lambda e, gh=gh: e.tensor_tensor(out=BSL[:], in0=bc(SLm[:, None, :], [64, NCH, 64]), in1=bc(gh, [64, NCH, 64]), op=ALU.mult),
                     reads=[LP + "gt", L + "cm"], writes=[LS + "BSL"])
                P.op('dve', lambda e, gh=gh: e.tensor_tensor(out=BUT[:], in0=bc(UTm[:, None, :], [64, NCH, 64]), in1=bc(gh, [64, NCH, 64]), op=ALU.mult),
                     reads=[LP + "gt", L + "cm"], writes=[LS + "BUT"])
                P.op('dve', lambda e, bh=bh: e.tensor_tensor(out=BI[:], in0=bc(Im[:, None, :], [64, NCH, 64]), in1=bc(bh, [64, NCH, 64]), op=ALU.mult),
                     reads=[LP + "beta", L + "cm"], writes=[LS + "BI"])
                BSLf = BSL[:].rearrange("p c t -> p (c t)")
                BUTf = BUT[:].rearrange("p c t -> p (c t)")
                BIf = BI[:].rearrange("p c t -> p (c t)")
                pX, pXk = self.psum()
                pD, pDk = pX, pXk
                P.op('pe', lambda e, pD=pD, BSLf=BSLf: e.matmul(pD[0:64, 0:NW], lhsT=UTm, rhs=BSLf, start=True, stop=True), reads=[LS + "BSL", L + "cm"], writes=[pDk])
                pDT, pDTk = pX, pXk
                for c in range(NCH):
                    P.op('pe', lambda e, c=c, pDT=pDT: e.matmul(pDT[0:64, NW + c * 64:NW + (c + 1) * 64], lhsT=BSL[:, c, :], rhs=UTm, start=True, stop=True),
                         reads=[LS + "BSL", L + "cm"], writes=[pDTk])
                pG, pGk = self.psum()
                P.op('pe', lambda e, pG=pG, BUTf=BUTf: e.matmul(pG[:, 0:NW], lhsT=ones64[:], rhs=BUTf, start=True, stop=True), reads=[LS + "BUT", L + "ones64"], writes=[pGk])
                pB, pBk = pG, pGk
                P.op('pe', lambda e, pB=pB, BIf=BIf: e.matmul(pB[0:64, NW:2 * NW], lhsT=ones64[:, 0:64], rhs=BIf, start=True, stop=True), reads=[LS + "BI", L + "ones64"], writes=[pBk])
                P.op('act', lambda e, pD=pD: e.activation(out=eD[:].rearrange("p c t -> p (c t)"), in_=pD[0:64, 0:NW], func=AF.Exp), reads=[pDk], writes=[LS + "eD"])
                P.op('act', lambda e, pDT=pDT: e.activation(out=eDT[:].rearrange("p c t -> p (c t)"), in_=pDT[0:64, NW:2 * NW], func=AF.Exp), reads=[pDTk], writes=[LS + "eDT"])
                P.op('act', lambda e, pG=pG, hd=hd: e.activation(out=eG[hd][:], in_=pG[:, 0:NW], func=AF.Exp), reads=[pGk], writes=[L + f"eG{hd}"])
                P.op('act', lambda e, pB=pB: e.activation(out=bbc[:].rearrange("p c t -> p (c t)"), in_=pB[0:64, NW:2 * NW], func=AF.Identity), reads=[pBk], writes=[LS + "bbc"])
                P.op('dve', lambda e: e.tensor_tensor(out=eD[:], in0=eD[:], in1=bc(SLm[:, None, :], [64, NCH, 64]), op=ALU.mult), reads=[LS + "eD", L + "cm"], writes=[LS + "eD"])
                P.op('dve', lambda e: e.tensor_tensor(out=eDT[:], in0=eDT[:], in1=bc(UTm[:, None, :], [64, NCH, 64]), op=ALU.mult), reads=[LS + "eDT", L + "cm"], writes=[LS + "eDT"])
                P.op('dve', lambda e: e.tensor_tensor(out=eDTs[:], in0=eDT[:], in1=bc(cm[:, 3:4, :], [64, NCH, 64]), op=ALU.mult), reads=[LS + "eDT", L + "cm"], writes=[LS + "eDTs"])
                pK, pKk = self.psum()
                pQ, pQk = pK, pKk
                for c in range(NCH):
                    P.op('pe', lambda e, c=c, pK=pK, hd=hd: e.matmul(pK[0:64, c * 64:(c + 1) * 64], lhsT=kn[:, hd, c * 64:(c + 1) * 64], rhs=kn[:, hd, c * 64:(c + 1) * 64],
                                                                      start=True, stop=True), reads=[f"{LP}kn{hd}"], writes=[pKk])
                for c in range(NCH):
                    P.op('pe', lambda e, c=c, pQ=pQ, hd=hd: e.matmul(pQ[0:64, NW + c * 64:NW + (c + 1) * 64], lhsT=kn[:, hd, c * 64:(c + 1) * 64], rhs=qn[:, hd, c * 64:(c + 1) * 64],
                                                                      start=True, stop=True), reads=[f"{LP}kn{hd}", f"{LP}qn{hd}"], writes=[pQk])
                pK3 = pK[0:64, 0:NW].rearrange("p (c t) -> p c t", t=64)
                pQ3 = pQ[0:64, NW:2 * NW].rearrange("p (c t) -> p c t", t=64)
                P0, P0k = Pm[0], LS + "Pm0"
                PT0, PT0k = PTm[0], LS + "PTm0"
                AT, ATk = ATm[hd], L + f"AT{hd}"
                P.op('dve', lambda e, pK3=pK3, nbh=nbh: e.tensor_tensor(out=Nm[:], in0=pK3, in1=bc(nbh, [64, NCH, 64]), op=ALU.mult), reads=[pKk, LP + "nbeta"], writes=[LS + "Nm"])
                P.op('dve', lambda e, P0=P0: e.tensor_tensor(out=P0[:], in0=Nm[:], in1=eD[:], op=ALU.mult), reads=[LS + "Nm", LS + "eD"], writes=[P0k])
                P.op('dve', lambda e, pK3=pK3: e.tensor_tensor(out=Nm[:], in0=pK3, in1=eDTs[:], op=ALU.mult), reads=[pKk, LS + "eDTs", P0k], writes=[LS + "Nm"])
                P.op('dve', lambda e, PT0=PT0: e.scalar_tensor_tensor(out=PT0[:], in0=Nm[:], scalar=-1.0, in1=bbc[:], op0=ALU.mult, op1=ALU.mult),
                     reads=[LS + "Nm", LS + "bbc"], writes=[PT0k])
                P.op('dve', lambda e, pQ3=pQ3, hd=hd: e.tensor_tensor(out=attnT[hd][:], in0=pQ3, in1=eDT[:], op=ALU.mult), reads=[pQk, LS + "eDT"], writes=[L + f"attnT{hd}"])
                self.tri_inv(P0, P0k, PT0, PT0k, AT, ATk, NCH, TW)
                P.op('dve', lambda e, hd=hd: e.tensor_tensor(out=kgT[hd][:], in0=kn32[:, hd, :], in1=eG[hd][:], op=ALU.mult), reads=[f"{LP}kn32{hd}", L + f"eG{hd}"], writes=[L + f"kgT{hd}"])
                P.op('dve', lambda e, hd=hd: e.tensor_tensor(out=qdT[hd][:], in0=qn[:, hd, :], in1=eG[hd][:], op=ALU.mult), reads=[f"{LP}qn{hd}", L + f"eG{hd}"], writes=[L + f"qdT{hd}"])
                pv, pvk = self.psum()
                for c in range(NCH):
                    P.op('pe', lambda e, c=c, pv=pv, hd=hd: e.transpose(out=pv[0:64, c * 128:(c + 1) * 128], in_=vT[:, hd, c * 64:(c + 1) * 64], identity=ident[:]),
                         reads=[f"{LP}vT{hd}", L + "ident"], writes=[pvk])
                P.op('dve', lambda e, pv=pv, hd=hd, bh=bh: e.scalar_tensor_tensor(
                    out=vb[:, :, hd, :], in0=pv[0:64, 0:NCH * 128].rearrange("p (c d) -> p c d", d=128), scalar=0.5, in1=bc(bh, [64, NCH, 128]), op0=ALU.mult, op1=ALU.mult),
                    reads=[pvk, LP + "beta"], writes=[L + f"vb{hd}"])
                pk_, pkk_ = self.psum()
                for c in range(NCH):
                    P.op('pe', lambda e, c=c, pk_=pk_, hd=hd: e.transpose(out=pk_[0:64, c * 128:(c + 1) * 128], in_=kn32[:, hd, c * 64:(c + 1) * 64], identity=ident[:]),
                         reads=[f"{LP}kn32{hd}", L + "ident"], writes=[pkk_])
                P.op('dve', lambda e, pk_=pk_, hd=hd: e.tensor_tensor(
                    out=kdec[:, :, hd, :], in0=pk_[0:64, 0:NCH * 128].rearrange("p (c d) -> p c d", d=128), in1=bc(eDT[:, :, 63:64], [64, NCH, 128]), op=ALU.mult),
                    reads=[pkk_, LS + "eDT"], writes=[L + f"kdec{hd}"])
            self.defer_begin(pool=[1, 2, 3])
            do_head(0)
            do_head(1)
            HAl.append(self.defer_end())
            self.defer_begin(pool=[4, 5, 6])
            do_head(2)
            do_head(3)
            HBl.append(self.defer_end())
            def scan_stream(h0, sid):
                hsl = slice(h0, h0 + 2)
                S32k, Sbfk, Rtk, Rbk, vnk = (L + f"{nm}_{sid}" for nm in ("S32", "Sbf", "Rt", "Rb", "vnew"))
                for c in range(NCH):
                    p1, p1k = self.psum()
                    for hd in (h0, h0 + 1):
                        P.op('pe', lambda e, c=c, hd=hd, p1=p1: e.matmul(p1[0:64, (hd - h0) * 128:(hd - h0 + 1) * 128], lhsT=kgT[hd][:, c * 64:(c + 1) * 64], rhs=Sbf[:, hd, :], start=True, stop=True),
                             reads=[L + f"kgT{hd}", Sbfk], writes=[p1k])
                    P.op('dve', lambda e, c=c, p1=p1: e.tensor_tensor(out=Rt[:, hsl, :], in0=p1[0:64, 0:256].rearrange("p (h d) -> p h d", d=128), in1=bc(nbeta[:, c, hsl, None], [64, 2, 128]), op=ALU.mult),
                         reads=[p1k, LP + "nbeta"], writes=[Rtk])
                    P.op('dve', lambda e, c=c: e.tensor_tensor(out=Rb[:, hsl, :], in0=Rt[:, hsl, :], in1=vb[:, c, hsl, :], op=ALU.add), reads=[Rtk, L + f"vb{h0}", L + f"vb{h0 + 1}"], writes=[Rbk])
                    p2, p2k = self.psum()
                    for hd in (h0, h0 + 1):
                        P.op('pe', lambda e, c=c, hd=hd, p2=p2: e.matmul(p2[0:64, (hd - h0) * 128:(hd - h0 + 1) * 128], lhsT=ATm[hd][:, c, :], rhs=Rb[:, hd, :], start=True, stop=True),
                             reads=[L + f"AT{hd}", Rbk], writes=[p2k])
                    P.op('act', lambda e, p2=p2: e.activation(out=vnew[:, hsl, :], in_=p2[0:64, 0:256].rearrange("p (h d) -> p h d", d=128), func=AF.Identity), reads=[p2k], writes=[vnk])
                    p3, p3k = self.psum()
                    for hd in (h0, h0 + 1):
                        P.op('pe', lambda e, c=c, hd=hd, p3=p3: e.matmul(p3[:, (hd - h0) * 64:(hd - h0 + 1) * 64], lhsT=Sbf[:, hd, :], rhs=qdT[hd][:, c * 64:(c + 1) * 64], start=True, stop=False),
                             reads=[Sbfk, L + f"qdT{hd}"], writes=[p3k])
                        P.op('pe', lambda e, c=c, hd=hd, p3=p3: e.matmul(p3[:, (hd - h0) * 64:(hd - h0 + 1) * 64], lhsT=vnew[:, hd, :], rhs=attnT[hd][:, c, :], start=False, stop=True),
                             reads=[vnk, L + f"attnT{hd}"], writes=[p3k])
                    P.op('act', lambda e, c=c, p3=p3: e.activation(out=oT[:, hsl, c * 64:(c + 1) * 64], in_=p3[:, 0:128].rearrange("p (h t) -> p h t", t=64), func=AF.Identity),
                         reads=[p3k], writes=[L + f"oT{h0}", L + f"oT{h0 + 1}"])
                    p4, p4k = self.psum()
                    for hd in (h0, h0 + 1):
                        P.op('pe', lambda e, c=c, hd=hd, p4=p4: e.matmul(p4[:, (hd - h0) * 128:(hd - h0 + 1) * 128], lhsT=kdec[:, c, hd, :], rhs=vnew[:, hd, :], start=True, stop=True),
                             reads=[L + f"kdec{hd}", vnk], writes=[p4k])
                    for hd in (h0, h0 + 1):
                        P.op('dve', lambda e, c=c, hd=hd, p4=p4: e.scalar_tensor_tensor(
                            out=S32[:, hd, :], in0=S32[:, hd, :], scalar=eG[hd][:, c * 64 + 63:c * 64 + 64], in1=p4[:, (hd - h0) * 128:(hd - h0 + 1) * 128], op0=ALU.mult, op1=ALU.add),
                            reads=[S32k, L + f"eG{hd}", p4k], writes=[S32k])
                    P.op('act', lambda e: e.activation(out=Sbf[:, hsl, :], in_=S32[:, hsl, :], func=AF.Identity), reads=[S32k], writes=[Sbfk])

            streams_ = []
            for sid, h0 in enumerate((0, 2)):
                self.defer_begin(pool=[1, 2, 3] if sid == 0 else [4, 5, 6])
                scan_stream(h0, sid)
                streams_.append(self.defer_end())
            SCl.append(streams_)
            self.defer_begin(pool=[1])
            P.op('act', lambda e: e.activation(out=sqo[:], in_=oT[:], func=AF.Square), reads=[L + f"oT{hd}" for hd in range(4)], writes=[L + "sqo"])
            EPl0.append(self.defer_end())
            chains_ = []
            for hd in range(4):
                self.defer_begin(pool=[1 + hd])
                po, pok = self.psum()
                P.op('pe', lambda e, hd=hd, po=po: e.matmul(po[:, 0:TT], lhsT=self.ones128_bf[:], rhs=sqo[:, hd, 0:TT], start=True, stop=True), reads=[L + "sqo"], writes=[pok])
                self.rsqrt_ps(tmp[:, hd, :], f"{L}tmp{hd}", po[:, 0:TT], pok, EPS)
                P.op('dve', lambda e, hd=hd: e.scalar_tensor_tensor(out=ob[:, hd, :], in0=oT[:, hd, :], scalar=nw[:, 0:1], in1=tmp[:, hd, :], op0=ALU.mult, op1=ALU.mult),
                     reads=[L + f"oT{hd}", L + "nw", f"{L}tmp{hd}"], writes=[L + f"ob{hd}"])
                P.op('dve', lambda e, hd=hd: e.scalar_tensor_tensor(out=ob[:, hd, :], in0=ob[:, hd, :], scalar=0.5, in1=zs[:, hd, :], op0=ALU.mult, op1=ALU.mult),
                     reads=[L + f"ob{hd}", f"{LP}zs{hd}"], writes=[L + f"ob{hd}"])
                chains_.append(self.defer_end())
            EPl.append(chains_)
            self.defer_begin(pool=[1])
            P.op('sp', lambda e, t0=t0: e.dma_start(out=gout_v[:, :, t0:t0 + TT], in_=ob[:]), reads=[L + f"ob{hd}" for hd in range(4)], writes=[self.key('gout')], dma=True)
            Bl.append(self.defer_end())
        NTL = self.NT // TT
        for j in range(NTL):
            do_tile(j)
        self.replay([Fl[0]])
        for j in range(NTL):
            self.replay([HAl[j], HBl[j]] + ([Fl[j + 1]] if j + 1 < NTL else []))
            self.replay(SCl[j])
            self.replay([EPl0[j]])
            self.replay(EPl[j])
            self.replay([Bl[j]])
        self.end()


def declare_inputs(b, NT):
    I = {}
    I['cT'] = b.din("cT", [128, 8]); I['ada_w'] = b.din("ada_w", [4, 8, 128, 3072]); I['ada_b'] = b.din("ada_b", [4, 128, 24])
    I['npre'] = b.din("npre", [4, 128, 8]); I['npost'] = b.din("npost", [4, 128, 8])
    I['xT'] = b.din("xT", [1024, NT])
    I['wgd'] = b.din("wgd", [128, 8, 2056]); I['g_cvec'] = b.din("g_cvec", [128, 4, 12]); I['g_tok'] = b.din("g_tok", [64, 2, 4])
    I['g_nw'] = b.din("g_nw", [128, 1]); I['cm'] = b.din("cm", [64, 5, 64]); I['timk'] = b.din("timk", [64, 13, 64])
    I['wrw'] = b.din("wrw", [128, 8, 1792]); I['mwout'] = b.din("mwout", [128, 8, 1024])
    I['rw_w2a2'] = b.din("rw_w2a2", [128, 1, 512]); I['rw_g2'] = b.din("rw_g2", [128, 1, 512])
    I['rw_mu'] = b.din("rw_mu", [128, 14]); I['rw_vec'] = b.din("rw_vec", [128, 8, 4])
    return I


def common_maps(inp):
    m = {}
    m["ada_w"] = np.ascontiguousarray(np.asarray(inp['ada_w'], np.float32).reshape(4, 8, 128, 3072))
    m["ada_b"] = np.stack([fm_vec(np.asarray(inp['ada_b']).reshape(4, 3072)[s]) for s in range(4)])
    m["npre"] = np.stack([fm_vec(np.asarray(inp['norm_pre']).reshape(4, 1024)[s]) for s in range(4)])
    m["npost"] = np.stack([fm_vec(np.asarray(inp['norm_post']).reshape(4, 1024)[s]) for s in range(4)])
    win = np.asarray(inp['mix_w_in'][0], np.float32)
    m["wgd"] = fm_mat(win[:, :2056])
    cw = np.asarray(inp['gdn_conv_w'][0], np.float32)
    m["g_cvec"] = np.ascontiguousarray(np.stack([fm_vec(cw[j]) for j in range(4)], axis=1))
    m["g_tok"] = np.ascontiguousarray(np.broadcast_to(np.stack([np.asarray(inp['gdn_a_log'][0]), np.asarray(inp['gdn_dt_bias'][0])])[None], (64, 2, 4)).astype(np.float32))
    m["g_nw"] = np.ascontiguousarray(np.asarray(inp['gdn_norm_w'][0], np.float32).reshape(128, 1))
    s = np.arange(64)[:, None]; t = np.arange(64)[None, :]
    m["cm"] = np.ascontiguousarray(np.stack([(s > t), (s <= t), (s == t), (s < t), np.ones((64, 64), bool)], axis=1).astype(np.float32))
    m["wrw"] = fm_mat(win[:, 2056:])
    m["mwout"] = fm_mat(np.asarray(inp['mix_w_out'][0], np.float32))
    m["rw_w2a2"] = np.ascontiguousarray(np.concatenate([np.asarray(inp['rwkv_w2'][0]), np.asarray(inp['rwkv_a2'][0])], axis=0).astype(np.float32).reshape(128, 1, 512))
    m["rw_g2"] = np.ascontiguousarray(np.asarray(inp['rwkv_g2'][0], np.float32).reshape(128, 1, 512))
    m["rw_mu"] = fm_vec(np.asarray(inp['rwkv_mu'][0]))
    z4 = np.zeros(512, np.float32)
    m["rw_vec"] = np.ascontiguousarray(np.stack([fm_vec(np.asarray(inp[k][0]).reshape(-1)) for k in
                                                 ('rwkv_w0', 'rwkv_a0', 'rwkv_k_k', 'rwkv_k_a', 'rwkv_r_k', 'rwkv_ln_w', 'rwkv_ln_b')] + [fm_vec(z4)], axis=1))
    mk = []
    for l in range(6):
        bsz = 2 ** l
        mk.append(((s // (2 * bsz)) == (t // (2 * bsz))) & ((s % (2 * bsz)) >= bsz) & ((t % (2 * bsz)) < bsz))
    mk = np.stack(mk + [m_.T for m_ in mk] + [s == t], axis=1).astype(np.float32)
    m["timk"] = np.ascontiguousarray(mk)
    return m


def core_maps(inp, i, NT):
    return {"cT": fm_vec(np.asarray(inp['c'][i])), "xT": np.ascontiguousarray(np.asarray(inp['x'][i, :NT], np.float32).T)}


def _rwkv_section(self, x_in, gout, x_out, I, rdbg=None):
    P = self.begin("rwkv")
    s = 0
    TT = 256
    NCH = TT // 64
    L = "rk_"
    wrw = P.sb([128, KC, 1792], BF16, L + "wrw")
    wout = P.sb([128, KC, D], BF16, L + "wout")
    w2a2 = P.sb([128, 1, 512], BF16, L + "w2a2")
    g2 = P.sb([128, 1, 512], BF16, L + "g2")
    self.stg_cols = 1024
    self.defer_begin()
    self.load_w_bf16(wrw, L + "wrw", I['wrw'], KC, 1792, piece=896)
    self.load_w_bf16(w2a2, L + "w2a2", I['rw_w2a2'], 1, 512)
    self.load_w_bf16(g2, L + "g2", I['rw_g2'], 1, 512)
    self.load_w_bf16(wout, L + "wout", I['mwout'], KC, D, piece=1024)
    WL = self.defer_end()
    self.stg_cols = 1408
    wk = [f"{L}wrw{k}" for k in range(KC)]
    mu = P.sb([128, 14], F32, L + "mu")
    rv = P.sb([128, 8, 4], F32, L + "rv")
    cm = P.sb([64, 5, 64], F32, L + "cm")
    P.op('sp', lambda e: e.dma_start(out=mu[:], in_=I['rw_mu']), writes=[L + "mu"], dma=True)
    P.op('sp', lambda e: e.dma_start(out=rv[:], in_=I['rw_vec']), writes=[L + "rv"], dma=True)
    P.op('sp', lambda e: e.dma_start(out=cm[:], in_=I['cm']), writes=[L + "cm"], dma=True)
    SLm, UTm, SUm = cm[:, 0, :], cm[:, 1, :], cm[:, 3, :]
    ident = P.sb([128, 128], F32, L + "ident")
    P.op('pool', lambda e: e.memset(ident[:], 0.0), writes=[L + "ident"])
    P.op('sp', lambda e: e.dma_start(out=ident[0:64, 0:64], in_=I['cm'][:, 2, :]), reads=[L + "ident"], writes=[L + "ident"], dma=True)
    P.op('sp', lambda e: e.dma_start(out=ident[64:128, 64:128], in_=I['cm'][:, 2, :]), reads=[L + "ident"], writes=[L + "ident"], dma=True)
    bd1 = P.sb([128, 128], BF16, L + "bd1")
    bd64 = P.sb([128, 128], BF16, L + "bd64")
    for t_, val, nm in ((bd1, 1.0, "bd1"), (bd64, 1.0 / 64.0, "bd64")):
        P.op('pool', lambda e, t_=t_: e.memset(t_[:], 0.0), writes=[L + nm])
        P.op('pool', lambda e, t_=t_, val=val: e.memset(t_[0:64, 0:64], val), reads=[L + nm], writes=[L + nm])
        P.op('pool', lambda e, t_=t_, val=val: e.memset(t_[64:128, 64:128], val), reads=[L + nm], writes=[L + nm])
    rmask = P.sb([128, TT], F32, L + "rmask")
    P.op('pool', lambda e: e.memset(rmask[:], 1.0), writes=[L + "rmask"])
    P.op('pool', lambda e: e.memset(rmask[:].rearrange("p (c t) -> p c t", t=64)[:, :, 0:1], 0.0), reads=[L + "rmask"], writes=[L + "rmask"])
    dv = P.sb([128, 3, 4], F32, L + "dv")
    P.op('dve', lambda e: e.tensor_scalar(out=dv[:, 0:2, :], in0=rv[:, 0:2, :], scalar1=0.5, scalar2=None, op0=ALU.mult), reads=[L + "rv"], writes=[L + "dv"])
    P.op('dve', lambda e: e.tensor_scalar(out=dv[:, 2, :], in0=rv[:, 3, :], scalar1=-1.0, scalar2=1.0, op0=ALU.mult, op1=ALU.add), reads=[L + "rv", L + "dv"], writes=[L + "dv"])
    W0H, A0H, OMKA = dv[:, 0, :], dv[:, 1, :], dv[:, 2, :]
    KK_, KA_, RK_, LNW, LNB = rv[:, 2, :], rv[:, 3, :], rv[:, 4, :], rv[:, 5, :], rv[:, 6, :]
    self.replay([WL])
    Z32 = P.sb([64, 8, 64], F32, L + "Z32")
    Zbf = P.sb([64, 8, 64], BF16, L + "Zbf")
    P.op('pool', lambda e: e.memset(Z32[:], 0.0), writes=[L + "Z32"])
    P.op('pool', lambda e: e.memset(Zbf[:], 0.0), writes=[L + "Zbf"])
    xr = P.sb([128, 14, TT + 1], F32, L + "xr")
    hst = P.sb([128, 14, 1], F32, L + "hst")
    P.op('pool', lambda e: e.memset(xr[:, :, 0:1], 0.0), writes=[L + f"xrh{n}" for n in range(14)])
    xt = P.sb([128, KC, TT], F32, L + "xt")
    h = P.sb([128, KC, TT], BF16, L + "h")
    tmp = P.sb([128, KC, TT], F32, L + "tmp")
    sq = P.sb([128, KC, TT], BF16, L + "sq")
    rstd = P.sb([128, TT], F32, L + "rstd")
    dtm = [P.sb([128, TT], F32, L + f"dtm{i}") for i in range(2)]
    twb = P.sb([128, TT], BF16, L + "twb")
    adb = P.sb([128, TT], BF16, L + "adb")
    sgb = P.sb([128, TT], BF16, L + "sgb")
    t_a = P.sb([128, 4, TT], F32, L + "t_a")
    t_g = P.sb([128, 4, TT], F32, L + "t_g")
    t_kk = P.sb([128, 4, TT], F32, L + "t_kk")
    t_kp = P.sb([128, 4, TT], F32, L + "t_kp")
    t_lw = P.sb([128, 4, TT], F32, L + "t_lw")
    t_LW = P.sb([128, 4, TT], F32, L + "t_LW")
    t_eP = P.sb([128, 4, TT], F32, L + "t_eP")
    t_eM = P.sb([128, 4, TT], F32, L + "t_eM")
    rq4 = P.sb([128, 4, TT], F32, L + "rq4")
    rt = P.sb([128, 4, TT], BF16, L + "rt")
    kt = P.sb([128, 4, TT], BF16, L + "kt")
    bt = P.sb([128, 4, TT], BF16, L + "bt")
    kkt = P.sb([128, 4, TT], BF16, L + "kkt")
    rtL = P.sb([64, 4, TT], BF16, L + "rtL")
    ktL = P.sb([64, 4, TT], BF16, L + "ktL")
    btL = P.sb([64, 4, TT], BF16, L + "btL")
    kktL = P.sb([64, 4, TT], BF16, L + "kktL")
    ePL = P.sb([64, 4, NCH, 1], F32, L + "ePL")
    wcall = P.sb([64, NCH, 8], F32, L + "wcall")
    Vtok = P.sb([64, NCH, 512], BF16, L + "Vtok")
    Ktok = P.sb([64, NCH, 512], BF16, L + "Ktok")
    Btok = P.sb([64, NCH, 512], BF16, L + "Btok")
    Nm_ = [P.sb([64, 8, 64], BF16, L + f"N{i}") for i in range(2)]
    NTm_ = [P.sb([64, 8, 64], BF16, L + f"NT{i}") for i in range(2)]
    ATm = [P.sb([64, 8, 64], BF16, L + f"AT{i}") for i in range(2)]
    LkT = [P.sb([64, 8, 64], BF16, L + f"LkT{i}") for i in range(2)]
    RkT = [P.sb([64, 8, 64], BF16, L + f"RkT{i}") for i in range(2)]
    RbT = [P.sb([64, 8, 64], BF16, L + f"RbT{i}") for i in range(2)]
    TW = self.tri_inv_alloc(L, 8, I['timk'])
    X1b = P.sb([64, 8, 64], BF16, L + "X1b")
    Eb = P.sb([64, 8, 64], BF16, L + "Eb")
    yT = P.sb([128, 4, TT], F32, L + "yT")
    ymix = h
    gld = t_LW
    yodd = t_lw
    xin_v = x_in.rearrange("(k p) t -> p k t", p=128)
    xout_v = x_out.rearrange("(k p) t -> p k t", p=128)
    gout_v = gout.rearrange("(k p) t -> p k t", p=128)
    rdbg_v = rdbg.rearrange("(k p) t -> p k t", p=128) if rdbg is not None else None

    def bc(ap, shape):
        return ap.to_broadcast(shape)

    def fl(t):
        return t[:].rearrange("p a t -> p (a t)")

    for j in range(self.NT // TT):
        t0 = j * TT
        xk = L + "xt"
        P.op('act', lambda e, t0=t0: e.dma_start(out=xt[:], in_=xin_v[:, :, t0:t0 + TT]), writes=[xk], dma=True)
        self.prenorm(xt, xk, TT, s, sq, L + "sq", rstd, L + "rstd", tmp, L + "tmp", h, L + "h")
        hk = [f"{L}h{k}" for k in range(KC)]
        for n in range(14):
            ps, psk = self.psum()
            for k in range(KC):
                P.op('pe', lambda e, n=n, k=k, ps=ps: e.matmul(ps[:, 0:TT], lhsT=wrw[:, k, n * 128:(n + 1) * 128], rhs=h[:, k, :],
                                                                start=(k == 0), stop=(k == KC - 1)), reads=[wk[k], hk[k]], writes=[psk])
            P.op('act', lambda e, n=n, ps=ps: e.activation(out=xr[:, n, 1:1 + TT], in_=ps[:, 0:TT], func=AF.Identity), reads=[psk], writes=[f"{L}xrm{n}"])
            P.op('pool', lambda e, n=n: e.tensor_copy(out=hst[:, n, :], in_=xr[:, n, TT:TT + 1]), reads=[f"{L}xrm{n}"], writes=[f"{L}hst{n}"])
            dt_ = dtm[n % 2]; dtk = L + f"dtm{n % 2}"
            P.op('dve', lambda e, n=n, dt_=dt_: e.tensor_tensor(out=dt_[:], in0=xr[:, n, 0:TT], in1=xr[:, n, 1:1 + TT], op=ALU.subtract),
                 reads=[f"{L}xrm{n}", f"{L}xrh{n}"], writes=[dtk])
            P.op('dve', lambda e, n=n, dt_=dt_: e.scalar_tensor_tensor(out=xr[:, n, 1:1 + TT], in0=dt_[:], scalar=mu[:, n:n + 1], in1=xr[:, n, 1:1 + TT], op0=ALU.mult, op1=ALU.add),
                 reads=[dtk, L + "mu", f"{L}xrm{n}", f"{L}hst{n}"], writes=[f"{L}xrm{n}"])
            P.op('pool', lambda e, n=n: e.tensor_copy(out=xr[:, n, 0:1], in_=hst[:, n, :]), reads=[f"{L}hst{n}", dtk], writes=[f"{L}xrh{n}"])
        R_ = lambda i: xr[:, i, 1:1 + TT]
        K_ = lambda i: xr[:, 4 + i, 1:1 + TT]
        V_ = lambda i: xr[:, 8 + i, 1:1 + TT]
        rkeys = [f"{L}xrm{i}" for i in range(4)]
        kkeys = [f"{L}xrm{4 + i}" for i in range(4)]
        vkeys = [f"{L}xrm{8 + i}" for i in range(4)]
        P.op('act', lambda e: e.activation(out=twb[0:64, :], in_=xr[0:64, 12, 1:1 + TT], func=AF.Tanh), reads=[f"{L}xrm12"], writes=[L + "twb"])
        P.op('act', lambda e: e.activation(out=adb[64:128, :], in_=xr[64:128, 12, 1:1 + TT], func=AF.Identity), reads=[f"{L}xrm12"], writes=[L + "adb"])
        d0 = dtm[0]; d0k = L + "dtm0"
        P.op('act', lambda e: e.activation(out=d0[:], in_=xr[:, 13, 1:1 + TT], func=AF.Tanh, scale=0.5), reads=[f"{L}xrm13"], writes=[d0k])
        P.op('dve', lambda e: e.tensor_scalar(out=sgb[:], in0=d0[:], scalar1=0.5, scalar2=0.5, op0=ALU.mult, op1=ALU.add), reads=[d0k], writes=[L + "sgb"])
        for i in range(4):
            pw, pwk = self.psum()
            P.op('pe', lambda e, i=i, pw=pw: e.matmul(pw[:, 0:TT], lhsT=w2a2[0:64, 0, i * 128:(i + 1) * 128], rhs=twb[0:64, :], start=True, stop=True),
                 reads=[L + "w2a20", L + "twb"], writes=[pwk])
            d1 = dtm[1]; d1k = L + "dtm1"
            P.op('act', lambda e, i=i, pw=pw: e.activation(out=d1[:], in_=pw[:, 0:TT], func=AF.Tanh, scale=0.5, bias=W0H[:, i:i + 1]), reads=[pwk, L + "dv"], writes=[d1k])
            P.op('dve', lambda e, i=i: e.tensor_scalar(out=t_lw[:, i, :], in0=d1[:], scalar1=1.0, scalar2=-0.30326533, op0=ALU.add, op1=ALU.mult), reads=[d1k], writes=[f"{L}t_lw{i}"])
            pa, pak = self.psum()
            P.op('pe', lambda e, i=i, pa=pa: e.matmul(pa[:, 0:TT], lhsT=w2a2[64:128, 0, i * 128:(i + 1) * 128], rhs=adb[64:128, :], start=True, stop=True),
                 reads=[L + "w2a20", L + "adb"], writes=[pak])
            P.op('act', lambda e, i=i, pa=pa: e.activation(out=d0[:], in_=pa[:, 0:TT], func=AF.Tanh, scale=0.5, bias=A0H[:, i:i + 1]), reads=[pak, L + "dv"], writes=[d0k])
            P.op('dve', lambda e, i=i: e.tensor_scalar(out=t_a[:, i, :], in0=d0[:], scalar1=0.5, scalar2=0.5, op0=ALU.mult, op1=ALU.add), reads=[d0k], writes=[f"{L}t_a{i}"])
            pg, pgk = self.psum()
            P.op('pe', lambda e, i=i, pg=pg: e.matmul(pg[:, 0:TT], lhsT=g2[:, 0, i * 128:(i + 1) * 128], rhs=sgb[:], start=True, stop=True),
                 reads=[L + "g20", L + "sgb"], writes=[pgk])
            P.op('act', lambda e, i=i, pg=pg: e.activation(out=t_g[:, i, :], in_=pg[:, 0:TT], func=AF.Identity), reads=[pgk], writes=[f"{L}t_g{i}"])
        chains = []
        for i in range(4):
            self.defer_begin(pool=[2 * i, 2 * i + 1])
            P.op('dve', lambda e, i=i: e.tensor_scalar(out=t_kk[:, i, :], in0=K_(i), scalar1=KK_[:, i:i + 1], scalar2=None, op0=ALU.mult), reads=[kkeys[i], L + "rv"], writes=[f"{L}t_kk{i}"])
            P.op('act', lambda e, i=i: e.activation(out=sq[:, i, :], in_=t_kk[:, i, :], func=AF.Square), reads=[f"{L}t_kk{i}", L + "sq"], writes=[f"{L}sqk{i}"])
            pq, pqk = self.psum()
            P.op('pe', lambda e, i=i, pq=pq: e.matmul(pq[:, 0:TT], lhsT=bd1[:], rhs=sq[:, i, :], start=True, stop=True), reads=[f"{L}sqk{i}", L + "bd1"], writes=[pqk])
            self.rsqrt_ps(rq4[:, i, :], f"{L}rq4_{i}", pq[:, 0:TT], pqk, EPS)
            P.op('dve', lambda e, i=i: e.tensor_tensor(out=t_kk[:, i, :], in0=t_kk[:, i, :], in1=rq4[:, i, :], op=ALU.mult), reads=[f"{L}t_kk{i}", f"{L}rq4_{i}"], writes=[f"{L}t_kk{i}"])
            P.op('dve', lambda e, i=i: e.tensor_scalar(out=t_kp[:, i, :], in0=t_a[:, i, :], scalar1=KA_[:, i:i + 1], scalar2=OMKA[:, i:i + 1], op0=ALU.mult, op1=ALU.add),
                 reads=[f"{L}t_a{i}", L + "rv", L + "dv"], writes=[f"{L}t_kp{i}"])
            P.op('dve', lambda e, i=i: e.tensor_tensor(out=t_kp[:, i, :], in0=K_(i), in1=t_kp[:, i, :], op=ALU.mult), reads=[kkeys[i], f"{L}t_kp{i}"], writes=[f"{L}t_kp{i}"])
            chains.append(self.defer_end())
        self.replay(chains)
        for i in range(4):
            P.op('dve', lambda e, i=i: e.tensor_tensor_scan(out=t_LW[:, i, :], data0=rmask[:], data1=t_lw[:, i, :], initial=0.0, op0=ALU.mult, op1=ALU.add),
                 reads=[L + "rmask", f"{L}t_lw{i}"], writes=[f"{L}t_LW{i}"])
        a4 = lambda nm: [f"{L}{nm}{i}" for i in range(4)]
        P.op('act', lambda e: e.activation(out=fl(t_eP), in_=fl(t_LW), func=AF.Exp), reads=a4("t_LW"), writes=[L + "t_eP"])
        P.op('act', lambda e: e.activation(out=fl(t_eM), in_=fl(t_LW), func=AF.Exp, scale=-1.0), reads=a4("t_LW"), writes=[L + "t_eM"])
        P.op('dve', lambda e: e.tensor_tensor(out=fl(t_lw), in0=fl(t_LW), in1=fl(t_lw), op=ALU.subtract), reads=a4("t_LW") + a4("t_lw"), writes=a4("t_lw"))
        P.op('act', lambda e: e.activation(out=fl(t_lw), in_=fl(t_lw), func=AF.Exp), reads=a4("t_lw"), writes=a4("t_lw"))
        P.op('dve', lambda e: e.tensor_tensor(out=rt[:], in0=xr[:, 0:4, 1:1 + TT], in1=t_eP[:], op=ALU.mult), reads=rkeys + [L + "t_eP"], writes=[L + "rt"])
        P.op('dve', lambda e: e.tensor_tensor(out=fl(kkt), in0=fl(t_kk), in1=fl(t_lw), op=ALU.mult), reads=a4("t_kk") + a4("t_lw"), writes=[L + "kkt"])
        P.op('dve', lambda e: e.tensor_tensor(out=fl(t_kk), in0=fl(t_kk), in1=fl(t_a), op=ALU.mult), reads=a4("t_kk") + a4("t_a"), writes=a4("t_kk"))
        ePc = t_eP[:].rearrange("p a (c t) -> p a c t", t=64)[:, :, :, 63:64]
        P.op('dve', lambda e: e.tensor_tensor(out=fl(t_lw), in0=fl(t_kp), in1=fl(t_eM), op=ALU.mult), reads=a4("t_kp") + [L + "t_eM"] + a4("t_lw"), writes=a4("t_lw"))
        P.op('act', lambda e: e.activation(out=fl(kt), in_=fl(t_lw), func=AF.Identity), reads=a4("t_lw"), writes=[L + "kt"])
        P.op('dve', lambda e: e.tensor_tensor(out=t_lw[:].rearrange("p a (c t) -> p a c t", t=64), in0=t_lw[:].rearrange("p a (c t) -> p a c t", t=64),
                                              in1=bc(ePc, [128, 4, NCH, 64]), op=ALU.mult), reads=a4("t_lw") + [L + "t_eP"], writes=a4("t_lw"))
        P.op('dve', lambda e: e.tensor_tensor(out=fl(t_LW), in0=fl(t_kk), in1=fl(t_eM), op=ALU.mult), reads=a4("t_kk") + [L + "t_eM"] + a4("t_LW"), writes=a4("t_LW"))
        P.op('act', lambda e: e.activation(out=fl(bt), in_=fl(t_LW), func=AF.Identity), reads=a4("t_LW"), writes=[L + "bt"])
        P.op('dve', lambda e: e.tensor_tensor(out=t_LW[:].rearrange("p a (c t) -> p a c t", t=64), in0=t_LW[:].rearrange("p a (c t) -> p a c t", t=64),
                                              in1=bc(ePc, [128, 4, NCH, 64]), op=ALU.mult), reads=a4("t_LW") + [L + "t_eP"], writes=a4("t_LW"))
        for qi, (src_, srck_, dst_) in enumerate(((rt, L + "rt", rtL), (kt, L + "kt", ktL), (bt, L + "bt", btL), (kkt, L + "kkt", kktL))):
            P.op('sp' if qi % 2 == 0 else 'act', lambda e, src_=src_, dst_=dst_: e.dma_start(out=dst_[:], in_=src_[64:128, :, :]),
                 reads=[srck_], writes=[srck_ + "L"], dma=True)
        P.op('sp', lambda e: e.dma_start(out=ePL[:], in_=t_eP[64:128].rearrange("p a (c t) -> p a c t", t=64)[:, :, :, 63:64], allow_slow_non_contiguous=True), reads=[L + "t_eP"], writes=[L + "t_ePL"], dma=True)
        P.op('dve', lambda e: e.tensor_copy(out=wcall[:].rearrange("p c (i two) -> p c i two", two=2)[:, :, :, 0],
                                            in_=t_eP[0:64].rearrange("p a (c t) -> p c a t", t=64)[:, :, :, 63]), reads=[L + "t_eP"], writes=[L + "wcall"])
        P.op('dve', lambda e: e.tensor_copy(out=wcall[:].rearrange("p c (i two) -> p c i two", two=2)[:, :, :, 1],
                                            in_=ePL[:].rearrange("p a c o -> p c (a o)")), reads=[L + "t_ePL", L + "wcall"], writes=[L + "wcall"])
        OP = lambda X, XL, hh, cs_: (X if hh % 2 == 0 else XL)[0:64, hh // 2, cs_]
        OPK = [L + "rt", L + "kt", L + "bt", L + "kkt", L + "rtL", L + "ktL", L + "btL", L + "kktL"]
        LO = {id(rt): rtL, id(kt): ktL, id(bt): btL, id(kkt): kktL}
        for c in range(NCH):
            for src, srck, dst, dstk, sc_ in ((None, vkeys, Vtok, L + "Vtok", 1.0), (t_lw, a4("t_lw"), Ktok, L + "Ktok", 1.0), (t_LW, a4("t_LW"), Btok, L + "Btok", -1.0)):
                pT, pTk = self.psum()
                for i in range(4):
                    in_ap = (xr[:, 8 + i, 1 + c * 64:1 + (c + 1) * 64] if src is None else src[:, i, c * 64:(c + 1) * 64])
                    P.op('pe', lambda e, i=i, pT=pT, in_ap=in_ap: e.transpose(out=pT[0:64, i * 128:(i + 1) * 128], in_=in_ap, identity=ident[:]),
                         reads=srck + [L + "ident"], writes=[pTk])
                P.op('act', lambda e, c=c, pT=pT, dst=dst, sc_=sc_: e.activation(out=dst[:, c, :], in_=pT[0:64, 0:512], func=AF.Identity, scale=sc_), reads=[pTk], writes=[f"{dstk}{c}"])
        Ml, Sl = [], []
        for c in range(NCH):
            q = c % 2
            cs = slice(c * 64, (c + 1) * 64)
            banks = {}
            self.defer_begin(pool=[0, 1, 2, 3, 4])
            for nm, A_, B_ in (("Lb", kkt, bt), ("LbT", bt, kkt), ("LkT", kt, kkt), ("RkT", kt, rt), ("RbT", bt, rt)):
                pm, pmk = self.psum()
                banks[nm] = (pm, pmk)
                for hh in range(8):
                    P.op('pe', lambda e, hh=hh, pm=pm, A_=A_, B_=B_, cs=cs: e.matmul(pm[0:64, hh * 64:(hh + 1) * 64], lhsT=OP(A_, LO[id(A_)], hh, cs), rhs=OP(B_, LO[id(B_)], hh, cs), start=True, stop=True),
                         reads=OPK, writes=[pmk])
            v3 = lambda pm: pm[0:64, 0:512].rearrange("p (h t) -> p h t", t=64)
            P.op('dve', lambda e, q=q, pm=banks["Lb"][0]: e.scalar_tensor_tensor(out=Nm_[q][:], in0=v3(pm), scalar=-1.0, in1=bc(SLm[:, None, :], [64, 8, 64]), op0=ALU.mult, op1=ALU.mult),
                 reads=[banks["Lb"][1], L + "cm"], writes=[L + f"N{q}"])
            P.op('dve', lambda e, q=q, pm=banks["LbT"][0]: e.scalar_tensor_tensor(out=NTm_[q][:], in0=v3(pm), scalar=-1.0, in1=bc(SUm[:, None, :], [64, 8, 64]), op0=ALU.mult, op1=ALU.mult),
                 reads=[banks["LbT"][1], L + "cm"], writes=[L + f"NT{q}"])
            P.op('dve', lambda e, q=q, pm=banks["LkT"][0]: e.tensor_tensor(out=LkT[q][:], in0=v3(pm), in1=bc(SUm[:, None, :], [64, 8, 64]), op=ALU.mult),
                 reads=[banks["LkT"][1], L + "cm"], writes=[L + f"LkT{q}"])
            P.op('dve', lambda e, q=q, pm=banks["RkT"][0]: e.tensor_tensor(out=RkT[q][:], in0=v3(pm), in1=bc(UTm[:, None, :], [64, 8, 64]), op=ALU.mult),
                 reads=[banks["RkT"][1], L + "cm"], writes=[L + f"RkT{q}"])
            P.op('dve', lambda e, q=q, pm=banks["RbT"][0]: e.scalar_tensor_tensor(out=RbT[q][:], in0=v3(pm), scalar=-1.0, in1=bc(UTm[:, None, :], [64, 8, 64]), op0=ALU.mult, op1=ALU.mult),
                 reads=[banks["RbT"][1], L + "cm"], writes=[L + f"RbT{q}"])
            self.tri_inv(Nm_[q], L + f"N{q}", NTm_[q], L + f"NT{q}", ATm[q], L + f"AT{q}", 8, TW)
            Ml.append(self.defer_end())
            self.defer_begin(pool=[5, 6, 7])
            p1, p1k = self.psum()
            for hh in range(8):
                P.op('pe', lambda e, hh=hh, p1=p1, cs=cs: e.matmul(p1[0:64, hh * 64:(hh + 1) * 64], lhsT=OP(kkt, kktL, hh, cs), rhs=Zbf[:, hh, :], start=True, stop=False),
                     reads=[L + "kkt", L + "kktL", L + "Zbf"], writes=[p1k])
                P.op('pe', lambda e, hh=hh, p1=p1, q=q, c=c: e.matmul(p1[0:64, hh * 64:(hh + 1) * 64], lhsT=LkT[q][:, hh, :], rhs=Vtok[:, c, hh * 64:(hh + 1) * 64], start=False, stop=True),
                     reads=[L + f"LkT{q}", f"{L}Vtok{c}"], writes=[p1k])
            P.op('act', lambda e, p1=p1: e.activation(out=X1b[:].rearrange("p h t -> p (h t)"), in_=p1[0:64, 0:512], func=AF.Identity), reads=[p1k], writes=[L + "X1b"])
            p2, p2k = self.psum()
            for hh in range(8):
                P.op('pe', lambda e, hh=hh, p2=p2, q=q: e.matmul(p2[0:64, hh * 64:(hh + 1) * 64], lhsT=ATm[q][:, hh, :], rhs=X1b[:, hh, :], start=True, stop=True),
                     reads=[L + f"AT{q}", L + "X1b"], writes=[p2k])
            P.op('act', lambda e, p2=p2: e.activation(out=Eb[:].rearrange("p h t -> p (h t)"), in_=p2[0:64, 0:512], func=AF.Identity), reads=[p2k], writes=[L + "Eb"])
            p3, p3k = self.psum()
            p4, p4k = self.psum()
            for hh in range(8):
                o3 = p3[0:64, hh * 64:(hh + 1) * 64]
                P.op('pe', lambda e, hh=hh, o3=o3, cs=cs: e.matmul(o3, lhsT=Zbf[:, hh, :], rhs=OP(rt, rtL, hh, cs), start=True, stop=False), reads=[L + "Zbf", L + "rt", L + "rtL"], writes=[p3k])
                P.op('pe', lambda e, hh=hh, o3=o3, q=q, c=c: e.matmul(o3, lhsT=Vtok[:, c, hh * 64:(hh + 1) * 64], rhs=RkT[q][:, hh, :], start=False, stop=False),
                     reads=[f"{L}Vtok{c}", L + f"RkT{q}"], writes=[p3k])
                P.op('pe', lambda e, hh=hh, o3=o3, q=q: e.matmul(o3, lhsT=Eb[:, hh, :], rhs=RbT[q][:, hh, :], start=False, stop=True), reads=[L + "Eb", L + f"RbT{q}"], writes=[p3k])
            for hh in range(8):
                o4 = p4[0:64, hh * 64:(hh + 1) * 64]
                P.op('pe', lambda e, hh=hh, o4=o4, c=c: e.matmul(o4, lhsT=Ktok[:, c, hh * 64:(hh + 1) * 64], rhs=Vtok[:, c, hh * 64:(hh + 1) * 64], start=True, stop=False),
                     reads=[f"{L}Ktok{c}", f"{L}Vtok{c}"], writes=[p4k])
                P.op('pe', lambda e, hh=hh, o4=o4, c=c: e.matmul(o4, lhsT=Btok[:, c, hh * 64:(hh + 1) * 64], rhs=Eb[:, hh, :], start=False, stop=True),
                     reads=[f"{L}Btok{c}", L + "Eb"], writes=[p4k])
            p3v = p3[0:64, 0:512].rearrange("p (i two t) -> p i two t", two=2, t=64)
            P.op('act', lambda e, p3v=p3v, cs=cs: e.activation(out=yT[0:64, :, cs], in_=p3v[:, :, 0, :], func=AF.Identity), reads=[p3k], writes=[L + "yT"])
            P.op('act', lambda e, p3v=p3v, cs=cs: e.activation(out=yodd[0:64, :, cs], in_=p3v[:, :, 1, :], func=AF.Identity), reads=[p3k], writes=a4("t_lw"))
            P.op('dve', lambda e, c=c: e.tensor_tensor(out=Z32[:], in0=Z32[:], in1=bc(wcall[:, c, :, None], [64, 8, 64]), op=ALU.mult), reads=[L + "Z32", L + "wcall"], writes=[L + "Z32"])
            P.op('dve', lambda e, p4=p4: e.tensor_tensor(out=Z32[:].rearrange("p h v -> p (h v)"), in0=Z32[:].rearrange("p h v -> p (h v)"), in1=p4[0:64, 0:512], op=ALU.add),
                 reads=[L + "Z32", p4k], writes=[L + "Z32"])
            P.op('act', lambda e: e.activation(out=Zbf[:], in_=Z32[:], func=AF.Identity), reads=[L + "Z32"], writes=[L + "Zbf"])
            Sl.append(self.defer_end())
        self.replay([Ml[0]])
        for c in range(NCH):
            self.replay([Sl[c]] + ([Ml[c + 1]] if c + 1 < NCH else []))
        P.op('sp', lambda e: e.dma_start(out=yT[64:128, :, :], in_=yodd[0:64, :, :]), reads=a4("t_lw") + [L + "yT"], writes=[L + "yT"], dma=True)
        P.op('act', lambda e, t0=t0: e.dma_start(out=gld[:], in_=gout_v[:, :, t0:t0 + TT]), writes=a4("t_LW"), dma=True)
        P.op('act', lambda e: e.activation(out=sq[:, 0:4, :], in_=yT[:], func=AF.Identity), reads=[L + "yT"],
             writes=[L + "sq"] + [f"{L}sqk{i}" for i in range(4)] + [f"{L}twb4_{i}" for i in range(4)] + [f"{L}sqc{i}" for i in range(4)])
        chains = []
        for i in range(4):
            self.defer_begin(pool=[2 * i, 2 * i + 1])
            pm, pmk = self.psum()
            P.op('pe', lambda e, i=i, pm=pm: e.matmul(pm[:, 0:TT], lhsT=bd64[:], rhs=sq[:, i, :], start=True, stop=True), reads=[L + "sq", L + "bd64"], writes=[pmk])
            P.op('dve', lambda e, i=i, pm=pm: e.tensor_tensor(out=yT[:, i, :], in0=yT[:, i, :], in1=pm[:, 0:TT], op=ALU.subtract), reads=[L + "yT", pmk], writes=[f"{L}yc{i}"])
            P.op('act', lambda e, i=i: e.activation(out=sq[:, 4 + i, :], in_=yT[:, i, :], func=AF.Square), reads=[f"{L}yc{i}"], writes=[f"{L}sqc{i}"])
            pv, pvk = self.psum()
            P.op('pe', lambda e, i=i, pv=pv: e.matmul(pv[:, 0:TT], lhsT=bd64[:], rhs=sq[:, 4 + i, :], start=True, stop=True), reads=[f"{L}sqc{i}", L + "bd64"], writes=[pvk])
            self.rsqrt_ps(rq4[:, i, :], f"{L}rq4_{i}", pv[:, 0:TT], pvk, 64e-5)
            P.op('dve', lambda e, i=i: e.tensor_tensor(out=yT[:, i, :], in0=yT[:, i, :], in1=rq4[:, i, :], op=ALU.mult), reads=[f"{L}yc{i}", f"{L}rq4_{i}"], writes=[f"{L}yc{i}"])
            P.op('act', lambda e, i=i: e.activation(out=yT[:, i, :], in_=yT[:, i, :], func=AF.Identity, scale=LNW[:, i:i + 1], bias=LNB[:, i:i + 1]), reads=[f"{L}yc{i}", L + "rv"], writes=[f"{L}yc{i}"])
            P.op('dve', lambda e, i=i: e.scalar_tensor_tensor(out=sq[:, i, :], in0=R_(i), scalar=RK_[:, i:i + 1], in1=t_kp[:, i, :], op0=ALU.mult, op1=ALU.mult),
                 reads=[rkeys[i], L + "rv", f"{L}t_kp{i}", f"{L}yc{i}"], writes=[f"{L}twb4_{i}"])
            pb, pbk = self.psum()
            P.op('pe', lambda e, i=i, pb=pb: e.matmul(pb[:, 0:TT], lhsT=bd1[:], rhs=sq[:, i, :], start=True, stop=True), reads=[f"{L}twb4_{i}", L + "bd1"], writes=[pbk])
            P.op('dve', lambda e, i=i, pb=pb: e.tensor_tensor(out=t_a[:, i, :], in0=pb[:, 0:TT], in1=V_(i), op=ALU.mult), reads=[pbk, vkeys[i]], writes=[f"{L}t_a{i}"])
            P.op('dve', lambda e, i=i: e.tensor_tensor(out=yT[:, i, :], in0=yT[:, i, :], in1=t_a[:, i, :], op=ALU.add), reads=[f"{L}yc{i}", f"{L}t_a{i}"], writes=[f"{L}yc{i}"])
            if rdbg_v is not None:
                P.op('dve', lambda e, i=i: e.tensor_tensor(out=t_a[:, i, :], in0=yT[:, i, :], in1=t_g[:, i, :], op=ALU.mult), reads=[f"{L}yc{i}", f"{L}t_g{i}", f"{L}t_a{i}"], writes=[f"{L}t_a{i}"])
            P.op('dve', lambda e, i=i: e.tensor_tensor(out=ymix[:, 4 + i, :], in0=yT[:, i, :], in1=t_g[:, i, :], op=ALU.mult), reads=[f"{L}yc{i}", f"{L}t_g{i}"], writes=[f"{L}h{4 + i}"])
            P.op('act', lambda e, i=i: e.activation(out=ymix[:, i, :], in_=gld[:, i, :], func=AF.Identity), reads=a4("t_LW"), writes=[f"{L}h{i}"])
            chains.append(self.defer_end())
        self.replay(chains)
        if rdbg_v is not None:
            P.op('sp', lambda e, t0=t0: e.dma_start(out=rdbg_v[:, :, t0:t0 + TT], in_=t_a[:]), reads=a4("t_a"), writes=[self.key("rdbg")], dma=True)
        yk = [f"{L}tmp{k}" for k in range(KC)]
        for d in range(KC):
            pd, pdk = self.psum()
            for k in range(KC):
                P.op('pe', lambda e, k=k, d=d, pd=pd: e.matmul(pd[:, 0:TT], lhsT=wout[:, k, d * 128:(d + 1) * 128], rhs=ymix[:, k, :], start=(k == 0), stop=(k == KC - 1)),
                     reads=[f"{L}wout{k}", f"{L}h{k}"], writes=[pdk])
            P.op('act', lambda e, d=d, pd=pd: e.activation(out=tmp[:, d, :], in_=pd[:, 0:TT], func=AF.Identity), reads=[pdk], writes=[yk[d]])
        self.postnorm_residual(tmp, yk, xt, xk, TT, s, sq, L + "sq", rstd, L + "rstd", tmp, L + "tmp")
        P.op('sp', lambda e, t0=t0: e.dma_start(out=xout_v[:, :, t0:t0 + TT], in_=tmp[:]), reads=yk, writes=[self.key('xout')], dma=True)
    self.end()


Builder.rwkv_section = _rwkv_section


def declare_rest(b, I):
    I['lru_win'] = b.din("lru_win", [128, 8, 2048]); I['lru_wa'] = b.din("lru_wa", [128, 8, 256]); I['lru_wx'] = b.din("lru_wx", [128, 8, 256])
    I['lru_wout'] = b.din("lru_wout", [128, 8, 1024]); I['lru_vec'] = b.din("lru_vec", [128, 8, 8])
    for l in range(2):
        I[f'wg{l}'] = b.din(f"wg{l}", [128, 8, 2816]); I[f'wu{l}'] = b.din(f"wu{l}", [128, 8, 2816]); I[f'wd{l}'] = b.din(f"wd{l}", [128, 22, 1024])


def rest_maps(inp):
    m = {}
    m["lru_win"] = fm_mat(inp['lru_w_in'][0])
    m["lru_wa"] = np.ascontiguousarray(np.asarray(inp['lru_wa'][0], np.float32).reshape(4, 2, 128, 256).transpose(2, 0, 1, 3).reshape(128, 8, 256))
    m["lru_wx"] = np.ascontiguousarray(np.asarray(inp['lru_wx'][0], np.float32).reshape(4, 2, 128, 256).transpose(2, 0, 1, 3).reshape(128, 8, 256))
    m["lru_wout"] = fm_mat(inp['lru_w_out'][0])
    cw = np.asarray(inp['lru_conv_w'][0], np.float32)
    m["lru_vec"] = np.ascontiguousarray(np.stack([fm_vec(cw[0]), fm_vec(cw[1]), fm_vec(cw[2]), fm_vec(cw[3]), fm_vec(inp['lru_conv_b'][0]),
                                                  fm_vec(inp['lru_ba'][0]), fm_vec(inp['lru_bx'][0]), fm_vec(inp['lru_lambda'][0])], axis=1))
    for l in range(2):
        m[f"wg{l}"] = fm_mat(inp['ffn_w_gate'][l]); m[f"wu{l}"] = fm_mat(inp['ffn_w_up'][l]); m[f"wd{l}"] = fm_mat(inp['ffn_w_down'][l])
    return m


_CACHE = {}


def build_full(NT=SEQ):
    if NT in _CACHE:
        return _CACHE[NT]
    b = Builder(nt=NT)
    I = declare_inputs(b, NT)
    declare_rest(b, I)
    outT = b.dout("outT", [1024, NT])
    gout = b.dscratch("s_gout", [512, NT])
    xa = b.dscratch("s_xa", [1024, NT])
    xb = b.dscratch("s_xb", [1024, NT])
    xc = b.dscratch("s_xc", [1024, NT])
    b.prologue(I['cT'], I['ada_w'], I['ada_b'], I['npre'], I['npost'])
    b.gdn_section(I['xT'], gout, I['wgd'], I['g_cvec'], I['g_tok'], I['g_nw'], I['cm'], I['timk'])
    b.rwkv_section(I['xT'], gout, xa, I, None)
    b.ffn_sublayer(0, xa, xb, I['wg0'], I['wu0'], I['wd0'])
    b.lru_sublayer(xb, xc, I['lru_win'], I['lru_wa'], I['lru_wx'], I['lru_wout'], I['lru_vec'])
    b.ffn_sublayer(1, xc, outT, I['wg1'], I['wu1'], I['wd1'])
    _CACHE[NT] = b
    return b


def kernel(**inputs):
    inp = {k: np.asarray(v) for k, v in inputs.items()}
    B, T, _ = inp['x'].shape
    b = build_full(T)
    cm = common_maps(inp)
    cm.update(rest_maps(inp))
    n_cores = 4
    maps = []
    for i in range(n_cores):
        m = dict(cm)
        m.update(core_maps(inp, i % B, T))
        maps.append(m)
    res = run_bass_kernel_spmd(b.nc, maps, core_ids=list(range(n_cores)))
    out = np.stack([np.ascontiguousarray(res.results[i]["outT"].T) for i in range(B)], axis=0)
    return out.astype(np.float32)
```

```python
import contextlib
import math
import numpy as np
import concourse.bass as bass
import concourse.mybir as mybir
from concourse.bass_utils import run_bass_kernel_spmd

F32 = mybir.dt.float32
BF16 = mybir.dt.bfloat16
AF = mybir.ActivationFunctionType
ALU = mybir.AluOpType
AX = mybir.AxisListType

SEM_CHUNK = 20000
ATTACH_WAITS = True
N_DMA_SEMS = 40
DMA_CAST = True

D = 1024
KC = 8
SEQ = 4096
DFF = 2816
FC = 22
EPS = 1e-6


class Prog:
    def __init__(self, nc):
        self.nc = nc
        self.ins = []
        self.last_w = {}
        self.readers = {}
        self.stack = contextlib.ExitStack()
        self._n = 0

    def sb(self, shape, dtype, name=None):
        self._n += 1
        return self.stack.enter_context(self.nc.sbuf_tensor(name or f"sb{self._n}", list(shape), dtype))

    def ps(self, shape, dtype=F32, name=None):
        self._n += 1
        return self.stack.enter_context(self.nc.psum_tensor(name or f"ps{self._n}", list(shape), dtype))

    def op(self, eng, fn, reads=(), writes=(), dma=False):
        i = len(self.ins)
        deps = set()
        for k in reads:
            w = self.last_w.get(k)
            if w is not None:
                deps.add(w)
        for k in writes:
            w = self.last_w.get(k)
            if w is not None:
                deps.add(w)
            for r in self.readers.get(k, ()):
                deps.add(r)
        best = {}
        keep = set()
        for d in deps:
            Dd = self.ins[d]
            if Dd['dma'] or Dd['eng'] in ('pool', 'sp'):
                keep.add(d)
            elif Dd['eng'] not in best or d > best[Dd['eng']]:
                best[Dd['eng']] = d
        deps = keep | set(best.values())
        self.ins.append(dict(eng=eng, fn=fn, deps=deps, dma=dma))
        for k in reads:
            self.readers.setdefault(k, []).append(i)
        for k in writes:
            self.last_w[k] = i
            self.readers[k] = []
        return i

    def emit(self, S):
        nc = self.nc
        ins = self.ins
        engs = ['pe', 'act', 'dve', 'pool', 'sp']
        need = [False] * len(ins)
        for i, I in enumerate(ins):
            for d in I['deps']:
                Dd = ins[d]
                if Dd['dma'] or Dd['eng'] != I['eng'] or I['eng'] != 'pe' or I['dma']:
                    need[d] = True
        ticket = [None] * len(ins)
        streams = {e: [] for e in engs}
        for i, I in enumerate(ins):
            e = I['eng']
            st = streams[e]
            for d in sorted(I['deps']):
                t = ticket[d]
                if t is None:
                    continue
                if t[0] == 'c':
                    _, oe, g = t
                    if S.seen_c[e][oe] >= g:
                        continue
                    S.seen_c[e][oe] = g
                    st.append(('wait', S.csem(oe, g // SEM_CHUNK), g % SEM_CHUNK + 1))
                else:
                    _, k, v = t
                    if S.seen_d[e][k] >= v:
                        continue
                    S.seen_d[e][k] = v
                    st.append(('wait', S.dsems[k], v))
            if I['dma']:
                k = S.ndma % N_DMA_SEMS
                S.ndma += 1
                if S.dcount[k] > 0 and S.seen_d[e][k] < S.dcount[k]:
                    st.append(('wait', S.dsems[k], S.dcount[k]))
                    S.seen_d[e][k] = S.dcount[k]
                S.dcount[k] += 16
                ticket[i] = ('d', k, S.dcount[k])
                st.append(('ins', I['fn'], S.dsems[k], 16))
            elif need[i]:
                g = S.cnt[e]
                S.cnt[e] += 1
                ticket[i] = ('c', e, g)
                st.append(('ins', I['fn'], S.csem(e, g // SEM_CHUNK), 1))
            else:
                st.append(('ins', I['fn'], None, 0))
        for k in range(N_DMA_SEMS):
            if S.dcount[k] > S.seen_d['sp'][k]:
                streams['sp'].append(('wait', S.dsems[k], S.dcount[k]))
                S.seen_d['sp'][k] = S.dcount[k]
        self.n_ins = {e: len(streams[e]) for e in engs}

        def run(engobj, st, fuse=False):
            pend = None
            for it in st:
                if it[0] == 'wait':
                    if pend is not None:
                        engobj.wait_ge(pend[0], pend[1])
                    pend = (it[1], it[2])
                    if not (fuse and ATTACH_WAITS):
                        engobj.wait_ge(pend[0], pend[1])
                        pend = None
                else:
                    if pend is not None and it[3] == 16:
                        engobj.wait_ge(pend[0], pend[1])
                        pend = None
                    r = it[1](engobj)
                    if pend is not None:
                        r._wait_ge(pend[0], pend[1])
                        pend = None
                    if it[2] is not None:
                        r.then_inc(it[2], it[3])
            if pend is not None:
                engobj.wait_ge(pend[0], pend[1])

        with nc.Block() as block:
            @block.tensor
            def _(e):
                run(e, streams['pe'])

            @block.scalar
            def _(e):
                run(e, streams['act'], fuse=True)

            @block.vector
            def _(e):
                run(e, streams['dve'], fuse=True)

            @block.gpsimd
            def _(e):
                run(e, streams['pool'])

            @block.sync
            def _(e):
                run(e, streams['sp'])

    def close(self):
        self.stack.close()


class SemState:
    def __init__(self, nc, stack):
        engs = ['pe', 'act', 'dve', 'pool', 'sp']
        self.nc = nc
        self.stack = stack
        self.cs = {e: [] for e in engs}
        self.cnt = {e: 0 for e in engs}
        self.dsems = [stack.enter_context(nc.semaphore(f"d_{k}")) for k in range(N_DMA_SEMS)]
        self.dcount = [0] * N_DMA_SEMS
        self.ndma = 0
        self.seen_c = {e: {o: -1 for o in engs} for e in engs}
        self.seen_d = {e: [0] * N_DMA_SEMS for e in engs}
        for e in engs:
            self.csem(e, 0)

    def csem(self, e, gen):
        while len(self.cs[e]) <= gen:
            self.cs[e].append(self.stack.enter_context(self.nc.semaphore(f"c_{e}_{len(self.cs[e])}")))
        return self.cs[e][gen]


def fm_vec(v):
    v = np.asarray(v, np.float32)
    n = v.shape[-1] // 128
    return np.ascontiguousarray(v.reshape(n, 128).T)


def fm_mat(w):
    w = np.asarray(w, np.float32)
    K, N = w.shape
    return np.ascontiguousarray(w.reshape(K // 128, 128, N).transpose(1, 0, 2))


class Builder:
    def __init__(self, nt=SEQ):
        self.NT = nt
        self.nc = bass.Bass("TRN2", target_bir_lowering=False)
        self.G = contextlib.ExitStack()
        self.S = SemState(self.nc, self.G)
        self.P = None
        self.dram = {}
        self.uid = 0
        self.ninstr = {}

    def din(self, name, shape, dtype=F32):
        t = self.nc.dram_tensor(name, list(shape), dtype, kind="ExternalInput").ap()
        self.dram[name] = t
        return t

    def dout(self, name, shape, dtype=F32):
        t = self.nc.dram_tensor(name, list(shape), dtype, kind="ExternalOutput").ap()
        self.dram[name] = t
        return t

    def dscratch(self, name, shape, dtype=F32):
        t = self.nc.dram_tensor(name, list(shape), dtype, kind="Internal").ap()
        self.dram[name] = t
        return t

    def key(self, s):
        self.uid += 1
        return f"{s}#{self.uid}"

    def global_extra(self):
        pass

    def gsb(self, shape, dtype, name):
        return self.G.enter_context(self.nc.sbuf_tensor(name, list(shape), dtype))

    def begin(self, tag):
        self.P = Prog(self.nc)
        self.tag = tag
        self.psb = [self.P.ps([128, 512], F32, f"{tag}_psb{i}") for i in range(8)]
        self.ps_rr = 0
        return self.P

    def end(self):
        self.P.emit(self.S)
        self.ninstr[self.tag] = dict(self.P.n_ins)
        self.P.close()
        self.P = None

    def defer_begin(self, pool=None):
        self.ps_pool = pool
        if not hasattr(self, 'ps_prr'):
            self.ps_prr = {}
        self._buf = []
        self.P.op = lambda *a, **k: self._buf.append((a, k))

    def defer_end(self):
        del self.P.op
        self.ps_pool = None
        b = self._buf
        self._buf = None
        return b

    def replay(self, lists):
        lists = [l for l in lists if l]
        idx = [0] * len(lists)
        while True:
            best, bf = None, None
            for i, l in enumerate(lists):
                if idx[i] < len(l):
                    f = idx[i] / len(l)
                    if bf is None or f < bf:
                        best, bf = i, f
            if best is None:
                break
            a, k = lists[best][idx[best]]
            idx[best] += 1
            self.P.op(*a, **k)

    def psum(self):
        pool = getattr(self, 'ps_pool', None)
        if pool is not None:
            self.ps_prr[pool[0]] = self.ps_prr.get(pool[0], 0) + 1
            i = pool[(self.ps_prr[pool[0]] - 1) % len(pool)]
            return self.psb[i], f"psb{i}"
        i = self.ps_rr % 8
        self.ps_rr += 1
        return self.psb[i], f"psb{i}"

    def prologue(self, cT, ada_w, ada_b, npre, npost):
        self.ones_bf = self.gsb([128, 128], BF16, "ones_bf")
        self.ones128_bf = self.gsb([128, 128], BF16, "ones128_bf")
        self.mhalf = self.gsb([128, 1], F32, "mhalf")
        self.phalf = self.gsb([128, 1], F32, "phalf")
        self.epsc = {EPS: self.gsb([128, 1], F32, "eps_a"), 64e-5: self.gsb([128, 1], F32, "eps_b")}
        mres = self.gsb([128, 4, 24], F32, "ada_m")
        s1s = [self.gsb([128, 8], F32, f"ada_s1_{s}") for s in range(4)]
        g1s = [self.gsb([128, 8], F32, f"ada_g1_{s}") for s in range(4)]
        self.global_extra()
        P = self.begin("pro")
        P.op('pool', lambda e: e.memset(self.ones_bf[:], 1.0 / 1024.0), writes=['ones_bf'])
        P.op('pool', lambda e: e.memset(self.ones128_bf[:], 1.0 / 128.0), writes=['ones128_bf'])
        P.op('pool', lambda e: e.memset(self.mhalf[:], -0.5), writes=['mhalf'])
        P.op('pool', lambda e: e.memset(self.phalf[:], 0.5), writes=['phalf'])
        P.op('pool', lambda e: e.memset(self.epsc[EPS][:], EPS), writes=['eps_a'])
        P.op('pool', lambda e: e.memset(self.epsc[64e-5][:], 64e-5), writes=['eps_b'])
        sc = P.sb([128, 8], F32, "ada_sc")
        craw = P.sb([128, 8], F32, "ada_craw")
        P.op('sp', lambda e: e.dma_start(out=craw[:], in_=cT), writes=['craw'], dma=True)
        P.op('act', lambda e: e.activation(out=sc[:], in_=craw[:], func=AF.Tanh, scale=0.5), reads=['craw'], writes=['sc'])
        P.op('dve', lambda e: e.scalar_tensor_tensor(out=sc[:], in0=sc[:], scalar=1.0, in1=craw[:], op0=ALU.add, op1=ALU.mult),
             reads=['sc', 'craw'], writes=['sc'])
        P.op('dve', lambda e: e.tensor_scalar(out=sc[:], in0=sc[:], scalar1=0.5, scalar2=None, op0=ALU.mult), reads=['sc'], writes=['sc'])
        wbuf = [P.sb([128, 3072], BF16, f"ada_wbuf{i}") for i in range(12)]
        sc_bf = P.sb([128, 8], BF16, "ada_sc_bf")
        P.op('dve', lambda e: e.tensor_copy(out=sc_bf[:], in_=sc[:]), reads=['sc'], writes=['sc_bf'])
        bias = P.sb([128, 4, 24], F32, "ada_bias")
        pre = P.sb([128, 4, 8], F32, "ada_pre")
        post = P.sb([128, 4, 8], F32, "ada_post")
        P.op('sp', lambda e: e.dma_start(out=bias[:], in_=ada_b.rearrange("s p n -> p s n")), writes=['ada_bias'], dma=True)
        P.op('sp', lambda e: e.dma_start(out=pre[:], in_=npre.rearrange("s p n -> p s n")), writes=['ada_pre'], dma=True)
        P.op('sp', lambda e: e.dma_start(out=post[:], in_=npost.rearrange("s p n -> p s n")), writes=['ada_post'], dma=True)
        self.mod = {}
        it = 0
        for s in range(4):
            ps, psk = self.psum()
            for k in range(8):
                wb = wbuf[it % 12]
                wk = f"ada_wbuf{it % 12}"
                it += 1
                P.op('pool', lambda e, wb=wb, s=s, k=k: e.dma_start(out=wb[:], in_=ada_w[s, k]),
                     writes=[wk], dma=True)
                for n in range(24):
                    P.op('pe', lambda e, wb=wb, n=n, k=k, ps=ps: e.matmul(
                        ps[:, n * 8 + k:n * 8 + k + 1], lhsT=wb[:, n * 128:(n + 1) * 128], rhs=sc_bf[:, k:k + 1],
                        start=True, stop=True), reads=[wk, 'sc_bf'], writes=[psk])
            P.op('dve', lambda e, ps=ps, s=s: e.tensor_reduce(
                out=mres[:, s, :], in_=ps[:, 0:192].rearrange("p (n k) -> p n k", k=8), axis=AX.X, op=ALU.add),
                reads=[psk], writes=[f'ada_m{s}'])
            P.op('dve', lambda e, s=s: e.tensor_tensor(out=mres[:, s, :], in0=mres[:, s, :], in1=bias[:, s, :], op=ALU.add),
                 reads=[f'ada_m{s}', 'ada_bias'], writes=[f'ada_m{s}'])
            s1 = s1s[s]
            g1 = g1s[s]
            P.op('dve', lambda e, s=s, s1=s1: e.scalar_tensor_tensor(
                out=s1[:], in0=mres[:, s, 8:16], scalar=1.0, in1=pre[:, s, :], op0=ALU.add, op1=ALU.mult),
                reads=[f'ada_m{s}', 'ada_pre'], writes=[f'ada_s1_{s}'])
            P.op('dve', lambda e, s=s, g1=g1: e.tensor_tensor(out=g1[:], in0=mres[:, s, 16:24], in1=post[:, s, :], op=ALU.mult),
                 reads=[f'ada_m{s}', 'ada_post'], writes=[f'ada_g1_{s}'])
            self.mod[s] = dict(s1=s1, shift=mres, sidx=s, g1=g1)
        self.end()

    def rsqrt_ps(self, out_ap, outk, ps_ap, psk, eps):
        P = self.P
        P.op('act', lambda e: e.activation(out=out_ap, in_=ps_ap, func=AF.Ln, bias=self.epsc[eps][0:out_ap.shape[0], 0:1]), reads=[psk], writes=[outk])
        P.op('act', lambda e: e.activation(out=out_ap, in_=out_ap, func=AF.Exp, scale=-0.5), reads=[outk], writes=[outk])

    def rms_rstd(self, src, srckeys, TT, sq, sqk, rstd, rstdk, nchunk=KC, ones=None, eps=EPS):
        P = self.P
        ones = ones if ones is not None else self.ones_bf
        sqk = sqk if isinstance(sqk, list) else [sqk]
        P.op('act', lambda e: e.activation(out=sq[:, 0:nchunk, 0:TT], in_=src, func=AF.Square), reads=srckeys, writes=sqk)
        ps, psk = self.psum()
        for k in range(nchunk):
            P.op('pe', lambda e, k=k: e.matmul(ps[:, 0:TT], lhsT=ones[:], rhs=sq[:, k, 0:TT], start=(k == 0), stop=(k == nchunk - 1)),
                 reads=sqk, writes=[psk])
        self.rsqrt_ps(rstd[:, 0:TT], rstdk, ps[:, 0:TT], psk, eps)

    def prenorm(self, xt, xk, TT, s, sq, sqk, rstd, rstdk, tmp, tmpk, h, hk):
        P = self.P
        m = self.mod[s]
        self.rms_rstd(xt[:, :, 0:TT], [xk], TT, sq, sqk, rstd, rstdk)
        for k in range(KC):
            P.op('dve', lambda e, k=k: e.tensor_tensor(out=tmp[:, k, 0:TT], in0=xt[:, k, 0:TT], in1=rstd[:, 0:TT], op=ALU.mult),
                 reads=[xk, rstdk], writes=[f"{tmpk}{k}"])
            P.op('act', lambda e, k=k: e.activation(out=h[:, k, 0:TT], in_=tmp[:, k, 0:TT], func=AF.Identity,
                                                     scale=m['s1'][:, k:k + 1], bias=m['shift'][:, m['sidx'], k:k + 1]),
                 reads=[f"{tmpk}{k}"], writes=[f"{hk}{k}"])

    def postnorm_residual(self, y, yk_list, xt, xk, TT, s, sq, sqk, rstd, rstdk, xo, xok):
        P = self.P
        m = self.mod[s]
        self.rms_rstd(y[:, :, 0:TT], yk_list, TT, sq, sqk, rstd, rstdk)
        for k in range(KC):
            P.op('dve', lambda e, k=k: e.tensor_tensor(out=y[:, k, 0:TT], in0=y[:, k, 0:TT], in1=rstd[:, 0:TT], op=ALU.mult),
                 reads=[yk_list[k], rstdk], writes=[yk_list[k]])
            P.op('dve', lambda e, k=k: e.scalar_tensor_tensor(out=xo[:, k, 0:TT], in0=y[:, k, 0:TT], scalar=m['g1'][:, k:k + 1],
                                                               in1=xt[:, k, 0:TT], op0=ALU.mult, op1=ALU.add),
                 reads=[yk_list[k], xk], writes=[f"{xok}{k}"])

    def _stg(self):
        P = self.P
        if not hasattr(P, 'stg'):
            P.stg = [P.sb([128, getattr(self, 'stg_cols', 1408)], F32, f"{self.tag}_stg{i}") for i in range(2)]
            P.stg_i = 0
        i = P.stg_i % 2
        P.stg_i += 1
        return i, P.stg[i]

    def load_piece(self, dst, src, k, c0, w, key, first):
        P = self.P
        if DMA_CAST:
            P.op('pool', lambda e: e.dma_start(out=dst[:, k, c0:c0 + w], in_=src[:, k, c0:c0 + w]), reads=([] if first else [key]), writes=[key], dma=True)
            return
        i, sg = self._stg()
        P.op('sp', lambda e: e.dma_start(out=sg[:, 0:w], in_=src[:, k, c0:c0 + w]), writes=[f"stg{i}"], dma=True)
        P.op('pool', lambda e: e.tensor_copy(out=dst[:, k, c0:c0 + w], in_=sg[:, 0:w]), reads=[f"stg{i}"] + ([] if first else [key]), writes=[key])

    def load_w_bf16(self, dst, dstk, src, kchunks, ncols, piece=1408):
        for k in range(kchunks):
            c0 = 0
            first = True
            while c0 < ncols:
                w = min(piece, ncols - c0)
                self.load_piece(dst, src, k, c0, w, f"{dstk}{k}", first)
                first = False
                c0 += w

    def ffn_sublayer(self, layer, x_in, x_out, wg_d, wu_d, wd_d):
        P = self.begin(f"ffn{layer}")
        s = layer * 2 + 1
        TT = 256
        L = f"f{layer}_"
        wg = P.sb([128, KC, DFF], BF16, L + "wg")
        wu = P.sb([128, KC, DFF], BF16, L + "wu")
        wd = P.sb([128, FC, D], BF16, L + "wd")
        for pc in range(2):
            for k in range(KC):
                self.load_piece(wg, wg_d, k, pc * 1408, 1408, f"{L}wg{k}p{pc}", True)
                self.load_piece(wu, wu_d, k, pc * 1408, 1408, f"{L}wu{k}p{pc}", True)
        self.load_w_bf16(wd, L + "wd", wd_d, FC, D)
        wdk = [f"{L}wd{k}" for k in range(FC)]
        xt = [P.sb([128, KC, TT], F32, L + f"xt{i}") for i in range(2)]
        hL = [P.sb([128, KC, TT], BF16, L + f"h_{i}") for i in range(2)]
        tmp = P.sb([128, KC, TT], F32, L + "tmp")
        ftmp = P.sb([128, KC, TT], F32, L + "ftmp")
        fsq = P.sb([128, KC, TT], BF16, L + "fsq")
        frstd = P.sb([128, TT], F32, L + "frstd")
        rstd = P.sb([128, TT], F32, L + "rstd")
        act = P.sb([128, FC, TT], BF16, L + "act")
        sq = act
        SQK = [f"{L}act{f}" for f in range(KC)]
        sg = [P.sb([128, TT], F32, L + f"sg{i}") for i in range(2)]
        y = tmp
        xo = tmp
        xin_v = x_in.rearrange("(k p) t -> p k t", p=128)
        xout_v = x_out.rearrange("(k p) t -> p k t", p=128)
        ntile = self.NT // TT
        Fl, Bl = [], []

        def do_tile(j):
            t0 = j * TT
            pj = j % 2
            h = hL[pj]
            xb = xt[pj]
            xk = L + f"xt{pj}"
            self.defer_begin(pool=[0])
            P.op('act', lambda e, xb=xb, t0=t0: e.dma_start(out=xb[:], in_=xin_v[:, :, t0:t0 + TT]), writes=[xk], dma=True)
            self.prenorm(xb, xk, TT, s, fsq, L + "fsq", frstd, L + "frstd", ftmp, L + "ftmp", h, L + f"h{pj}_")
            hk = [f"{L}h{pj}_{k}" for k in range(KC)]
            Fl.append(self.defer_end())
            self.defer_begin(pool=[1, 2, 3, 4, 5, 6, 7])
            for f in range(FC):
                pg, pgk = self.psum()
                pu, puk = self.psum()
                for k in range(KC):
                    P.op('pe', lambda e, f=f, k=k, pg=pg: e.matmul(pg[:, 0:TT], lhsT=wg[:, k, f * 128:(f + 1) * 128], rhs=h[:, k, :],
                                                                    start=(k == 0), stop=(k == KC - 1)),
                         reads=[f"{L}wg{k}p{0 if f < 11 else 1}", hk[k]], writes=[pgk])
                for k in range(KC):
                    P.op('pe', lambda e, f=f, k=k, pu=pu: e.matmul(pu[:, 0:TT], lhsT=wu[:, k, f * 128:(f + 1) * 128], rhs=h[:, k, :],
                                                                    start=(k == 0), stop=(k == KC - 1)),
                         reads=[f"{L}wu{k}p{0 if f < 11 else 1}", hk[k]], writes=[puk])
                sgb = sg[f % 2]
                sgk = L + f"sg{f % 2}"
                P.op('act', lambda e, sgb=sgb, pg=pg: e.activation(out=sgb[:], in_=pg[:, 0:TT], func=AF.Tanh, scale=0.5), reads=[pgk], writes=[sgk])
                P.op('dve', lambda e, sgb=sgb, pg=pg: e.scalar_tensor_tensor(out=sgb[:], in0=sgb[:], scalar=1.0, in1=pg[:, 0:TT], op0=ALU.add, op1=ALU.mult),
                     reads=[sgk, pgk], writes=[sgk])
                P.op('dve', lambda e, sgb=sgb, pu=pu, f=f: e.scalar_tensor_tensor(out=act[:, f, :], in0=sgb[:], scalar=0.5, in1=pu[:, 0:TT], op0=ALU.mult, op1=ALU.mult),
                     reads=[sgk, puk], writes=[f"{L}act{f}"])
            yk = [f"{L}tmp{k}" for k in range(KC)]
            for d in range(KC):
                pd, pdk = self.psum()
                for f in range(FC):
                    P.op('pe', lambda e, f=f, d=d, pd=pd: e.matmul(pd[:, 0:TT], lhsT=wd[:, f, d * 128:(d + 1) * 128], rhs=act[:, f, :],
                                                                    start=(f == 0), stop=(f == FC - 1)),
                         reads=[wdk[f], f"{L}act{f}"], writes=[pdk])
                P.op('act', lambda e, d=d, pd=pd: e.activation(out=y[:, d, :], in_=pd[:, 0:TT], func=AF.Identity), reads=[pdk], writes=[yk[d]])
            self.postnorm_residual(y, yk, xb, xk, TT, s, sq, SQK, rstd, L + "rstd", xo, L + "tmp")
            P.op('sp', lambda e, t0=t0: e.dma_start(out=xout_v[:, :, t0:t0 + TT], in_=xo[:]),
                 reads=[f"{L}tmp{k}" for k in range(KC)], writes=[self.key('xout')], dma=True)
            Bl.append(self.defer_end())
        for j in range(ntile):
            do_tile(j)
        self.replay([Fl[0]])
        for j in range(ntile):
            self.replay([Bl[j]] + ([Fl[j + 1]] if j + 1 < ntile else []))
        self.end()

    def lru_sublayer(self, x_in, x_out, win_d, wa_d, wx_d, wout_d, vec_d):
        P = self.begin("lru")
        s = 2
        TT = 256
        L = "m1_"
        win = P.sb([128, KC, 2048], BF16, L + "win")
        wa = P.sb([128, 8, 256], BF16, L + "wa")
        wx = P.sb([128, 8, 256], BF16, L + "wx")
        wout = P.sb([128, KC, D], BF16, L + "wout")
        vec = P.sb([128, 8, 8], F32, L + "vec")
        P.op('sp', lambda e: e.dma_start(out=vec[:], in_=vec_d), writes=[L + "vec"], dma=True)
        self.load_w_bf16(win, L + "win", win_d, KC, 2048, piece=1024)
        self.load_w_bf16(wa, L + "wa", wa_d, 8, 256)
        self.load_w_bf16(wx, L + "wx", wx_d, 8, 256)
        self.load_w_bf16(wout, L + "wout", wout_d, KC, D)
        c8 = P.sb([128, 8], F32, L + "c8")
        c4 = P.sb([128, 8], F32, L + "c4")
        hb = P.sb([128, 2, 8], F32, L + "hb")
        P.op('act', lambda e: e.activation(out=c8[:], in_=vec[:, 7, :], func=AF.Exp, scale=-1.0), reads=[L + "vec"], writes=[L + "c8"])
        P.op('act', lambda e: e.activation(out=c8[:], in_=c8[:], func=AF.Ln, bias=1.0), reads=[L + "c8"], writes=[L + "c8"])
        P.op('dve', lambda e: e.tensor_scalar(out=c4[:], in0=c8[:], scalar1=-4.0, scalar2=None, op0=ALU.mult), reads=[L + "c8"], writes=[L + "c4"])
        P.op('dve', lambda e: e.tensor_scalar(out=c8[:], in0=c8[:], scalar1=-8.0, scalar2=None, op0=ALU.mult), reads=[L + "c8", L + "c4"], writes=[L + "c8"])
        P.op('dve', lambda e: e.tensor_scalar(out=hb[:], in0=vec[:, 5:7, :], scalar1=0.5, scalar2=None, op0=ALU.mult), reads=[L + "vec"], writes=[L + "hb"])
        xt = [P.sb([128, KC, TT], F32, L + f"xt{i}") for i in range(2)]
        h = P.sb([128, KC, TT], BF16, L + "h")
        tmp = P.sb([128, KC, TT], F32, L + "tmp")
        sq = P.sb([128, KC, TT], BF16, L + "sq")
        rstd = P.sb([128, TT], F32, L + "rstd")
        ggL = [P.sb([128, KC, TT], F32, L + f"gg_{i}") for i in range(2)]
        xb = P.sb([128, KC, TT + 3], F32, L + "xb")
        cvL = [P.sb([128, KC, TT], F32, L + f"cv_{i}") for i in range(2)]
        cvbL = [P.sb([128, KC, TT], BF16, L + f"cvb_{i}") for i in range(2)]
        f1 = [P.sb([128, TT], F32, L + f"f1_{i}") for i in range(2)]
        f2 = [P.sb([128, TT], F32, L + f"f2_{i}") for i in range(2)]
        ybuf = P.sb([128, KC, TT], F32, L + "ybuf")
        aA = P.sb([128, KC, TT], F32, L + "aA")
        mA = P.sb([128, KC, TT], F32, L + "mA")
        qA = P.sb([128, KC, TT], F32, L + "qA")
        sq2 = P.sb([128, KC, TT], BF16, L + "sq2")
        rstd2 = P.sb([128, TT], F32, L + "rstd2")
        t1 = [P.sb([128, TT], F32, L + f"t1_{i}") for i in range(2)]
        t2 = [P.sb([128, TT], F32, L + f"t2_{i}") for i in range(2)]
        t3 = [P.sb([128, TT], F32, L + f"t3_{i}") for i in range(2)]
        t4 = [P.sb([128, TT], F32, L + f"t4_{i}") for i in range(2)]
        hs = [P.sb([128, TT], F32, L + f"hs_{i}") for i in range(2)]
        st = P.sb([128, KC], F32, L + "state")
        yb = P.sb([128, KC, TT], BF16, L + "yb")
        y = ybuf
        P.op('pool', lambda e: e.memset(st[:], 0.0), writes=[L + f"state{k}" for k in range(KC)])
        P.op('pool', lambda e: e.memset(xb[:, :, 0:3], 0.0), writes=[L + f"xbh{k}" for k in range(KC)])
        xin_v = x_in.rearrange("(k p) t -> p k t", p=128)
        xout_v = x_out.rearrange("(k p) t -> p k t", p=128)
        C1 = 0.044715
        C2 = math.sqrt(2.0 / math.pi)
        Fl, Bl = [], []

        def do_tile(j):
            t0 = j * TT
            pj = j % 2
            gg, cv, cvb = ggL[pj], cvL[pj], cvbL[pj]
            xo = gg
            LP = L + f"p{pj}_"
            self.defer_begin(pool=[0, 1, 2])
            xtb = xt[j % 2]
            xk = L + f"xt{j % 2}"
            P.op('act', lambda e, xtb=xtb, t0=t0: e.dma_start(out=xtb[:], in_=xin_v[:, :, t0:t0 + TT]), writes=[xk], dma=True)
            self.prenorm(xtb, xk, TT, s, sq, L + "sq", rstd, L + "rstd", tmp, L + "tmp", h, L + "h")
            hk = [f"{L}h{k}" for k in range(KC)]
            wink = [f"{L}win{k}" for k in range(KC)]
            for n in range(KC):
                ps, psk = self.psum()
                for k in range(KC):
                    P.op('pe', lambda e, n=n, k=k, ps=ps: e.matmul(ps[:, 0:TT], lhsT=win[:, k, n * 128:(n + 1) * 128], rhs=h[:, k, :],
                                                                    start=(k == 0), stop=(k == KC - 1)), reads=[wink[k], hk[k]], writes=[psk])
                a1 = f1[n % 2]; a1k = L + f"f1_{n % 2}"
                a2 = f2[n % 2]; a2k = L + f"f2_{n % 2}"
                P.op('act', lambda e, a1=a1, ps=ps: e.activation(out=a1[:], in_=ps[:, 0:TT], func=AF.Square), reads=[psk], writes=[a1k])
                P.op('dve', lambda e, a1=a1: e.tensor_scalar(out=a1[:], in0=a1[:], scalar1=C1, scalar2=1.0, op0=ALU.mult, op1=ALU.add),
                     reads=[a1k], writes=[a1k])
                P.op('dve', lambda e, a1=a1, ps=ps: e.tensor_tensor(out=a1[:], in0=a1[:], in1=ps[:, 0:TT], op=ALU.mult), reads=[a1k, psk], writes=[a1k])
                P.op('act', lambda e, a1=a1, a2=a2: e.activation(out=a2[:], in_=a1[:], func=AF.Tanh, scale=C2), reads=[a1k], writes=[a2k])
                P.op('dve', lambda e, a2=a2, ps=ps, n=n: e.scalar_tensor_tensor(out=gg[:, n, :], in0=a2[:], scalar=1.0, in1=ps[:, 0:TT], op0=ALU.add, op1=ALU.mult),
                     reads=[a2k, psk], writes=[f"{LP}gg{n}"])
            for n in range(KC):
                ps, psk = self.psum()
                for k in range(KC):
                    P.op('pe', lambda e, n=n, k=k, ps=ps: e.matmul(ps[:, 0:TT], lhsT=win[:, k, D + n * 128:D + (n + 1) * 128], rhs=h[:, k, :],
                                                                    start=(k == 0), stop=(k == KC - 1)), reads=[wink[k], hk[k]], writes=[psk])
                P.op('act', lambda e, n=n, ps=ps: e.activation(out=xb[:, n, 3:3 + TT], in_=ps[:, 0:TT], func=AF.Identity),
                     reads=[psk], writes=[f"{L}xbm{n}"])
                rk = [f"{L}xbm{n}", f"{L}xbh{n}", L + "vec"]
                P.op('dve', lambda e, n=n: e.tensor_scalar(out=cv[:, n, :], in0=xb[:, n, 0:TT], scalar1=vec[:, 0, n:n + 1], scalar2=vec[:, 4, n:n + 1],
                                                            op0=ALU.mult, op1=ALU.add), reads=rk, writes=[f"{LP}cv{n}"])
                for jj in range(1, 4):
                    P.op('dve', lambda e, n=n, jj=jj: e.scalar_tensor_tensor(
                        out=cv[:, n, :], in0=xb[:, n, jj:jj + TT], scalar=vec[:, jj, n:n + 1], in1=cv[:, n, :],
                        op0=ALU.mult, op1=ALU.add), reads=rk + [f"{LP}cv{n}"], writes=[f"{LP}cv{n}"])
                P.op('act', lambda e, n=n: e.activation(out=cvb[:, n, :], in_=cv[:, n, :], func=AF.Identity), reads=[f"{LP}cv{n}"], writes=[f"{LP}cvb{n}"])
                P.op('pool', lambda e, n=n: e.tensor_copy(out=xb[:, n, 0:3], in_=xb[:, n, TT:TT + 3]),
                     reads=[f"{L}xbm{n}", f"{LP}cv{n}"], writes=[f"{L}xbh{n}"])
            Fl.append(self.defer_end())
            self.defer_begin(pool=[3, 4, 5, 6, 7])
            for n in range(KC):
                blk, eo = n // 2, n % 2
                pr, prk = self.psum()
                pi, pik = self.psum()
                for kk in range(2):
                    P.op('pe', lambda e, kk=kk, blk=blk, eo=eo, pr=pr: e.matmul(
                        pr[:, 0:TT], lhsT=wa[:, blk * 2 + kk, eo * 128:(eo + 1) * 128], rhs=cvb[:, blk * 2 + kk, :], start=(kk == 0), stop=(kk == 1)),
                        reads=[f"{L}wa{blk * 2 + kk}", f"{LP}cvb{blk * 2 + kk}"], writes=[prk])
                for kk in range(2):
                    P.op('pe', lambda e, kk=kk, blk=blk, eo=eo, pi=pi: e.matmul(
                        pi[:, 0:TT], lhsT=wx[:, blk * 2 + kk, eo * 128:(eo + 1) * 128], rhs=cvb[:, blk * 2 + kk, :], start=(kk == 0), stop=(kk == 1)),
                        reads=[f"{L}wx{blk * 2 + kk}", f"{LP}cvb{blk * 2 + kk}"], writes=[pik])
                tr = t1[n % 2]; trk = L + f"t1_{n % 2}"
                ti = t2[n % 2]; tik = L + f"t2_{n % 2}"
                P.op('act', lambda e, tr=tr, pr=pr, n=n: e.activation(out=tr[:], in_=pr[:, 0:TT], func=AF.Tanh, scale=0.5, bias=hb[:, 0, n:n + 1]),
                     reads=[prk, L + "hb"], writes=[trk])
                P.op('act', lambda e, ti=ti, pi=pi, n=n: e.activation(out=ti[:], in_=pi[:, 0:TT], func=AF.Tanh, scale=0.5, bias=hb[:, 1, n:n + 1]),
                     reads=[pik, L + "hb"], writes=[tik])
                P.op('act', lambda e, tr=tr, n=n: e.activation(out=aA[:, n, :], in_=tr[:], func=AF.Exp, scale=c4[:, n:n + 1], bias=c4[:, n:n + 1]),
                     reads=[trk, L + "c4"], writes=[f"{L}aA{n}"])
                P.op('act', lambda e, tr=tr, n=n: e.activation(out=mA[:, n, :], in_=tr[:], func=AF.Exp, scale=c8[:, n:n + 1], bias=c8[:, n:n + 1]),
                     reads=[trk, L + "c8"], writes=[f"{L}mA{n}"])
                P.op('dve', lambda e, n=n: e.tensor_scalar(out=mA[:, n, :], in0=mA[:, n, :], scalar1=-1.0, scalar2=1.0, op0=ALU.mult, op1=ALU.add),
                     reads=[f"{L}mA{n}"], writes=[f"{L}mA{n}"])
                P.op('dve', lambda e, n=n: e.tensor_scalar(out=mA[:, n, :], in0=mA[:, n, :], scalar1=1e-30, scalar2=None, op0=ALU.max), reads=[f"{L}mA{n}"], writes=[f"{L}mA{n}"])
                P.op('dve', lambda e, ti=ti, n=n: e.scalar_tensor_tensor(out=qA[:, n, :], in0=ti[:], scalar=1.0, in1=cv[:, n, :], op0=ALU.add, op1=ALU.mult),
                     reads=[tik, f"{LP}cv{n}"], writes=[f"{L}qA{n}"])
            mks = [f"{L}mA{n}" for n in range(KC)]
            P.op('act', lambda e: e.activation(out=mA[:], in_=mA[:], func=AF.Ln), reads=mks, writes=mks)
            P.op('act', lambda e: e.activation(out=mA[:], in_=mA[:], func=AF.Exp, scale=0.5), reads=mks, writes=mks)
            for n in range(KC):
                hs_ = hs[n % 2]; hsk = L + f"hs_{n % 2}"
                P.op('dve', lambda e, n=n: e.scalar_tensor_tensor(out=qA[:, n, :], in0=qA[:, n, :], scalar=0.5, in1=mA[:, n, :], op0=ALU.mult, op1=ALU.mult),
                     reads=[f"{L}qA{n}", f"{L}mA{n}"], writes=[f"{L}qA{n}"])
                P.op('dve', lambda e, hs_=hs_, n=n: e.tensor_tensor_scan(
                    out=hs_[:], data0=aA[:, n, :], data1=qA[:, n, :], initial=st[:, n:n + 1], op0=ALU.mult, op1=ALU.add),
                    reads=[f"{L}aA{n}", f"{L}qA{n}", f"{L}state{n}"], writes=[hsk])
                P.op('dve', lambda e, hs_=hs_, n=n: e.tensor_copy(out=st[:, n:n + 1], in_=hs_[:, TT - 1:TT]), reads=[hsk], writes=[f"{L}state{n}"])
                P.op('dve', lambda e, hs_=hs_, n=n: e.scalar_tensor_tensor(out=yb[:, n, :], in0=hs_[:], scalar=0.5, in1=gg[:, n, :], op0=ALU.mult, op1=ALU.mult),
                     reads=[hsk, f"{LP}gg{n}"], writes=[f"{L}yb{n}"])
            yk = [f"{L}ybuf{k}" for k in range(KC)]
            for d in range(KC):
                pd, pdk = self.psum()
                for k in range(KC):
                    P.op('pe', lambda e, k=k, d=d, pd=pd: e.matmul(pd[:, 0:TT], lhsT=wout[:, k, d * 128:(d + 1) * 128], rhs=yb[:, k, :],
                                                                    start=(k == 0), stop=(k == KC - 1)),
                         reads=[f"{L}wout{k}", f"{L}yb{k}"], writes=[pdk])
                P.op('act', lambda e, d=d, pd=pd: e.activation(out=y[:, d, :], in_=pd[:, 0:TT], func=AF.Identity), reads=[pdk], writes=[yk[d]])
            self.postnorm_residual(y, yk, xtb, xk, TT, s, sq2, L + "sq2", rstd2, L + "rstd2", xo, LP + "gg")
            P.op('sp', lambda e, t0=t0: e.dma_start(out=xout_v[:, :, t0:t0 + TT], in_=xo[:]),
                 reads=[f"{LP}gg{k}" for k in range(KC)], writes=[self.key('xout')], dma=True)
            Bl.append(self.defer_end())
        NTL = self.NT // TT
        for j in range(NTL):
            do_tile(j)
        self.replay([Fl[0]])
        for j in range(NTL):
            self.replay([Bl[j]] + ([Fl[j + 1]] if j + 1 < NTL else []))
        self.end()

    def tri_inv_alloc(self, L, nb, mk_d):
        P = self.P
        W = dict(L=L, nb=nb)
        W['mk32'] = P.sb([64, 13, 64], F32, L + "ti_mk32")
        W['mk'] = P.sb([64, 13, 64], BF16, L + "ti_mk")
        W['I'] = P.sb([64, 64], BF16, L + "ti_I")
        P.op('sp', lambda e: e.dma_start(out=W['mk32'][:], in_=mk_d), writes=[L + "ti_mk32"], dma=True)
        P.op('act', lambda e: e.activation(out=W['mk'][:], in_=W['mk32'][:], func=AF.Identity), reads=[L + "ti_mk32"], writes=[L + "ti_mk"])
        P.op('act', lambda e: e.activation(out=W['I'][:], in_=W['mk32'][:, 12, :], func=AF.Identity), reads=[L + "ti_mk32"], writes=[L + "ti_I"])
        W['NoA'] = P.sb([64, nb, 6, 64], BF16, L + "ti_NoA")
        W['NoTA'] = P.sb([64, nb, 6, 64], BF16, L + "ti_NoTA")
        W['X'] = P.sb([64, nb, 64], BF16, L + "ti_X")
        W['Ub'] = P.sb([64, nb, 64], BF16, L + "ti_Ub")
        W['Upb'] = P.sb([64, nb, 64], BF16, L + "ti_Upb")
        return W

    def tri_inv(self, N_, Nk, NT_, NTk, XT, XTk, nb, W):
        P = self.P
        L = W['L']
        NW = nb * 64
        mk, I_, NoA, NoTA, X, Ub, Upb = W['mk'], W['I'], W['NoA'], W['NoTA'], W['X'], W['Ub'], W['Upb']
        Xk, Ubk, Upbk = L + "ti_X", L + "ti_Ub", L + "ti_Upb"
        f2 = lambda t: t[:].rearrange("p c t -> p (c t)")
        P.op('dve', lambda e: e.tensor_tensor(out=NoA[:], in0=N_[:, :, None, :].to_broadcast([64, nb, 6, 64]), in1=mk[:, None, 0:6, :].to_broadcast([64, nb, 6, 64]), op=ALU.mult),
             reads=[Nk, L + "ti_mk"], writes=[L + "ti_NoA"])
        P.op('dve', lambda e: e.tensor_tensor(out=NoTA[:], in0=NT_[:, :, None, :].to_broadcast([64, nb, 6, 64]), in1=mk[:, None, 6:12, :].to_broadcast([64, nb, 6, 64]), op=ALU.mult),
             reads=[NTk, L + "ti_mk"], writes=[L + "ti_NoTA"])
        P.op('dve', lambda e: e.tensor_tensor(out=X[:], in0=NoA[:, :, 0, :], in1=I_[:, None, :].to_broadcast([64, nb, 64]), op=ALU.add),
             reads=[L + "ti_NoA", L + "ti_I"], writes=[Xk])
        P.op('dve', lambda e: e.tensor_tensor(out=XT[:], in0=NoTA[:, :, 0, :], in1=I_[:, None, :].to_broadcast([64, nb, 64]), op=ALU.add),
             reads=[L + "ti_NoTA", L + "ti_I"], writes=[XTk])
        for lvl in range(1, 6):
            last = (lvl == 5)
            pu, puk = self.psum()
            for c in range(nb):
                P.op('pe', lambda e, c=c, pu=pu, lvl=lvl: e.matmul(pu[0:64, c * 64:(c + 1) * 64], lhsT=NoA[:, c, lvl, :], rhs=XT[:, c, :], start=True, stop=True),
                     reads=[L + "ti_NoA", XTk], writes=[puk])
            P.op('act', lambda e, pu=pu: e.activation(out=f2(Ub), in_=pu[0:64, 0:NW], func=AF.Identity), reads=[puk], writes=[Ubk])
            if not last:
                pu2, pu2k = self.psum()
                for c in range(nb):
                    P.op('pe', lambda e, c=c, pu2=pu2, lvl=lvl: e.matmul(pu2[0:64, c * 64:(c + 1) * 64], lhsT=NoTA[:, c, lvl, :], rhs=X[:, c, :], start=True, stop=True),
                         reads=[L + "ti_NoTA", Xk], writes=[pu2k])
                P.op('act', lambda e, pu2=pu2: e.activation(out=f2(Upb), in_=pu2[0:64, 0:NW], func=AF.Identity), reads=[pu2k], writes=[Upbk])
            pv, pvk = self.psum()
            for c in range(nb):
                P.op('pe', lambda e, c=c, pv=pv: e.matmul(pv[0:64, c * 64:(c + 1) * 64], lhsT=X[:, c, :], rhs=Ub[:, c, :], start=True, stop=True),
                     reads=[Xk, Ubk], writes=[pvk])
            if not last:
                pv2, pv2k = self.psum()
                for c in range(nb):
                    P.op('pe', lambda e, c=c, pv2=pv2: e.matmul(pv2[0:64, c * 64:(c + 1) * 64], lhsT=XT[:, c, :], rhs=Upb[:, c, :], start=True, stop=True),
                         reads=[XTk, Upbk], writes=[pv2k])
            P.op('dve', lambda e, pv=pv: e.tensor_tensor(out=f2(XT), in0=f2(XT), in1=pv[0:64, 0:NW], op=ALU.add), reads=[XTk, pvk], writes=[XTk])
            if not last:
                P.op('dve', lambda e, pv2=pv2: e.tensor_tensor(out=f2(X), in0=f2(X), in1=pv2[0:64, 0:NW], op=ALU.add), reads=[Xk, pv2k], writes=[Xk])

    def gdn_section(self, x_in, gout, wgd_d, cvec_d, tok_d, nw_d, cm_d, mk_d):
        P = self.begin("gdn")
        s = 0
        TT = 256
        NCH = TT // 64
        L = "gd_"
        NW = NCH * 64
        wgd = P.sb([128, KC, 2056], BF16, L + "wgd")
        self.stg_cols = 1028
        self.defer_begin()
        self.load_w_bf16(wgd, L + "wgd", wgd_d, KC, 2056, piece=1028)
        WL = self.defer_end()
        self.stg_cols = 1408
        wk = [f"{L}wgd{k}" for k in range(KC)]
        cvec = P.sb([128, 4, 12], F32, L + "cvec")
        tokc = P.sb([64, 2, 4], F32, L + "tokc")
        nw = P.sb([128, 1], F32, L + "nw")
        cm = P.sb([64, 5, 64], F32, L + "cm")
        P.op('sp', lambda e: e.dma_start(out=cvec[:], in_=cvec_d), writes=[L + "cvec"], dma=True)
        P.op('sp', lambda e: e.dma_start(out=tokc[:], in_=tok_d), writes=[L + "tokc"], dma=True)
        P.op('sp', lambda e: e.dma_start(out=nw[:], in_=nw_d), writes=[L + "nw"], dma=True)
        P.op('sp', lambda e: e.dma_start(out=cm[:], in_=cm_d), writes=[L + "cm"], dma=True)
        SLm, UTm, Im = cm[:, 0, :], cm[:, 1, :], cm[:, 2, :]
        ones64 = P.sb([64, 128], F32, L + "ones64")
        ident = P.sb([128, 128], F32, L + "ident")
        P.op('pool', lambda e: e.memset(ones64[:], 1.0), writes=[L + "ones64"])
        P.op('pool', lambda e: e.memset(ident[:], 0.0), writes=[L + "ident"])
        P.op('sp', lambda e: e.dma_start(out=ident[0:64, 0:64], in_=cm_d[:, 2, :]), reads=[L + "ident"], writes=[L + "ident"], dma=True)
        P.op('sp', lambda e: e.dma_start(out=ident[64:128, 64:128], in_=cm_d[:, 2, :]), reads=[L + "ident"], writes=[L + "ident"], dma=True)
        ones4_bf = P.sb([128, 128], BF16, L + "ones4")
        P.op('pool', lambda e: e.memset(ones4_bf[:], 0.25), writes=[L + "ones4"])
        nA = P.sb([64, 4], F32, L + "nA")
        P.op('act', lambda e: e.activation(out=nA[:], in_=tokc[:, 0, :], func=AF.Exp), reads=[L + "tokc"], writes=[L + "nA"])
        P.op('dve', lambda e: e.tensor_scalar(out=nA[:], in0=nA[:], scalar1=-1.0, scalar2=None, op0=ALU.mult), reads=[L + "nA"], writes=[L + "nA"])
        self.replay([WL])
        S32 = P.sb([128, 4, 128], F32, L + "S32")
        Sbf = P.sb([128, 4, 128], BF16, L + "Sbf")
        P.op('pool', lambda e: e.memset(S32[:], 0.0), writes=[L + "S32_0", L + "S32_1"])
        P.op('pool', lambda e: e.memset(Sbf[:], 0.0), writes=[L + "Sbf_0", L + "Sbf_1"])
        xq = P.sb([128, 12, TT + 3], F32, L + "xq")
        P.op('pool', lambda e: e.memset(xq[:, :, 0:3], 0.0), writes=[L + f"xqh{n}" for n in range(12)])
        xt = P.sb([128, KC, TT], F32, L + "xt")
        h = P.sb([128, KC, TT], BF16, L + "h")
        tmp = P.sb([128, KC, TT], F32, L + "tmp")
        sq = P.sb([128, KC, TT], BF16, L + "sq")
        rstd = P.sb([128, TT], F32, L + "rstd")
        cva = [P.sb([128, TT], F32, L + f"cva{i}") for i in range(2)]
        cvt = [P.sb([128, TT], F32, L + f"cvt{i}") for i in range(2)]
        qnL = [P.sb([128, 4, TT], BF16, L + f"qn_{i}") for i in range(2)]
        knL = [P.sb([128, 4, TT], BF16, L + f"kn_{i}") for i in range(2)]
        kn32L = [P.sb([128, 4, TT], F32, L + f"kn32_{i}") for i in range(2)]
        vTL = [P.sb([128, 4, TT], F32, L + f"vT_{i}") for i in range(2)]
        zsL = [P.sb([128, 4, TT], F32, L + f"zs_{i}") for i in range(2)]
        rq = P.sb([128, 2, TT], F32, L + "rq")
        rq2 = P.sb([128, TT], F32, L + "rq2")
        sqo = P.sb([128, 4, TT], BF16, L + "sqo")
        ab = P.sb([64, NCH, 8], F32, L + "ab")
        betaL = [P.sb([64, NCH, 4], F32, L + f"beta_{i}") for i in range(2)]
        nbetaL = [P.sb([64, NCH, 4], F32, L + f"nbeta_{i}") for i in range(2)]
        gtL = [P.sb([64, NCH, 4], F32, L + f"gt_{i}") for i in range(2)]
        SCR = []
        for ss in range(2):
            LSs = L + f"s{ss}_"
            d_ = dict(LS=LSs)
            for nm in ("BSL", "BUT", "BI", "eD", "eDT", "eDTs", "bbc", "Nm"):
                d_[nm] = P.sb([64, NCH, 64], F32, LSs + nm)
            d_["eGl"] = P.sb([64, NCH], F32, LSs + "eGl")
            d_["Pm"] = [P.sb([64, NCH, 64], BF16, LSs + f"Pm{i}") for i in range(2)]
            d_["PTm"] = [P.sb([64, NCH, 64], BF16, LSs + f"PTm{i}") for i in range(2)]
            SCR.append(d_)
        ATm = [P.sb([64, NCH, 64], BF16, L + f"AT{hd}") for hd in range(4)]
        for ss in range(2):
            SCR[ss]["TW"] = self.tri_inv_alloc(SCR[ss]["LS"], NCH, mk_d)
        attnT = [P.sb([64, NCH, 64], BF16, L + f"attnT{hd}") for hd in range(4)]
        eG = [P.sb([128, NW], F32, L + f"eG{hd}") for hd in range(4)]
        kgT = [P.sb([128, NW], BF16, L + f"kgT{hd}") for hd in range(4)]
        qdT = [P.sb([128, NW], BF16, L + f"qdT{hd}") for hd in range(4)]
        vb = P.sb([64, NCH, 4, 128], F32, L + "vb")
        kdec = P.sb([64, NCH, 4, 128], BF16, L + "kdec")
        Rt = P.sb([64, 4, 128], F32, L + "Rt")
        Rb = P.sb([64, 4, 128], BF16, L + "Rb")
        vnew = P.sb([64, 4, 128], BF16, L + "vnew")
        oT = P.sb([128, 4, TT], F32, L + "oT")
        ob = P.sb([128, 4, TT], F32, L + "ob")
        xin_v = x_in.rearrange("(k p) t -> p k t", p=128)
        gout_v = gout.rearrange("(k p) t -> p k t", p=128)
        SCQ = 0.5 * (128 ** -0.5)

        def bc(ap, shape):
            return ap.to_broadcast(shape)

        Fl, Bl, HAl, HBl, SCl, EPl0, EPl = [], [], [], [], [], [], []

        def do_tile(j):
            t0 = j * TT
            pj = j % 2
            qn, kn, kn32, vT, zs, beta, nbeta, gt = qnL[pj], knL[pj], kn32L[pj], vTL[pj], zsL[pj], betaL[pj], nbetaL[pj], gtL[pj]
            LP = L + f"p{pj}_"
            self.defer_begin(pool=[0, 7])
            xk = L + "xt"
            P.op('act', lambda e, t0=t0: e.dma_start(out=xt[:], in_=xin_v[:, :, t0:t0 + TT]), writes=[xk], dma=True)
            self.prenorm(xt, xk, TT, s, sq, L + "sq", rstd, L + "rstd", tmp, L + "tmp", h, L + "h")
            hk = [f"{L}h{k}" for k in range(KC)]
            for n in range(12):
                ps, psk = self.psum()
                for k in range(KC):
                    P.op('pe', lambda e, n=n, k=k, ps=ps: e.matmul(ps[:, 0:TT], lhsT=wgd[:, k, n * 128:(n + 1) * 128], rhs=h[:, k, :],
                                                                    start=(k == 0), stop=(k == KC - 1)), reads=[wk[k], hk[k]], writes=[psk])
                P.op('act', lambda e, n=n, ps=ps: e.activation(out=xq[:, n, 3:3 + TT], in_=ps[:, 0:TT], func=AF.Identity),
                     reads=[psk], writes=[f"{L}xqm{n}"])
                ca = cva[n % 2]; cak = L + f"cva{n % 2}"
                ct = cvt[n % 2]; ctk = L + f"cvt{n % 2}"
                rk = [f"{L}xqm{n}", f"{L}xqh{n}", L + "cvec"]
                P.op('dve', lambda e, n=n, ca=ca: e.tensor_scalar(out=ca[:], in0=xq[:, n, 0:TT], scalar1=cvec[:, 0, n:n + 1], scalar2=None, op0=ALU.mult),
                     reads=rk, writes=[cak])
                for jj in range(1, 4):
                    P.op('dve', lambda e, n=n, jj=jj, ca=ca: e.scalar_tensor_tensor(
                        out=ca[:], in0=xq[:, n, jj:jj + TT], scalar=cvec[:, jj, n:n + 1], in1=ca[:], op0=ALU.mult, op1=ALU.add),
                        reads=rk + [cak], writes=[cak])
                P.op('pool', lambda e, n=n: e.tensor_copy(out=xq[:, n, 0:3], in_=xq[:, n, TT:TT + 3]),
                     reads=[f"{L}xqm{n}", cak], writes=[f"{L}xqh{n}"])
                P.op('act', lambda e, ca=ca, ct=ct: e.activation(out=ct[:], in_=ca[:], func=AF.Tanh, scale=0.5), reads=[cak], writes=[ctk])
                typ, hd = n // 4, n % 4
                if typ == 2:
                    P.op('dve', lambda e, ca=ca, ct=ct, hd=hd: e.scalar_tensor_tensor(out=vT[:, hd, :], in0=ct[:], scalar=1.0, in1=ca[:], op0=ALU.add, op1=ALU.mult),
                         reads=[cak, ctk], writes=[f"{LP}vT{hd}"])
                else:
                    P.op('dve', lambda e, ca=ca, ct=ct, n=n: e.scalar_tensor_tensor(out=tmp[:, n, :], in0=ct[:], scalar=1.0, in1=ca[:], op0=ALU.add, op1=ALU.mult),
                         reads=[cak, ctk], writes=[f"{L}tmp{n}"])
            P.op('act', lambda e: e.activation(out=sq[:], in_=tmp[:], func=AF.Square), reads=[f"{L}tmp{n}" for n in range(8)], writes=[L + "sq"])
            for g2_ in range(4):
                p2, p2k = self.psum()
                for u_ in range(2):
                    n = g2_ * 2 + u_
                    P.op('pe', lambda e, p2=p2, n=n, u_=u_: e.matmul(p2[:, u_ * TT:(u_ + 1) * TT], lhsT=ones4_bf[:], rhs=sq[:, n, :], start=True, stop=True),
                         reads=[L + "sq", L + "ones4"], writes=[p2k])
                self.rsqrt_ps(rq[:].rearrange("p u t -> p (u t)"), L + "rq", p2[:, 0:2 * TT], p2k, EPS)
                for u_ in range(2):
                    n = g2_ * 2 + u_
                    typ, hd = n // 4, n % 4
                    if typ == 0:
                        P.op('dve', lambda e, n=n, hd=hd, u_=u_: e.scalar_tensor_tensor(out=qn[:, hd, :], in0=tmp[:, n, :], scalar=SCQ, in1=rq[:, u_, :], op0=ALU.mult, op1=ALU.mult),
                             reads=[f"{L}tmp{n}", L + "rq"], writes=[f"{LP}qn{hd}"])
                    else:
                        P.op('dve', lambda e, n=n, hd=hd, u_=u_: e.scalar_tensor_tensor(out=kn32[:, hd, :], in0=tmp[:, n, :], scalar=0.5, in1=rq[:, u_, :], op0=ALU.mult, op1=ALU.mult),
                             reads=[f"{L}tmp{n}", L + "rq"], writes=[f"{LP}kn32{hd}"])
                        P.op('act', lambda e, hd=hd: e.activation(out=kn[:, hd, :], in_=kn32[:, hd, :], func=AF.Identity), reads=[f"{LP}kn32{hd}"], writes=[f"{LP}kn{hd}"])
            for hd in range(4):
                ps, psk = self.psum()
                for k in range(KC):
                    P.op('pe', lambda e, hd=hd, k=k, ps=ps: e.matmul(ps[:, 0:TT], lhsT=wgd[:, k, 1536 + hd * 128:1536 + (hd + 1) * 128], rhs=h[:, k, :],
                                                                      start=(k == 0), stop=(k == KC - 1)), reads=[wk[k], hk[k]], writes=[psk])
                ct = cvt[hd % 2]; ctk = L + f"cvt{hd % 2}"
                P.op('act', lambda e, ct=ct, ps=ps: e.activation(out=ct[:], in_=ps[:, 0:TT], func=AF.Tanh, scale=0.5), reads=[psk], writes=[ctk])
                P.op('dve', lambda e, ct=ct, ps=ps, hd=hd: e.scalar_tensor_tensor(out=zs[:, hd, :], in0=ct[:], scalar=1.0, in1=ps[:, 0:TT], op0=ALU.add, op1=ALU.mult),
                     reads=[ctk, psk], writes=[f"{LP}zs{hd}"])
            ps, psk = self.psum()
            for c in range(NCH):
                for k in range(KC):
                    P.op('pe', lambda e, c=c, k=k, ps=ps: e.matmul(ps[0:64, c * 8:(c + 1) * 8], lhsT=h[:, k, c * 64:(c + 1) * 64], rhs=wgd[:, k, 2048:2056],
                                                                    start=(k == 0), stop=(k == KC - 1)), reads=[wk[k], hk[k]], writes=[psk])
            P.op('act', lambda e, ps=ps: e.activation(out=ab[:], in_=ps[0:64, 0:NCH * 8].rearrange("p (c n) -> p c n", n=8), func=AF.Identity),
                 reads=[psk], writes=[L + "ab"])
            P.op('act', lambda e: e.activation(out=beta[:], in_=ab[:, :, 4:8], func=AF.Tanh, scale=0.5), reads=[L + "ab"], writes=[LP + "beta"])
            P.op('dve', lambda e: e.tensor_scalar(out=nbeta[:], in0=beta[:], scalar1=-0.5, scalar2=-0.5, op0=ALU.mult, op1=ALU.add),
                 reads=[LP + "beta"], writes=[LP + "nbeta"])
            P.op('dve', lambda e: e.tensor_scalar(out=beta[:], in0=beta[:], scalar1=0.5, scalar2=0.5, op0=ALU.mult, op1=ALU.add),
                 reads=[LP + "beta", LP + "nbeta"], writes=[LP + "beta"])
            P.op('dve', lambda e: e.tensor_tensor(out=gt[:], in0=ab[:, :, 0:4], in1=bc(tokc[:, 1:2, :], [64, NCH, 4]), op=ALU.add),
                 reads=[L + "ab", L + "tokc"], writes=[LP + "gt"])
            P.op('act', lambda e: e.activation(out=gt[:], in_=gt[:], func=AF.Exp), reads=[LP + "gt"], writes=[LP + "gt"])
            P.op('act', lambda e: e.activation(out=gt[:], in_=gt[:], func=AF.Ln, bias=1.0), reads=[LP + "gt"], writes=[LP + "gt"])
            P.op('dve', lambda e: e.tensor_tensor(out=gt[:], in0=gt[:], in1=bc(nA[:, None, :], [64, NCH, 4]), op=ALU.mult),
                 reads=[LP + "gt", L + "nA"], writes=[LP + "gt"])
            Fl.append(self.defer_end())
            def do_head(hd):
                SS = SCR[hd // 2]
                LS = SS["LS"]
                BSL, BUT, BI, eD, eDT, eDTs, bbc, Nm, eGl, Pm, PTm, TW = (SS[k_] for k_ in ("BSL", "BUT", "BI", "eD", "eDT", "eDTs", "bbc", "Nm", "eGl", "Pm", "PTm", "TW"))
                gh = gt[:, :, hd:hd + 1]
                bh = beta[:, :, hd:hd + 1]
                nbh = nbeta[:, :, hd:hd + 1]
                P.op('dve', lambda e, gh=gh: e.tensor_tensor(out=BSL[:], in0=bc(SLm[:, None, :], [64, NCH, 64]), in1=bc(gh, [64, NCH, 64]), op=ALU.mult),
                     reads=[LP + "gt", L + "cm"], writes=[LS + "BSL"])
                P.op('dve', lambda e, gh=gh: e.tensor_tensor(out=BUT[:], in0=bc(UTm[:, None, :], [64, NCH, 64]), in1=bc(gh, [64, NCH, 64]), op=ALU.mult),
                     reads=[LP + "gt", L + "cm"], writes=[LS + "BUT"])
                P.op('dve', lambda e, bh=bh: e.tensor_tensor(out=BI[:], in0=bc(Im[:, None, :], [64, NCH, 64]), in1=bc(bh, [64, NCH, 64]), op=ALU.mult),
                     reads=[LP + "beta", L + "cm"], writes=[LS + "BI"])
                BSLf = BSL[:].rearrange("p c t -> p (c t)")
                BUTf = BUT[:].rearrange("p c t -> p (c t)")
                BIf = BI[:].rearrange("p c t -> p (c t)")
                pX, pXk = self.psum()
                pD, pDk = pX, pXk
                P.op('pe', lambda e, pD=pD, BSLf=BSLf: e.matmul(pD[0:64, 0:NW], lhsT=UTm, rhs=BSLf, start=True, stop=True), reads=[LS + "BSL", L + "cm"], writes=[pDk])
                pDT, pDTk = pX, pXk
                for c in range(NCH):
                    P.op('pe', lambda e, c=c, pDT=pDT: e.matmul(pDT[0:64, NW + c * 64:NW + (c + 1) * 64], lhsT=BSL[:, c, :], rhs=UTm, start=True, stop=True),
                         reads=[LS + "BSL", L + "cm"], writes=[pDTk])
                pG, pGk = self.psum()
                P.op('pe', lambda e, pG=pG, BUTf=BUTf: e.matmul(pG[:, 0:NW], lhsT=ones64[:], rhs=BUTf, start=True, stop=True), reads=[LS + "BUT", L + "ones64"], writes=[pGk])
                pB, pBk = pG, pGk
                P.op('pe', lambda e, pB=pB, BIf=BIf: e.matmul(pB[0:64, NW:2 * NW], lhsT=ones64[:, 0:64], rhs=BIf, start=True, stop=True), reads=[LS + "BI", L + "ones64"], writes=[pBk])
                P.op('act', lambda e, pD=pD: e.activation(out=eD[:].rearrange("p c t -> p (c t)"), in_=pD[0:64, 0:NW], func=AF.Exp), reads=[pDk], writes=[LS + "eD"])
                P.op('act', lambda e, pDT=pDT: e.activation(out=eDT[:].rearrange("p c t -> p (c t)"), in_=pDT[0:64, NW:2 * NW], func=AF.Exp), reads=[pDTk], writes=[LS + "eDT"])
                P.op('act', lambda e, pG=pG, hd=hd: e.activation(out=eG[hd][:], in_=pG[:, 0:NW], func=AF.Exp), reads=[pGk], writes=[L + f"eG{hd}"])
                P.op('act', lambda e, pB=pB: e.activation(out=bbc[:].rearrange("p c t -> p (c t)"), in_=pB[0:64, NW:2 * NW], func=AF.Identity), reads=[pBk], writes=[LS + "bbc"])
                P.op('dve', lambda e: e.tensor_tensor(out=eD[:], in0=eD[:], in1=bc(SLm[:, None, :], [64, NCH, 64]), op=ALU.mult), reads=[LS + "eD", L + "cm"], writes=[LS + "eD"])
                P.op('dve', lambda e: e.tensor_tensor(out=eDT[:], in0=eDT[:], in1=bc(UTm[:, None, :], [64, NCH, 64]), op=ALU.mult), reads=[LS + "eDT", L + "cm"], writes=[LS + "eDT"])
                P.op('dve', lambda e: e.tensor_tensor(out=eDTs[:], in0=eDT[:], in1=bc(cm[:, 3:4, :], [64, NCH, 64]), op=ALU.mult), reads=[LS + "eDT", L + "cm"], writes=[LS + "eDTs"])
                pK, pKk = self.psum()
                pQ, pQk = pK, pKk
                for c in range(NCH):
                    P.op('pe', lambda e, c=c, pK=pK, hd=hd: e.matmul(pK[0:64, c * 64:(c + 1) * 64], lhsT=kn[:, hd, c * 64:(c + 1) * 64], rhs=kn[:, hd, c * 64:(c + 1) * 64],
                                                                      start=True, stop=True), reads=[f"{LP}kn{hd}"], writes=[pKk])
                for c in range(NCH):
                    P.op('pe', lambda e, c=c, pQ=pQ, hd=hd: e.matmul(pQ[0:64, NW + c * 64:NW + (c + 1) * 64], lhsT=kn[:, hd, c * 64:(c + 1) * 64], rhs=qn[:, hd, c * 64:(c + 1) * 64],
                                                                      start=True, stop=True), reads=[f"{LP}kn{hd}", f"{LP}qn{hd}"], writes=[pQk])
                pK3 = pK[0:64, 0:NW].rearrange("p (c t) -> p c t", t=64)
                pQ3 = pQ[0:64, NW:2 * NW].rearrange("p (c t) -> p c t", t=64)
                P0, P0k = Pm[0], LS + "Pm0"
                PT0, PT0k = PTm[0], LS + "PTm0"
                AT, ATk = ATm[hd], L + f"AT{hd}"
                P.op('dve', lambda e, pK3=pK3, nbh=nbh: e.tensor_tensor(out=Nm[:], in0=pK3, in1=bc(nbh, [64, NCH, 64]), op=ALU.mult), reads=[pKk, LP + "nbeta"], writes=[LS + "Nm"])
                P.op('dve', lambda e, P0=P0: e.tensor_tensor(out=P0[:], in0=Nm[:], in1=eD[:], op=ALU.mult), reads=[LS + "Nm", LS + "eD"], writes=[P0k])
                P.op('dve', lambda e, pK3=pK3: e.tensor_tensor(out=Nm[:], in0=pK3, in1=eDTs[:], op=ALU.mult), reads=[pKk, LS + "eDTs", P0k], writes=[LS + "Nm"])
                P.op('dve', lambda e, PT0=PT0: e.scalar_tensor_tensor(out=PT0[:], in0=Nm[:], scalar=-1.0, in1=bbc[:], op0=ALU.mult, op1=ALU.mult),
                     reads=[LS + "Nm", LS + "bbc"], writes=[PT0k])
                P.op('dve', lambda e, pQ3=pQ3, hd=hd: e.tensor_tensor(out=attnT[hd][:], in0=pQ3, in1=eDT[:], op=ALU.mult), reads=[pQk, LS + "eDT"], writes=[L + f"attnT{hd}"])
                self.tri_inv(P0, P0k, PT0, PT0k, AT, ATk, NCH, TW)
                P.op('dve', lambda e, hd=hd: e.tensor_tensor(out=kgT[hd][:], in0=kn32[:, hd, :], in1=eG[hd][:], op=ALU.mult), reads=[f"{LP}kn32{hd}", L + f"eG{hd}"], writes=[L + f"kgT{hd}"])
                P.op('dve', lambda e, hd=hd: e.tensor_tensor(out=qdT[hd][:], in0=qn[:, hd, :], in1=eG[hd][:], op=ALU.mult), reads=[f"{LP}qn{hd}", L + f"eG{hd}"], writes=[L + f"qdT{hd}"])
                pv, pvk = self.psum()
                for c in range(NCH):
                    P.op('pe', lambda e, c=c, pv=pv, hd=hd: e.transpose(out=pv[0:64, c * 128:(c + 1) * 128], in_=vT[:, hd, c * 64:(c + 1) * 64], identity=ident[:]),
                         reads=[f"{LP}vT{hd}", L + "ident"], writes=[pvk])
                P.op('dve', lambda e, pv=pv, hd=hd, bh=bh: e.scalar_tensor_tensor(
                    out=vb[:, :, hd, :], in0=pv[0:64, 0:NCH * 128].rearrange("p (c d) -> p c d", d=128), scalar=0.5, in1=bc(bh, [64, NCH, 128]), op0=ALU.mult, op1=ALU.mult),
                    reads=[pvk, LP + "beta"], writes=[L + f"vb{hd}"])
                pk_, pkk_ = self.psum()
                for c in range(NCH):
                    P.op('pe', lambda e, c=c, pk_=pk_, hd=hd: e.transpose(out=pk_[0:64, c * 128:(c + 1) * 128], in_=kn32[:, hd, c * 64:(c + 1) * 64], identity=ident[:]),
                         reads=[f"{LP}kn32{hd}", L + "ident"], writes=[pkk_])
                P.op('dve', lambda e, pk_=pk_, hd=hd: e.tensor_tensor(
                    out=kdec[:, :, hd, :], in0=pk_[0:64, 0:NCH * 128].rearrange("p (c d) -> p c d", d=128), in1=bc(eDT[:, :, 63:64], [64, NCH, 128]), op=ALU.mult),
                    reads=[pkk_, LS + "eDT"], writes=[L + f"kdec{hd}"])
            self.defer_begin(pool=[1, 2, 3])
            do_head(0)
            do_head(1)
            HAl.append(self.defer_end())
            self.defer_begin(pool=[4, 5, 6])
            do_head(2)
            do_head(3)
            HBl.append(self.defer_end())
            def scan_stream(h0, sid):
                hsl = slice(h0, h0 + 2)
                S32k, Sbfk, Rtk, Rbk, vnk = (L + f"{nm}_{sid}" for nm in ("S32", "Sbf", "Rt", "Rb", "vnew"))
                for c in range(NCH):
                    p1, p1k = self.psum()
                    for hd in (h0, h0 + 1):
                        P.op('pe', lambda e, c=c, hd=hd, p1=p1: e.matmul(p1[0:64, (hd - h0) * 128:(hd - h0 + 1) * 128], lhsT=kgT[hd][:, c * 64:(c + 1) * 64], rhs=Sbf[:, hd, :], start=True, stop=True),
                             reads=[L + f"kgT{hd}", Sbfk], writes=[p1k])
                    P.op('dve', lambda e, c=c, p1=p1: e.tensor_tensor(out=Rt[:, hsl, :], in0=p1[0:64, 0:256].rearrange("p (h d) -> p h d", d=128), in1=bc(nbeta[:, c, hsl, None], [64, 2, 128]), op=ALU.mult),
                         reads=[p1k, LP + "nbeta"], writes=[Rtk])
                    P.op('dve', lambda e, c=c: e.tensor_tensor(out=Rb[:, hsl, :], in0=Rt[:, hsl, :], in1=vb[:, c, hsl, :], op=ALU.add), reads=[Rtk, L + f"vb{h0}", L + f"vb{h0 + 1}"], writes=[Rbk])
                    p2, p2k = self.psum()
                    for hd in (h0, h0 + 1):
                        P.op('pe', lambda e, c=c, hd=hd, p2=p2: e.matmul(p2[0:64, (hd - h0) * 128:(hd - h0 + 1) * 128], lhsT=ATm[hd][:, c, :], rhs=Rb[:, hd, :], start=True, stop=True),
                             reads=[L + f"AT{hd}", Rbk], writes=[p2k])
                    P.op('act', lambda e, p2=p2: e.activation(out=vnew[:, hsl, :], in_=p2[0:64, 0:256].rearrange("p (h d) -> p h d", d=128), func=AF.Identity), reads=[p2k], writes=[vnk])
                    p3, p3k = self.psum()
                    for hd in (h0, h0 + 1):
                        P.op('pe', lambda e, c=c, hd=hd, p3=p3: e.matmul(p3[:, (hd - h0) * 64:(hd - h0 + 1) * 64], lhsT=Sbf[:, hd, :], rhs=qdT[hd][:, c * 64:(c + 1) * 64], start=True, stop=False),
                             reads=[Sbfk, L + f"qdT{hd}"], writes=[p3k])
                        P.op('pe', lambda e, c=c, hd=hd, p3=p3: e.matmul(p3[:, (hd - h0) * 64:(hd - h0 + 1) * 64], lhsT=vnew[:, hd, :], rhs=attnT[hd][:, c, :], start=False, stop=True),
                             reads=[vnk, L + f"attnT{hd}"], writes=[p3k])
                    P.op('act', lambda e, c=c, p3=p3: e.activation(out=oT[:, hsl, c * 64:(c + 1) * 64], in_=p3[:, 0:128].rearrange("p (h t) -> p h t", t=64), func=AF.Identity),
                         reads=[p3k], writes=[L + f"oT{h0}", L + f"oT{h0 + 1}"])
                    p4, p4k = self.psum()
                    for hd in (h0, h0 + 1):
                        P.op('pe', lambda e, c=c, hd=hd, p4=p4: e.matmul(p4[:, (hd - h0) * 128:(hd - h0 + 1) * 128], lhsT=kdec[:, c, hd, :], rhs=vnew[:, hd, :], start=True, stop=True),
                             reads=[L + f"kdec{hd}", vnk], writes=[p4k])
                    for hd in (h0, h0 + 1):
                        P.op('dve', lambda e, c=c, hd=hd, p4=p4: e.scalar_tensor_tensor(
                            out=S32[:, hd, :], in0=S32[:, hd, :], scalar=eG[hd][:, c * 64 + 63:c * 64 + 64], in1=p4[:, (hd - h0) * 128:(hd - h0 + 1) * 128], op0=ALU.mult, op1=ALU.add),
                            reads=[S32k, L + f"eG{hd}", p4k], writes=[S32k])
                    P.op('act', lambda e: e.activation(out=Sbf[:, hsl, :], in_=S32[:, hsl, :], func=AF.Identity), reads=[S32k], writes=[Sbfk])

            streams_ = []
            for sid, h0 in enumerate((0, 2)):
                self.defer_begin(pool=[1, 2, 3] if sid == 0 else [4, 5, 6])
                scan_stream(h0, sid)
                streams_.append(self.defer_end())
            SCl.append(streams_)
            self.defer_begin(pool=[1])
            P.op('act', lambda e: e.activation(out=sqo[:], in_=oT[:], func=AF.Square), reads=[L + f"oT{hd}" for hd in range(4)], writes=[L + "sqo"])
            EPl0.append(self.defer_end())
            chains_ = []
            for hd in range(4):
                self.defer_begin(pool=[1 + hd])
                po, pok = self.psum()
                P.op('pe', lambda e, hd=hd, po=po: e.matmul(po[:, 0:TT], lhsT=self.ones128_bf[:], rhs=sqo[:, hd, 0:TT], start=True, stop=True), reads=[L + "sqo"], writes=[pok])
                self.rsqrt_ps(tmp[:, hd, :], f"{L}tmp{hd}", po[:, 0:TT], pok, EPS)
                P.op('dve', lambda e, hd=hd: e.scalar_tensor_tensor(out=ob[:, hd, :], in0=oT[:, hd, :], scalar=nw[:, 0:1], in1=tmp[:, hd, :], op0=ALU.mult, op1=ALU.mult),
                     reads=[L + f"oT{hd}", L + "nw", f"{L}tmp{hd}"], writes=[L + f"ob{hd}"])
                P.op('dve', lambda e, hd=hd: e.scalar_tensor_tensor(out=ob[:, hd, :], in0=ob[:, hd, :], scalar=0.5, in1=zs[:, hd, :], op0=ALU.mult, op1=ALU.mult),
                     reads=[L + f"ob{hd}", f"{LP}zs{hd}"], writes=[L + f"ob{hd}"])
                chains_.append(self.defer_end())
            EPl.append(chains_)
            self.defer_begin(pool=[1])
            P.op('sp', lambda e, t0=t0: e.dma_start(out=gout_v[:, :, t0:t0 + TT], in_=ob[:]), reads=[L + f"ob{hd}" for hd in range(4)], writes=[self.key('gout')], dma=True)
            Bl.append(self.defer_end())
        NTL = self.NT // TT
        for j in range(NTL):
            do_tile(j)
        self.replay([Fl[0]])
        for j in range(NTL):
            self.replay([HAl[j], HBl[j]] + ([Fl[j + 1]] if j + 1 < NTL else []))
            self.replay(SCl[j])
            self.replay([EPl0[j]])
            self.replay(EPl[j])
            self.replay([Bl[j]])
        self.end()


def declare_inputs(b, NT):
    I = {}
    I['cT'] = b.din("cT", [128, 8]); I['ada_w'] = b.din("ada_w", [4, 8, 128, 3072]); I['ada_b'] = b.din("ada_b", [4, 128, 24])
    I['npre'] = b.din("npre", [4, 128, 8]); I['npost'] = b.din("npost", [4, 128, 8])
    I['xT'] = b.din("xT", [1024, NT])
    I['wgd'] = b.din("wgd", [128, 8, 2056]); I['g_cvec'] = b.din("g_cvec", [128, 4, 12]); I['g_tok'] = b.din("g_tok", [64, 2, 4])
    I['g_nw'] = b.din("g_nw", [128, 1]); I['cm'] = b.din("cm", [64, 5, 64]); I['timk'] = b.din("timk", [64, 13, 64])
    I['wrw'] = b.din("wrw", [128, 8, 1792]); I['mwout'] = b.din("mwout", [128, 8, 1024])
    I['rw_w2a2'] = b.din("rw_w2a2", [128, 1, 512]); I['rw_g2'] = b.din("rw_g2", [128, 1, 512])
    I['rw_mu'] = b.din("rw_mu", [128, 14]); I['rw_vec'] = b.din("rw_vec", [128, 8, 4])
    return I


def common_maps(inp):
    m = {}
    m["ada_w"] = np.ascontiguousarray(np.asarray(inp['ada_w'], np.float32).reshape(4, 8, 128, 3072))
    m["ada_b"] = np.stack([fm_vec(np.asarray(inp['ada_b']).reshape(4, 3072)[s]) for s in range(4)])
    m["npre"] = np.stack([fm_vec(np.asarray(inp['norm_pre']).reshape(4, 1024)[s]) for s in range(4)])
    m["npost"] = np.stack([fm_vec(np.asarray(inp['norm_post']).reshape(4, 1024)[s]) for s in range(4)])
    win = np.asarray(inp['mix_w_in'][0], np.float32)
    m["wgd"] = fm_mat(win[:, :2056])
    cw = np.asarray(inp['gdn_conv_w'][0], np.float32)
    m["g_cvec"] = np.ascontiguousarray(np.stack([fm_vec(cw[j]) for j in range(4)], axis=1))
    m["g_tok"] = np.ascontiguousarray(np.broadcast_to(np.stack([np.asarray(inp['gdn_a_log'][0]), np.asarray(inp['gdn_dt_bias'][0])])[None], (64, 2, 4)).astype(np.float32))
    m["g_nw"] = np.ascontiguousarray(np.asarray(inp['gdn_norm_w'][0], np.float32).reshape(128, 1))
    s = np.arange(64)[:, None]; t = np.arange(64)[None, :]
    m["cm"] = np.ascontiguousarray(np.stack([(s > t), (s <= t), (s == t), (s < t), np.ones((64, 64), bool)], axis=1).astype(np.float32))
    m["wrw"] = fm_mat(win[:, 2056:])
    m["mwout"] = fm_mat(np.asarray(inp['mix_w_out'][0], np.float32))
    m["rw_w2a2"] = np.ascontiguousarray(np.concatenate([np.asarray(inp['rwkv_w2'][0]), np.asarray(inp['rwkv_a2'][0])], axis=0).astype(np.float32).reshape(128, 1, 512))
    m["rw_g2"] = np.ascontiguousarray(np.asarray(inp['rwkv_g2'][0], np.float32).reshape(128, 1, 512))
    m["rw_mu"] = fm_vec(np.asarray(inp['rwkv_mu'][0]))
    z4 = np.zeros(512, np.float32)
    m["rw_vec"] = np.ascontiguousarray(np.stack([fm_vec(np.asarray(inp[k][0]).reshape(-1)) for k in
                                                 ('rwkv_w0', 'rwkv_a0', 'rwkv_k_k', 'rwkv_k_a', 'rwkv_r_k', 'rwkv_ln_w', 'rwkv_ln_b')] + [fm_vec(z4)], axis=1))
    mk = []
    for l in range(6):
        bsz = 2 ** l
        mk.append(((s // (2 * bsz)) == (t // (2 * bsz))) & ((s % (2 * bsz)) >= bsz) & ((t % (2 * bsz)) < bsz))
    mk = np.stack(mk + [m_.T for m_ in mk] + [s == t], axis=1).astype(np.float32)
    m["timk"] = np.ascontiguousarray(mk)
    return m


def core_maps(inp, i, NT):
    return {"cT": fm_vec(np.asarray(inp['c'][i])), "xT": np.ascontiguousarray(np.asarray(inp['x'][i, :NT], np.float32).T)}


def _rwkv_section(self, x_in, gout, x_out, I, rdbg=None):
    P = self.begin("rwkv")
    s = 0
    TT = 256
    NCH = TT // 64
    L = "rk_"
    wrw = P.sb([128, KC, 1792], BF16, L + "wrw")
    wout = P.sb([128, KC, D], BF16, L + "wout")
    w2a2 = P.sb([128, 1, 512], BF16, L + "w2a2")
    g2 = P.sb([128, 1, 512], BF16, L + "g2")
    self.stg_cols = 1024
    self.defer_begin()
    self.load_w_bf16(wrw, L + "wrw", I['wrw'], KC, 1792, piece=896)
    self.load_w_bf16(w2a2, L + "w2a2", I['rw_w2a2'], 1, 512)
    self.load_w_bf16(g2, L + "g2", I['rw_g2'], 1, 512)
    self.load_w_bf16(wout, L + "wout", I['mwout'], KC, D, piece=1024)
    WL = self.defer_end()
    self.stg_cols = 1408
    wk = [f"{L}wrw{k}" for k in range(KC)]
    mu = P.sb([128, 14], F32, L + "mu")
    rv = P.sb([128, 8, 4], F32, L + "rv")
    cm = P.sb([64, 5, 64], F32, L + "cm")
    P.op('sp', lambda e: e.dma_start(out=mu[:], in_=I['rw_mu']), writes=[L + "mu"], dma=True)
    P.op('sp', lambda e: e.dma_start(out=rv[:], in_=I['rw_vec']), writes=[L + "rv"], dma=True)
    P.op('sp', lambda e: e.dma_start(out=cm[:], in_=I['cm']), writes=[L + "cm"], dma=True)
    SLm, UTm, SUm = cm[:, 0, :], cm[:, 1, :], cm[:, 3, :]
    ident = P.sb([128, 128], F32, L + "ident")
    P.op('pool', lambda e: e.memset(ident[:], 0.0), writes=[L + "ident"])
    P.op('sp', lambda e: e.dma_start(out=ident[0:64, 0:64], in_=I['cm'][:, 2, :]), reads=[L + "ident"], writes=[L + "ident"], dma=True)
    P.op('sp', lambda e: e.dma_start(out=ident[64:128, 64:128], in_=I['cm'][:, 2, :]), reads=[L + "ident"], writes=[L + "ident"], dma=True)
    bd1 = P.sb([128, 128], BF16, L + "bd1")
    bd64 = P.sb([128, 128], BF16, L + "bd64")
    for t_, val, nm in ((bd1, 1.0, "bd1"), (bd64, 1.0 / 64.0, "bd64")):
        P.op('pool', lambda e, t_=t_: e.memset(t_[:], 0.0), writes=[L + nm])
        P.op('pool', lambda e, t_=t_, val=val: e.memset(t_[0:64, 0:64], val), reads=[L + nm], writes=[L + nm])
        P.op('pool', lambda e, t_=t_, val=val: e.memset(t_[64:128, 64:128], val), reads=[L + nm], writes=[L + nm])
    rmask = P.sb([128, TT], F32, L + "rmask")
    P.op('pool', lambda e: e.memset(rmask[:], 1.0), writes=[L + "rmask"])
    P.op('pool', lambda e: e.memset(rmask[:].rearrange("p (c t) -> p c t", t=64)[:, :, 0:1], 0.0), reads=[L + "rmask"], writes=[L + "rmask"])
    dv = P.sb([128, 3, 4], F32, L + "dv")
    P.op('dve', lambda e: e.tensor_scalar(out=dv[:, 0:2, :], in0=rv[:, 0:2, :], scalar1=0.5, scalar2=None, op0=ALU.mult), reads=[L + "rv"], writes=[L + "dv"])
    P.op('dve', lambda e: e.tensor_scalar(out=dv[:, 2, :], in0=rv[:, 3, :], scalar1=-1.0, scalar2=1.0, op0=ALU.mult, op1=ALU.add), reads=[L + "rv", L + "dv"], writes=[L + "dv"])
    W0H, A0H, OMKA = dv[:, 0, :], dv[:, 1, :], dv[:, 2, :]
    KK_, KA_, RK_, LNW, LNB = rv[:, 2, :], rv[:, 3, :], rv[:, 4, :], rv[:, 5, :], rv[:, 6, :]
    self.replay([WL])
    Z32 = P.sb([64, 8, 64], F32, L + "Z32")
    Zbf = P.sb([64, 8, 64], BF16, L + "Zbf")
    P.op('pool', lambda e: e.memset(Z32[:], 0.0), writes=[L + "Z32"])
    P.op('pool', lambda e: e.memset(Zbf[:], 0.0), writes=[L + "Zbf"])
    xr = P.sb([128, 14, TT + 1], F32, L + "xr")
    hst = P.sb([128, 14, 1], F32, L + "hst")
    P.op('pool', lambda e: e.memset(xr[:, :, 0:1], 0.0), writes=[L + f"xrh{n}" for n in range(14)])
    xt = P.sb([128, KC, TT], F32, L + "xt")
    h = P.sb([128, KC, TT], BF16, L + "h")
    tmp = P.sb([128, KC, TT], F32, L + "tmp")
    sq = P.sb([128, KC, TT], BF16, L + "sq")
    rstd = P.sb([128, TT], F32, L + "rstd")
    dtm = [P.sb([128, TT], F32, L + f"dtm{i}") for i in range(2)]
    twb = P.sb([128, TT], BF16, L + "twb")
    adb = P.sb([128, TT], BF16, L + "adb")
    sgb = P.sb([128, TT], BF16, L + "sgb")
    t_a = P.sb([128, 4, TT], F32, L + "t_a")
    t_g = P.sb([128, 4, TT], F32, L + "t_g")
    t_kk = P.sb([128, 4, TT], F32, L + "t_kk")
    t_kp = P.sb([128, 4, TT], F32, L + "t_kp")
    t_lw = P.sb([128, 4, TT], F32, L + "t_lw")
    t_LW = P.sb([128, 4, TT], F32, L + "t_LW")
    t_eP = P.sb([128, 4, TT], F32, L + "t_eP")
    t_eM = P.sb([128, 4, TT], F32, L + "t_eM")
    rq4 = P.sb([128, 4, TT], F32, L + "rq4")
    rt = P.sb([128, 4, TT], BF16, L + "rt")
    kt = P.sb([128, 4, TT], BF16, L + "kt")
    bt = P.sb([128, 4, TT], BF16, L + "bt")
    kkt = P.sb([128, 4, TT], BF16, L + "kkt")
    rtL = P.sb([64, 4, TT], BF16, L + "rtL")
    ktL = P.sb([64, 4, TT], BF16, L + "ktL")
    btL = P.sb([64, 4, TT], BF16, L + "btL")
    kktL = P.sb([64, 4, TT], BF16, L + "kktL")
    ePL = P.sb([64, 4, NCH, 1], F32, L + "ePL")
    wcall = P.sb([64, NCH, 8], F32, L + "wcall")
    Vtok = P.sb([64, NCH, 512], BF16, L + "Vtok")
    Ktok = P.sb([64, NCH, 512], BF16, L + "Ktok")
    Btok = P.sb([64, NCH, 512], BF16, L + "Btok")
    Nm_ = [P.sb([64, 8, 64], BF16, L + f"N{i}") for i in range(2)]
    NTm_ = [P.sb([64, 8, 64], BF16, L + f"NT{i}") for i in range(2)]
    ATm = [P.sb([64, 8, 64], BF16, L + f"AT{i}") for i in range(2)]
    LkT = [P.sb([64, 8, 64], BF16, L + f"LkT{i}") for i in range(2)]
    RkT = [P.sb([64, 8, 64], BF16, L + f"RkT{i}") for i in range(2)]
    RbT = [P.sb([64, 8, 64], BF16, L + f"RbT{i}") for i in range(2)]
    TW = self.tri_inv_alloc(L, 8, I['timk'])
    X1b = P.sb([64, 8, 64], BF16, L + "X1b")
    Eb = P.sb([64, 8, 64], BF16, L + "Eb")
    yT = P.sb([128, 4, TT], F32, L + "yT")
    ymix = h
    gld = t_LW
    yodd = t_lw
    xin_v = x_in.rearrange("(k p) t -> p k t", p=128)
    xout_v = x_out.rearrange("(k p) t -> p k t", p=128)
    gout_v = gout.rearrange("(k p) t -> p k t", p=128)
    rdbg_v = rdbg.rearrange("(k p) t -> p k t", p=128) if rdbg is not None else None

    def bc(ap, shape):
        return ap.to_broadcast(shape)

    def fl(t):
        return t[:].rearrange("p a t -> p (a t)")

    for j in range(self.NT // TT):
        t0 = j * TT
        xk = L + "xt"
        P.op('act', lambda e, t0=t0: e.dma_start(out=xt[:], in_=xin_v[:, :, t0:t0 + TT]), writes=[xk], dma=True)
        self.prenorm(xt, xk, TT, s, sq, L + "sq", rstd, L + "rstd", tmp, L + "tmp", h, L + "h")
        hk = [f"{L}h{k}" for k in range(KC)]
        for n in range(14):
            ps, psk = self.psum()
            for k in range(KC):
                P.op('pe', lambda e, n=n, k=k, ps=ps: e.matmul(ps[:, 0:TT], lhsT=wrw[:, k, n * 128:(n + 1) * 128], rhs=h[:, k, :],
                                                                start=(k == 0), stop=(k == KC - 1)), reads=[wk[k], hk[k]], writes=[psk])
            P.op('act', lambda e, n=n, ps=ps: e.activation(out=xr[:, n, 1:1 + TT], in_=ps[:, 0:TT], func=AF.Identity), reads=[psk], writes=[f"{L}xrm{n}"])
            P.op('pool', lambda e, n=n: e.tensor_copy(out=hst[:, n, :], in_=xr[:, n, TT:TT + 1]), reads=[f"{L}xrm{n}"], writes=[f"{L}hst{n}"])
            dt_ = dtm[n % 2]; dtk = L + f"dtm{n % 2}"
            P.op('dve', lambda e, n=n, dt_=dt_: e.tensor_tensor(out=dt_[:], in0=xr[:, n, 0:TT], in1=xr[:, n, 1:1 + TT], op=ALU.subtract),
                 reads=[f"{L}xrm{n}", f"{L}xrh{n}"], writes=[dtk])
            P.op('dve', lambda e, n=n, dt_=dt_: e.scalar_tensor_tensor(out=xr[:, n, 1:1 + TT], in0=dt_[:], scalar=mu[:, n:n + 1], in1=xr[:, n, 1:1 + TT], op0=ALU.mult, op1=ALU.add),
                 reads=[dtk, L + "mu", f"{L}xrm{n}", f"{L}hst{n}"], writes=[f"{L}xrm{n}"])
            P.op('pool', lambda e, n=n: e.tensor_copy(out=xr[:, n, 0:1], in_=hst[:, n, :]), reads=[f"{L}hst{n}", dtk], writes=[f"{L}xrh{n}"])
        R_ = lambda i: xr[:, i, 1:1 + TT]
        K_ = lambda i: xr[:, 4 + i, 1:1 + TT]
        V_ = lambda i: xr[:, 8 + i, 1:1 + TT]
        rkeys = [f"{L}xrm{i}" for i in range(4)]
        kkeys = [f"{L}xrm{4 + i}" for i in range(4)]
        vkeys = [f"{L}xrm{8 + i}" for i in range(4)]
        P.op('act', lambda e: e.activation(out=twb[0:64, :], in_=xr[0:64, 12, 1:1 + TT], func=AF.Tanh), reads=[f"{L}xrm12"], writes=[L + "twb"])
        P.op('act', lambda e: e.activation(out=adb[64:128, :], in_=xr[64:128, 12, 1:1 + TT], func=AF.Identity), reads=[f"{L}xrm12"], writes=[L + "adb"])
        d0 = dtm[0]; d0k = L + "dtm0"
        P.op('act', lambda e: e.activation(out=d0[:], in_=xr[:, 13, 1:1 + TT], func=AF.Tanh, scale=0.5), reads=[f"{L}xrm13"], writes=[d0k])
        P.op('dve', lambda e: e.tensor_scalar(out=sgb[:], in0=d0[:], scalar1=0.5, scalar2=0.5, op0=ALU.mult, op1=ALU.add), reads=[d0k], writes=[L + "sgb"])
        for i in range(4):
            pw, pwk = self.psum()
            P.op('pe', lambda e, i=i, pw=pw: e.matmul(pw[:, 0:TT], lhsT=w2a2[0:64, 0, i * 128:(i + 1) * 128], rhs=twb[0:64, :], start=True, stop=True),
                 reads=[L + "w2a20", L + "twb"], writes=[pwk])
            d1 = dtm[1]; d1k = L + "dtm1"
            P.op('act', lambda e, i=i, pw=pw: e.activation(out=d1[:], in_=pw[:, 0:TT], func=AF.Tanh, scale=0.5, bias=W0H[:, i:i + 1]), reads=[pwk, L + "dv"], writes=[d1k])
            P.op('dve', lambda e, i=i: e.tensor_scalar(out=t_lw[:, i, :], in0=d1[:], scalar1=1.0, scalar2=-0.30326533, op0=ALU.add, op1=ALU.mult), reads=[d1k], writes=[f"{L}t_lw{i}"])
            pa, pak = self.psum()
            P.op('pe', lambda e, i=i, pa=pa: e.matmul(pa[:, 0:TT], lhsT=w2a2[64:128, 0, i * 128:(i + 1) * 128], rhs=adb[64:128, :], start=True, stop=True),
                 reads=[L + "w2a20", L + "adb"], writes=[pak])
            P.op('act', lambda e, i=i, pa=pa: e.activation(out=d0[:], in_=pa[:, 0:TT], func=AF.Tanh, scale=0.5, bias=A0H[:, i:i + 1]), reads=[pak, L + "dv"], writes=[d0k])
            P.op('dve', lambda e, i=i: e.tensor_scalar(out=t_a[:, i, :], in0=d0[:], scalar1=0.5, scalar2=0.5, op0=ALU.mult, op1=ALU.add), reads=[d0k], writes=[f"{L}t_a{i}"])
            pg, pgk = self.psum()
            P.op('pe', lambda e, i=i, pg=pg: e.matmul(pg[:, 0:TT], lhsT=g2[:, 0, i * 128:(i + 1) * 128], rhs=sgb[:], start=True, stop=True),
                 reads=[L + "g20", L + "sgb"], writes=[pgk])
            P.op('act', lambda e, i=i, pg=pg: e.activation(out=t_g[:, i, :], in_=pg[:, 0:TT], func=AF.Identity), reads=[pgk], writes=[f"{L}t_g{i}"])
        chains = []
        for i in range(4):
            self.defer_begin(pool=[2 * i, 2 * i + 1])
            P.op('dve', lambda e, i=i: e.tensor_scalar(out=t_kk[:, i, :], in0=K_(i), scalar1=KK_[:, i:i + 1], scalar2=None, op0=ALU.mult), reads=[kkeys[i], L + "rv"], writes=[f"{L}t_kk{i}"])
            P.op('act', lambda e, i=i: e.activation(out=sq[:, i, :], in_=t_kk[:, i, :], func=AF.Square), reads=[f"{L}t_kk{i}", L + "sq"], writes=[f"{L}sqk{i}"])
            pq, pqk = self.psum()
            P.op('pe', lambda e, i=i, pq=pq: e.matmul(pq[:, 0:TT], lhsT=bd1[:], rhs=sq[:, i, :], start=True, stop=True), reads=[f"{L}sqk{i}", L + "bd1"], writes=[pqk])
            self.rsqrt_ps(rq4[:, i, :], f"{L}rq4_{i}", pq[:, 0:TT], pqk, EPS)
            P.op('dve', lambda e, i=i: e.tensor_tensor(out=t_kk[:, i, :], in0=t_kk[:, i, :], in1=rq4[:, i, :], op=ALU.mult), reads=[f"{L}t_kk{i}", f"{L}rq4_{i}"], writes=[f"{L}t_kk{i}"])
            P.op('dve', lambda e, i=i: e.tensor_scalar(out=t_kp[:, i, :], in0=t_a[:, i, :], scalar1=KA_[:, i:i + 1], scalar2=OMKA[:, i:i + 1], op0=ALU.mult, op1=ALU.add),
                 reads=[f"{L}t_a{i}", L + "rv", L + "dv"], writes=[f"{L}t_kp{i}"])
            P.op('dve', lambda e, i=i: e.tensor_tensor(out=t_kp[:, i, :], in0=K_(i), in1=t_kp[:, i, :], op=ALU.mult), reads=[kkeys[i], f"{L}t_kp{i}"], writes=[f"{L}t_kp{i}"])
            chains.append(self.defer_end())
        self.replay(chains)
        for i in range(4):
            P.op('dve', lambda e, i=i: e.tensor_tensor_scan(out=t_LW[:, i, :], data0=rmask[:], data1=t_lw[:, i, :], initial=0.0, op0=ALU.mult, op1=ALU.add),
                 reads=[L + "rmask", f"{L}t_lw{i}"], writes=[f"{L}t_LW{i}"])
        a4 = lambda nm: [f"{L}{nm}{i}" for i in range(4)]
        P.op('act', lambda e: e.activation(out=fl(t_eP), in_=fl(t_LW), func=AF.Exp), reads=a4("t_LW"), writes=[L + "t_eP"])
        P.op('act', lambda e: e.activation(out=fl(t_eM), in_=fl(t_LW), func=AF.Exp, scale=-1.0), reads=a4("t_LW"), writes=[L + "t_eM"])
        P.op('dve', lambda e: e.tensor_tensor(out=fl(t_lw), in0=fl(t_LW), in1=fl(t_lw), op=ALU.subtract), reads=a4("t_LW") + a4("t_lw"), writes=a4("t_lw"))
        P.op('act', lambda e: e.activation(out=fl(t_lw), in_=fl(t_lw), func=AF.Exp), reads=a4("t_lw"), writes=a4("t_lw"))
        P.op('dve', lambda e: e.tensor_tensor(out=rt[:], in0=xr[:, 0:4, 1:1 + TT], in1=t_eP[:], op=ALU.mult), reads=rkeys + [L + "t_eP"], writes=[L + "rt"])
        P.op('dve', lambda e: e.tensor_tensor(out=fl(kkt), in0=fl(t_kk), in1=fl(t_lw), op=ALU.mult), reads=a4("t_kk") + a4("t_lw"), writes=[L + "kkt"])
        P.op('dve', lambda e: e.tensor_tensor(out=fl(t_kk), in0=fl(t_kk), in1=fl(t_a), op=ALU.mult), reads=a4("t_kk") + a4("t_a"), writes=a4("t_kk"))
        ePc = t_eP[:].rearrange("p a (c t) -> p a c t", t=64)[:, :, :, 63:64]
        P.op('dve', lambda e: e.tensor_tensor(out=fl(t_lw), in0=fl(t_kp), in1=fl(t_eM), op=ALU.mult), reads=a4("t_kp") + [L + "t_eM"] + a4("t_lw"), writes=a4("t_lw"))
        P.op('act', lambda e: e.activation(out=fl(kt), in_=fl(t_lw), func=AF.Identity), reads=a4("t_lw"), writes=[L + "kt"])
        P.op('dve', lambda e: e.tensor_tensor(out=t_lw[:].rearrange("p a (c t) -> p a c t", t=64), in0=t_lw[:].rearrange("p a (c t) -> p a c t", t=64),
                                              in1=bc(ePc, [128, 4, NCH, 64]), op=ALU.mult), reads=a4("t_lw") + [L + "t_eP"], writes=a4("t_lw"))
        P.op('dve', lambda e: e.tensor_tensor(out=fl(t_LW), in0=fl(t_kk), in1=fl(t_eM), op=ALU.mult), reads=a4("t_kk") + [L + "t_eM"] + a4("t_LW"), writes=a4("t_LW"))
        P.op('act', lambda e: e.activation(out=fl(bt), in_=fl(t_LW), func=AF.Identity), reads=a4("t_LW"), writes=[L + "bt"])
        P.op('dve', lambda e: e.tensor_tensor(out=t_LW[:].rearrange("p a (c t) -> p a c t", t=64), in0=t_LW[:].rearrange("p a (c t) -> p a c t", t=64),
                                              in1=bc(ePc, [128, 4, NCH, 64]), op=ALU.mult), reads=a4("t_LW") + [L + "t_eP"], writes=a4("t_LW"))
        for qi, (src_, srck_, dst_) in enumerate(((rt, L + "rt", rtL), (kt, L + "kt", ktL), (bt, L + "bt", btL), (kkt, L + "kkt", kktL))):
            P.op('sp' if qi % 2 == 0 else 'act', lambda e, src_=src_, dst_=dst_: e.dma_start(out=dst_[:], in_=src_[64:128, :, :]),
                 reads=[srck_], writes=[srck_ + "L"], dma=True)
        P.op('sp', lambda e: e.dma_start(out=ePL[:], in_=t_eP[64:128].rearrange("p a (c t) -> p a c t", t=64)[:, :, :, 63:64], allow_slow_non_contiguous=True), reads=[L + "t_eP"], writes=[L + "t_ePL"], dma=True)
        P.op('dve', lambda e: e.tensor_copy(out=wcall[:].rearrange("p c (i two) -> p c i two", two=2)[:, :, :, 0],
                                            in_=t_eP[0:64].rearrange("p a (c t) -> p c a t", t=64)[:, :, :, 63]), reads=[L + "t_eP"], writes=[L + "wcall"])
        P.op('dve', lambda e: e.tensor_copy(out=wcall[:].rearrange("p c (i two) -> p c i two", two=2)[:, :, :, 1],
                                            in_=ePL[:].rearrange("p a c o -> p c (a o)")), reads=[L + "t_ePL", L + "wcall"], writes=[L + "wcall"])
        OP = lambda X, XL, hh, cs_: (X if hh % 2 == 0 else XL)[0:64, hh // 2, cs_]
        OPK = [L + "rt", L + "kt", L + "bt", L + "kkt", L + "rtL", L + "ktL", L + "btL", L + "kktL"]
        LO = {id(rt): rtL, id(kt): ktL, id(bt): btL, id(kkt): kktL}
        for c in range(NCH):
            for src, srck, dst, dstk, sc_ in ((None, vkeys, Vtok, L + "Vtok", 1.0), (t_lw, a4("t_lw"), Ktok, L + "Ktok", 1.0), (t_LW, a4("t_LW"), Btok, L + "Btok", -1.0)):
                pT, pTk = self.psum()
                for i in range(4):
                    in_ap = (xr[:, 8 + i, 1 + c * 64:1 + (c + 1) * 64] if src is None else src[:, i, c * 64:(c + 1) * 64])
                    P.op('pe', lambda e, i=i, pT=pT, in_ap=in_ap: e.transpose(out=pT[0:64, i * 128:(i + 1) * 128], in_=in_ap, identity=ident[:]),
                         reads=srck + [L + "ident"], writes=[pTk])
                P.op('act', lambda e, c=c, pT=pT, dst=dst, sc_=sc_: e.activation(out=dst[:, c, :], in_=pT[0:64, 0:512], func=AF.Identity, scale=sc_), reads=[pTk], writes=[f"{dstk}{c}"])
        Ml, Sl = [], []
        for c in range(NCH):
            q = c % 2
            cs = slice(c * 64, (c + 1) * 64)
            banks = {}
            self.defer_begin(pool=[0, 1, 2, 3, 4])
            for nm, A_, B_ in (("Lb", kkt, bt), ("LbT", bt, kkt), ("LkT", kt, kkt), ("RkT", kt, rt), ("RbT", bt, rt)):
                pm, pmk = self.psum()
                banks[nm] = (pm, pmk)
                for hh in range(8):
                    P.op('pe', lambda e, hh=hh, pm=pm, A_=A_, B_=B_, cs=cs: e.matmul(pm[0:64, hh * 64:(hh + 1) * 64], lhsT=OP(A_, LO[id(A_)], hh, cs), rhs=OP(B_, LO[id(B_)], hh, cs), start=True, stop=True),
                         reads=OPK, writes=[pmk])
            v3 = lambda pm: pm[0:64, 0:512].rearrange("p (h t) -> p h t", t=64)
            P.op('dve', lambda e, q=q, pm=banks["Lb"][0]: e.scalar_tensor_tensor(out=Nm_[q][:], in0=v3(pm), scalar=-1.0, in1=bc(SLm[:, None, :], [64, 8, 64]), op0=ALU.mult, op1=ALU.mult),
                 reads=[banks["Lb"][1], L + "cm"], writes=[L + f"N{q}"])
            P.op('dve', lambda e, q=q, pm=banks["LbT"][0]: e.scalar_tensor_tensor(out=NTm_[q][:], in0=v3(pm), scalar=-1.0, in1=bc(SUm[:, None, :], [64, 8, 64]), op0=ALU.mult, op1=ALU.mult),
                 reads=[banks["LbT"][1], L + "cm"], writes=[L + f"NT{q}"])
            P.op('dve', lambda e, q=q, pm=banks["LkT"][0]: e.tensor_tensor(out=LkT[q][:], in0=v3(pm), in1=bc(SUm[:, None, :], [64, 8, 64]), op=ALU.mult),
                 reads=[banks["LkT"][1], L + "cm"], writes=[L + f"LkT{q}"])
            P.op('dve', lambda e, q=q, pm=banks["RkT"][0]: e.tensor_tensor(out=RkT[q][:], in0=v3(pm), in1=bc(UTm[:, None, :], [64, 8, 64]), op=ALU.mult),
                 reads=[banks["RkT"][1], L + "cm"], writes=[L + f"RkT{q}"])
            P.op('dve', lambda e, q=q, pm=banks["RbT"][0]: e.scalar_tensor_tensor(out=RbT[q][:], in0=v3(pm), scalar=-1.0, in1=bc(UTm[:, None, :], [64, 8, 64]), op0=ALU.mult, op1=ALU.mult),
                 reads=[banks["RbT"][1], L + "cm"], writes=[L + f"RbT{q}"])
            self.tri_inv(Nm_[q], L + f"N{q}", NTm_[q], L + f"NT{q}", ATm[q], L + f"AT{q}", 8, TW)
            Ml.append(self.defer_end())
            self.defer_begin(pool=[5, 6, 7])
            p1, p1k = self.psum()
            for hh in range(8):
                P.op('pe', lambda e, hh=hh, p1=p1, cs=cs: e.matmul(p1[0:64, hh * 64:(hh + 1) * 64], lhsT=OP(kkt, kktL, hh, cs), rhs=Zbf[:, hh, :], start=True, stop=False),
                     reads=[L + "kkt", L + "kktL", L + "Zbf"], writes=[p1k])
                P.op('pe', lambda e, hh=hh, p1=p1, q=q, c=c: e.matmul(p1[0:64, hh * 64:(hh + 1) * 64], lhsT=LkT[q][:, hh, :], rhs=Vtok[:, c, hh * 64:(hh + 1) * 64], start=False, stop=True),
                     reads=[L + f"LkT{q}", f"{L}Vtok{c}"], writes=[p1k])
            P.op('act', lambda e, p1=p1: e.activation(out=X1b[:].rearrange("p h t -> p (h t)"), in_=p1[0:64, 0:512], func=AF.Identity), reads=[p1k], writes=[L + "X1b"])
            p2, p2k = self.psum()
            for hh in range(8):
                P.op('pe', lambda e, hh=hh, p2=p2, q=q: e.matmul(p2[0:64, hh * 64:(hh + 1) * 64], lhsT=ATm[q][:, hh, :], rhs=X1b[:, hh, :], start=True, stop=True),
                     reads=[L + f"AT{q}", L + "X1b"], writes=[p2k])
            P.op('act', lambda e, p2=p2: e.activation(out=Eb[:].rearrange("p h t -> p (h t)"), in_=p2[0:64, 0:512], func=AF.Identity), reads=[p2k], writes=[L + "Eb"])
            p3, p3k = self.psum()
            p4, p4k = self.psum()
            for hh in range(8):
                o3 = p3[0:64, hh * 64:(hh + 1) * 64]
                P.op('pe', lambda e, hh=hh, o3=o3, cs=cs: e.matmul(o3, lhsT=Zbf[:, hh, :], rhs=OP(rt, rtL, hh, cs), start=True, stop=False), reads=[L + "Zbf", L + "rt", L + "rtL"], writes=[p3k])
                P.op('pe', lambda e, hh=hh, o3=o3, q=q, c=c: e.matmul(o3, lhsT=Vtok[:, c, hh * 64:(hh + 1) * 64], rhs=RkT[q][:, hh, :], start=False, stop=False),
                     reads=[f"{L}Vtok{c}", L + f"RkT{q}"], writes=[p3k])
                P.op('pe', lambda e, hh=hh, o3=o3, q=q: e.matmul(o3, lhsT=Eb[:, hh, :], rhs=RbT[q][:, hh, :], start=False, stop=True), reads=[L + "Eb", L + f"RbT{q}"], writes=[p3k])
            for hh in range(8):
                o4 = p4[0:64, hh * 64:(hh + 1) * 64]
                P.op('pe', lambda e, hh=hh, o4=o4, c=c: e.matmul(o4, lhsT=Ktok[:, c, hh * 64:(hh + 1) * 64], rhs=Vtok[:, c, hh * 64:(hh + 1) * 64], start=True, stop=False),
                     reads=[f"{L}Ktok{c}", f"{L}Vtok{c}"], writes=[p4k])
                P.op('pe', lambda e, hh=hh, o4=o4, c=c: e.matmul(o4, lhsT=Btok[:, c, hh * 64:(hh + 1) * 64], rhs=Eb[:, hh, :], start=False, stop=True),
                     reads=[f"{L}Btok{c}", L + "Eb"], writes=[p4k])
            p3v = p3[0:64, 0:512].rearrange("p (i two t) -> p i two t", two=2, t=64)
            P.op('act', lambda e, p3v=p3v, cs=cs: e.activation(out=yT[0:64, :, cs], in_=p3v[:, :, 0, :], func=AF.Identity), reads=[p3k], writes=[L + "yT"])
            P.op('act', lambda e, p3v=p3v, cs=cs: e.activation(out=yodd[0:64, :, cs], in_=p3v[:, :, 1, :], func=AF.Identity), reads=[p3k], writes=a4("t_lw"))
            P.op('dve', lambda e, c=c: e.tensor_tensor(out=Z32[:], in0=Z32[:], in1=bc(wcall[:, c, :, None], [64, 8, 64]), op=ALU.mult), reads=[L + "Z32", L + "wcall"], writes=[L + "Z32"])
            P.op('dve', lambda e, p4=p4: e.tensor_tensor(out=Z32[:].rearrange("p h v -> p (h v)"), in0=Z32[:].rearrange("p h v -> p (h v)"), in1=p4[0:64, 0:512], op=ALU.add),
                 reads=[L + "Z32", p4k], writes=[L + "Z32"])
            P.op('act', lambda e: e.activation(out=Zbf[:], in_=Z32[:], func=AF.Identity), reads=[L + "Z32"], writes=[L + "Zbf"])
            Sl.append(self.defer_end())
        self.replay([Ml[0]])
        for c in range(NCH):
            self.replay([Sl[c]] + ([Ml[c + 1]] if c + 1 < NCH else []))
        P.op('sp', lambda e: e.dma_start(out=yT[64:128, :, :], in_=yodd[0:64, :, :]), reads=a4("t_lw") + [L + "yT"], writes=[L + "yT"], dma=True)
        P.op('act', lambda e, t0=t0: e.dma_start(out=gld[:], in_=gout_v[:, :, t0:t0 + TT]), writes=a4("t_LW"), dma=True)
        P.op('act', lambda e: e.activation(out=sq[:, 0:4, :], in_=yT[:], func=AF.Identity), reads=[L + "yT"],
             writes=[L + "sq"] + [f"{L}sqk{i}" for i in range(4)] + [f"{L}twb4_{i}" for i in range(4)] + [f"{L}sqc{i}" for i in range(4)])
        chains = []
        for i in range(4):
            self.defer_begin(pool=[2 * i, 2 * i + 1])
            pm, pmk = self.psum()
            P.op('pe', lambda e, i=i, pm=pm: e.matmul(pm[:, 0:TT], lhsT=bd64[:], rhs=sq[:, i, :], start=True, stop=True), reads=[L + "sq", L + "bd64"], writes=[pmk])
            P.op('dve', lambda e, i=i, pm=pm: e.tensor_tensor(out=yT[:, i, :], in0=yT[:, i, :], in1=pm[:, 0:TT], op=ALU.subtract), reads=[L + "yT", pmk], writes=[f"{L}yc{i}"])
            P.op('act', lambda e, i=i: e.activation(out=sq[:, 4 + i, :], in_=yT[:, i, :], func=AF.Square), reads=[f"{L}yc{i}"], writes=[f"{L}sqc{i}"])
            pv, pvk = self.psum()
            P.op('pe', lambda e, i=i, pv=pv: e.matmul(pv[:, 0:TT], lhsT=bd64[:], rhs=sq[:, 4 + i, :], start=True, stop=True), reads=[f"{L}sqc{i}", L + "bd64"], writes=[pvk])
            self.rsqrt_ps(rq4[:, i, :], f"{L}rq4_{i}", pv[:, 0:TT], pvk, 64e-5)
            P.op('dve', lambda e, i=i: e.tensor_tensor(out=yT[:, i, :], in0=yT[:, i, :], in1=rq4[:, i, :], op=ALU.mult), reads=[f"{L}yc{i}", f"{L}rq4_{i}"], writes=[f"{L}yc{i}"])
            P.op('act', lambda e, i=i: e.activation(out=yT[:, i, :], in_=yT[:, i, :], func=AF.Identity, scale=LNW[:, i:i + 1], bias=LNB[:, i:i + 1]), reads=[f"{L}yc{i}", L + "rv"], writes=[f"{L}yc{i}"])
            P.op('dve', lambda e, i=i: e.scalar_tensor_tensor(out=sq[:, i, :], in0=R_(i), scalar=RK_[:, i:i + 1], in1=t_kp[:, i, :], op0=ALU.mult, op1=ALU.mult),
                 reads=[rkeys[i], L + "rv", f"{L}t_kp{i}", f"{L}yc{i}"], writes=[f"{L}twb4_{i}"])
            pb, pbk = self.psum()
            P.op('pe', lambda e, i=i, pb=pb: e.matmul(pb[:, 0:TT], lhsT=bd1[:], rhs=sq[:, i, :], start=True, stop=True), reads=[f"{L}twb4_{i}", L + "bd1"], writes=[pbk])
            P.op('dve', lambda e, i=i, pb=pb: e.tensor_tensor(out=t_a[:, i, :], in0=pb[:, 0:TT], in1=V_(i), op=ALU.mult), reads=[pbk, vkeys[i]], writes=[f"{L}t_a{i}"])
            P.op('dve', lambda e, i=i: e.tensor_tensor(out=yT[:, i, :], in0=yT[:, i, :], in1=t_a[:, i, :], op=ALU.add), reads=[f"{L}yc{i}", f"{L}t_a{i}"], writes=[f"{L}yc{i}"])
            if rdbg_v is not None:
                P.op('dve', lambda e, i=i: e.tensor_tensor(out=t_a[:, i, :], in0=yT[:, i, :], in1=t_g[:, i, :], op=ALU.mult), reads=[f"{L}yc{i}", f"{L}t_g{i}", f"{L}t_a{i}"], writes=[f"{L}t_a{i}"])
            P.op('dve', lambda e, i=i: e.tensor_tensor(out=ymix[:, 4 + i, :], in0=yT[:, i, :], in1=t_g[:, i, :], op=ALU.mult), reads=[f"{L}yc{i}", f"{L}t_g{i}"], writes=[f"{L}h{4 + i}"])
            P.op('act', lambda e, i=i: e.activation(out=ymix[:, i, :], in_=gld[:, i, :], func=AF.Identity), reads=a4("t_LW"), writes=[f"{L}h{i}"])
            chains.append(self.defer_end())
        self.replay(chains)
        if rdbg_v is not None:
            P.op('sp', lambda e, t0=t0: e.dma_start(out=rdbg_v[:, :, t0:t0 + TT], in_=t_a[:]), reads=a4("t_a"), writes=[self.key("rdbg")], dma=True)
        yk = [f"{L}tmp{k}" for k in range(KC)]
        for d in range(KC):
            pd, pdk = self.psum()
            for k in range(KC):
                P.op('pe', lambda e, k=k, d=d, pd=pd: e.matmul(pd[:, 0:TT], lhsT=wout[:, k, d * 128:(d + 1) * 128], rhs=ymix[:, k, :], start=(k == 0), stop=(k == KC - 1)),
                     reads=[f"{L}wout{k}", f"{L}h{k}"], writes=[pdk])
            P.op('act', lambda e, d=d, pd=pd: e.activation(out=tmp[:, d, :], in_=pd[:, 0:TT], func=AF.Identity), reads=[pdk], writes=[yk[d]])
        self.postnorm_residual(tmp, yk, xt, xk, TT, s, sq, L + "sq", rstd, L + "rstd", tmp, L + "tmp")
        P.op('sp', lambda e, t0=t0: e.dma_start(out=xout_v[:, :, t0:t0 + TT], in_=tmp[:]), reads=yk, writes=[self.key('xout')], dma=True)
    self.end()


Builder.rwkv_section = _rwkv_section


def declare_rest(b, I):
    I['lru_win'] = b.din("lru_win", [128, 8, 2048]); I['lru_wa'] = b.din("lru_wa", [128, 8, 256]); I['lru_wx'] = b.din("lru_wx", [128, 8, 256])
    I['lru_wout'] = b.din("lru_wout", [128, 8, 1024]); I['lru_vec'] = b.din("lru_vec", [128, 8, 8])
    for l in range(2):
        I[f'wg{l}'] = b.din(f"wg{l}", [128, 8, 2816]); I[f'wu{l}'] = b.din(f"wu{l}", [128, 8, 2816]); I[f'wd{l}'] = b.din(f"wd{l}", [128, 22, 1024])


def rest_maps(inp):
    m = {}
    m["lru_win"] = fm_mat(inp['lru_w_in'][0])
    m["lru_wa"] = np.ascontiguousarray(np.asarray(inp['lru_wa'][0], np.float32).reshape(4, 2, 128, 256).transpose(2, 0, 1, 3).reshape(128, 8, 256))
    m["lru_wx"] = np.ascontiguousarray(np.asarray(inp['lru_wx'][0], np.float32).reshape(4, 2, 128, 256).transpose(2, 0, 1, 3).reshape(128, 8, 256))
    m["lru_wout"] = fm_mat(inp['lru_w_out'][0])
    cw = np.asarray(inp['lru_conv_w'][0], np.float32)
    m["lru_vec"] = np.ascontiguousarray(np.stack([fm_vec(cw[0]), fm_vec(cw[1]), fm_vec(cw[2]), fm_vec(cw[3]), fm_vec(inp['lru_conv_b'][0]),
                                                  fm_vec(inp['lru_ba'][0]), fm_vec(inp['lru_bx'][0]), fm_vec(inp['lru_lambda'][0])], axis=1))
    for l in range(2):
        m[f"wg{l}"] = fm_mat(inp['ffn_w_gate'][l]); m[f"wu{l}"] = fm_mat(inp['ffn_w_up'][l]); m[f"wd{l}"] = fm_mat(inp['ffn_w_down'][l])
    return m


_CACHE = {}


def build_full(NT=SEQ):
    if NT in _CACHE:
        return _CACHE[NT]
    b = Builder(nt=NT)
    I = declare_inputs(b, NT)
    declare_rest(b, I)
    outT = b.dout("outT", [1024, NT])
    gout = b.dscratch("s_gout", [512, NT])
    xa = b.dscratch("s_xa", [1024, NT])
    xb = b.dscratch("s_xb", [1024, NT])
    xc = b.dscratch("s_xc", [1024, NT])
    b.prologue(I['cT'], I['ada_w'], I['ada_b'], I['npre'], I['npost'])
    b.gdn_section(I['xT'], gout, I['wgd'], I['g_cvec'], I['g_tok'], I['g_nw'], I['cm'], I['timk'])
    b.rwkv_section(I['xT'], gout, xa, I, None)
    b.ffn_sublayer(0, xa, xb, I['wg0'], I['wu0'], I['wd0'])
    b.lru_sublayer(xb, xc, I['lru_win'], I['lru_wa'], I['lru_wx'], I['lru_wout'], I['lru_vec'])
    b.ffn_sublayer(1, xc, outT, I['wg1'], I['wu1'], I['wd1'])
    _CACHE[NT] = b
    return b


def kernel(**inputs):
    inp = {k: np.asarray(v) for k, v in inputs.items()}
    B, T, _ = inp['x'].shape
    b = build_full(T)
    cm = common_maps(inp)
    cm.update(rest_maps(inp))
    n_cores = 4
    maps = []
    for i in range(n_cores):
        m = dict(cm)
        m.update(core_maps(inp, i % B, T))
        maps.append(m)
    res = run_bass_kernel_spmd(b.nc, maps, core_ids=list(range(n_cores)))
    out = np.stack([np.ascontiguousarray(res.results[i]["outT"].T) for i in range(B)], axis=0)
    return out.astype(np.float32)
```
